# Optimizing a Trainium2 kernel written in Bass

```python
import math
import jax
import jax.numpy as jnp
from jax import lax
import numpy as np

D_MODEL = 1024
BATCH = 16
SEQ = 2048
DEPTH = 2

HEAD_DIM = 64
MIX_WIDTH = D_MODEL
N_HEADS_A = (MIX_WIDTH // 2) // (2 * HEAD_DIM)
DV_A = 2 * HEAD_DIM
N_HEADS_B = (MIX_WIDTH // 2) // HEAD_DIM
DILATED_PATTERNS = ((128, 1), (512, 4), (2048, 16))
N_HEADS_C = (MIX_WIDTH // 2) // HEAD_DIM
MOBA_BLOCK = 256
MOBA_TOPK = 3
MOBA_Q_CHUNK = 16
N_HEADS_D = (MIX_WIDTH // 2) // HEAD_DIM
FORGET_BIAS_INIT = 4.0
Q_BLOCK = 128
NUM_BUCKETS = 32
MAX_DISTANCE = 128
N_BIAS_HEADS = max(N_HEADS_A + N_HEADS_B, N_HEADS_C)
D_FF = -(-8 * D_MODEL // (3 * 256)) * 256
LN_EPS = 1e-5
DEEPNORM_ALPHA = (2 * DEPTH) ** 0.25
DEEPNORM_BETA = (8 * DEPTH) ** -0.25
ATTN_SCALE = HEAD_DIM ** -0.5

AB_SIZES = [N_HEADS_A * 2 * HEAD_DIM, N_HEADS_A * 2 * HEAD_DIM, N_HEADS_A * DV_A,
            N_HEADS_B * HEAD_DIM, N_HEADS_B * HEAD_DIM, N_HEADS_B * HEAD_DIM]
CD_SIZES = [N_HEADS_C * HEAD_DIM] * 3 + [N_HEADS_D * HEAD_DIM] * 3 + [N_HEADS_D]
AB_SPLITS = [int(s) for s in np.cumsum(AB_SIZES)[:-1]]
CD_SPLITS = [int(s) for s in np.cumsum(CD_SIZES)[:-1]]
IN_AB = sum(AB_SIZES)
IN_CD = sum(CD_SIZES)

kernel_name = "hybrid_diff_dilated_moba_fox_block"

f32 = jnp.float32


def t5_bucket(dist):
    n = jnp.maximum(dist, 0)
    max_exact = NUM_BUCKETS // 2
    nf = jnp.maximum(n, 1).astype(f32)
    large = max_exact + (jnp.log(nf / max_exact) / math.log(MAX_DISTANCE / max_exact)
                         * (NUM_BUCKETS - max_exact)).astype(jnp.int32)
    large = jnp.minimum(large, NUM_BUCKETS - 1)
    return jnp.where(n < max_exact, n, large)


def t5_bias(table_cols, dist):
    return jnp.moveaxis(table_cols[t5_bucket(dist)], -1, 0).astype(f32)


def layer_norm(x, g, b):
    xf = x.astype(f32)
    mu = xf.mean(-1, keepdims=True)
    var = jnp.square(xf - mu).mean(-1, keepdims=True)
    return ((xf - mu) * lax.rsqrt(var + LN_EPS) * g + b).astype(x.dtype)


def causal_block_attention(q, k, v, map_w, bias_fn):
    B, H, M, T, _ = q.shape
    k_pos = jnp.arange(T)

    def block(q0):
        qb = lax.dynamic_slice_in_dim(q, q0, Q_BLOCK, axis=3)
        q_pos = q0 + jnp.arange(Q_BLOCK)
        s = jnp.einsum('bhmqd,bhmkd->bhmqk', qb, k).astype(f32) + bias_fn(q_pos)
        s = jnp.where(k_pos[None, :] <= q_pos[:, None], s, -jnp.inf)
        p = jnp.einsum('bhmqk,m->bhqk', jax.nn.softmax(s, axis=-1), map_w)
        return jnp.einsum('bhqk,bhkd->bhqd', p.astype(v.dtype), v)

    out = lax.map(block, jnp.arange(T // Q_BLOCK) * Q_BLOCK)
    return jnp.moveaxis(out, 0, 2).reshape(B, H, T, v.shape[-1])


def dilated_attention(q, k, v, table_cols):
    B, H, T, dh = q.shape
    outs, lses = [], []
    for window, dil in DILATED_PATTERNS:
        W = window // dil
        L = T // dil
        nb = -(-L // W)
        pad = nb * W - L

        def strided(a):
            a = a.reshape(B, H, L, dil, a.shape[-1]).swapaxes(2, 3)
            return jnp.pad(a, ((0, 0), (0, 0), (0, 0), (0, pad), (0, 0)))

        def with_prev(a):
            ab = a.reshape(B, H, dil, nb, W, a.shape[-1])
            prev = jnp.pad(ab, ((0, 0), (0, 0), (0, 0), (1, 0), (0, 0), (0, 0)))[:, :, :, :-1]
            return jnp.concatenate([prev, ab], axis=4)

        qb = strided(q).reshape(B, H, dil, nb, W, dh)
        kb = with_prev(strided(k))
        vb = with_prev(strided(v))
        u_q = jnp.arange(nb)[:, None] * W + jnp.arange(W)[None, :]
        u_k = jnp.arange(nb)[:, None] * W + jnp.arange(-W, W)[None, :]
        du = u_q[:, :, None] - u_k[:, None, :]
        valid = (du >= 0) & (du <= W) & (u_k[:, None, :] >= 0)
        bias = t5_bias(table_cols, du * dil)
        s = jnp.einsum('bhrnqd,bhrnkd->bhrnqk', qb, kb).astype(f32) + bias[None, :, None]
        s = jnp.where(valid, s, -jnp.inf)
        m = s.max(-1, keepdims=True)
        p = jnp.exp(s - m)
        den = p.sum(-1)
        o = jnp.einsum('bhrnqk,bhrnkd->bhrnqd', p.astype(vb.dtype), vb) / den[..., None]
        o = o.reshape(B, H, dil, nb * W, dh)[:, :, :, :L].swapaxes(2, 3).reshape(B, H, T, dh)
        lse = (m[..., 0] + jnp.log(den)).reshape(B, H, dil, nb * W)[..., :L]
        outs.append(o)
        lses.append(lse.swapaxes(2, 3).reshape(B, H, T))
    w = jax.nn.softmax(jnp.stack(lses), axis=0)
    return jnp.sum(w[..., None] * jnp.stack(outs), axis=0).astype(q.dtype)


def moba_attention(q, k, v, table_cols):
    B, H, T, dh = q.shape
    n_blk = -(-T // MOBA_BLOCK)
    pad = n_blk * MOBA_BLOCK - T
    kp = jnp.pad(k, ((0, 0), (0, 0), (0, pad), (0, 0)))
    vp = jnp.pad(v, ((0, 0), (0, 0), (0, pad), (0, 0)))
    k_blk = kp.reshape(B, H, n_blk, MOBA_BLOCK, dh)
    v_blk = vp.reshape(B, H, n_blk, MOBA_BLOCK, dh)
    k_mean = k_blk.mean(axis=3)
    top = min(MOBA_TOPK, n_blk)
    b_idx = jnp.arange(B)[:, None, None, None]
    h_idx = jnp.arange(H)[None, :, None, None]
    bias_flat = table_cols.T.reshape(-1).astype(f32)
    in_blk = jnp.arange(MOBA_BLOCK)

    def chunk(q0):
        qc = lax.dynamic_slice_in_dim(q, q0, MOBA_Q_CHUNK, axis=2)
        q_pos = q0 + jnp.arange(MOBA_Q_CHUNK)
        own = q0 // MOBA_BLOCK
        gate = jnp.einsum('bhcd,bhnd->bhcn', qc, k_mean).astype(f32)
        gate = jnp.where(jnp.arange(n_blk) < own, gate, -jnp.inf)
        top_s, top_i = lax.top_k(gate, top)
        sel_ok = jnp.isfinite(top_s)
        k_sel = k_blk[b_idx, h_idx, top_i]
        v_sel = v_blk[b_idx, h_idx, top_i]
        sel_pos = top_i[..., None] * MOBA_BLOCK + in_blk
        sel_bucket = t5_bucket(q_pos[:, None, None] - sel_pos)
        s_sel = (jnp.einsum('bhcd,bhcnkd->bhcnk', qc, k_sel).astype(f32)
                 + bias_flat[h_idx[..., None] * NUM_BUCKETS + sel_bucket])
        s_sel = jnp.where(sel_ok[..., None], s_sel, -jnp.inf)
        own_start = own * MOBA_BLOCK
        k_own = lax.dynamic_slice_in_dim(kp, own_start, MOBA_BLOCK, axis=2)
        v_own = lax.dynamic_slice_in_dim(vp, own_start, MOBA_BLOCK, axis=2)
        own_dist = q_pos[:, None] - (own_start + in_blk)[None, :]
        s_own = (jnp.einsum('bhcd,bhkd->bhck', qc, k_own).astype(f32)
                 + t5_bias(table_cols, own_dist)[None])
        s_own = jnp.where(own_dist >= 0, s_own, -jnp.inf)
        s = jnp.concatenate([s_sel.reshape(B, H, MOBA_Q_CHUNK, top * MOBA_BLOCK), s_own], axis=-1)
        p = jax.nn.softmax(s, axis=-1).astype(v.dtype)
        p_sel = p[..., :top * MOBA_BLOCK].reshape(B, H, MOBA_Q_CHUNK, top, MOBA_BLOCK)
        p_own = p[..., top * MOBA_BLOCK:]
        return (jnp.einsum('bhcnk,bhcnkd->bhcd', p_sel, v_sel)
                + jnp.einsum('bhck,bhkd->bhcd', p_own, v_own))

    out = lax.map(chunk, jnp.arange(T // MOBA_Q_CHUNK) * MOBA_Q_CHUNK)
    return jnp.moveaxis(out, 0, 2).reshape(B, H, T, dh)


def split_heads(a, dh):
    B, T, _ = a.shape
    return a.reshape(B, T, -1, dh).transpose(0, 2, 1, 3)


def merge_heads(a):
    B, H, T, dh = a.shape
    return a.transpose(0, 2, 1, 3).reshape(B, T, H * dh)


def mixer_ab(h, w_in, w_o, lam, subln_g, table, lambda_init):
    B, T, _ = h.shape
    qa, ka, va, qb, kb, vb = jnp.split(h @ w_in, AB_SPLITS, axis=-1)
    qa = qa.reshape(B, T, N_HEADS_A, 2, HEAD_DIM).transpose(0, 2, 3, 1, 4) * ATTN_SCALE
    ka = ka.reshape(B, T, N_HEADS_A, 2, HEAD_DIM).transpose(0, 2, 3, 1, 4)
    va = split_heads(va, DV_A)
    lamf = lam.astype(f32)
    lam_full = (jnp.exp(jnp.sum(lamf[0] * lamf[1])) - jnp.exp(jnp.sum(lamf[2] * lamf[3]))
                + lambda_init)
    map_w = jnp.stack([jnp.ones((), f32), -lam_full])
    table_a = table[:, :N_HEADS_A]
    k_pos = jnp.arange(T)
    bias_a = lambda q_pos: t5_bias(table_a, q_pos[:, None] - k_pos[None, :])[None, :, None]
    oa = causal_block_attention(qa, ka, va, map_w, bias_a).astype(f32)
    oa = oa * lax.rsqrt(jnp.mean(jnp.square(oa), -1, keepdims=True) + LN_EPS)
    oa = (oa * subln_g * (1.0 - lambda_init)).astype(h.dtype)
    ob = dilated_attention(split_heads(qb, HEAD_DIM) * ATTN_SCALE, split_heads(kb, HEAD_DIM),
                           split_heads(vb, HEAD_DIM), table[:, N_HEADS_A:N_HEADS_A + N_HEADS_B])
    return jnp.concatenate([merge_heads(oa), merge_heads(ob)], axis=-1) @ w_o


def mixer_cd(h, w_in, w_o, forget_b, table):
    qc, kc, vc, qd, kd, vd, fd = jnp.split(h @ w_in, CD_SPLITS, axis=-1)
    oc = moba_attention(split_heads(qc, HEAD_DIM) * ATTN_SCALE, split_heads(kc, HEAD_DIM),
                        split_heads(vc, HEAD_DIM), table[:, :N_HEADS_C])
    log_f = jax.nn.log_sigmoid((fd + forget_b).astype(f32)).transpose(0, 2, 1)
    cum = jnp.cumsum(log_f, axis=-1)
    fox_bias = lambda q_pos: (cum[:, :, q_pos, None] - cum[:, :, None, :])[:, :, None]
    od = causal_block_attention((split_heads(qd, HEAD_DIM) * ATTN_SCALE)[:, :, None],
                                split_heads(kd, HEAD_DIM)[:, :, None],
                                split_heads(vd, HEAD_DIM), jnp.ones((1,), f32), fox_bias)
    return jnp.concatenate([merge_heads(oc), merge_heads(od)], axis=-1) @ w_o


def swiglu(h, w_in, w_out):
    g, u = jnp.split(h @ w_in, 2, axis=-1)
    return (jax.nn.silu(g) * u) @ w_out


def setup_inputs(seed: int = 0) -> dict:
    key = jax.random.key(seed)
    ks = jax.random.split(key, 15)
    n_even = (DEPTH + 1) // 2
    n_odd = DEPTH // 2
    nrm = lambda k, shape, std: jax.random.normal(k, shape, f32) * std
    return {
        "x": nrm(ks[0], (BATCH, SEQ, D_MODEL), 1.0),
        "c": nrm(ks[1], (BATCH, D_MODEL), 1.0),
        "rel_bias": nrm(ks[2], (NUM_BUCKETS, N_BIAS_HEADS), 0.5),
        "w_ada": nrm(ks[3], (DEPTH, D_MODEL, 6 * D_MODEL), 0.1 * D_MODEL ** -0.5),
        "b_ada": nrm(ks[4], (DEPTH, 6 * D_MODEL), 0.01),
        "ln_g": 1.0 + nrm(ks[5], (DEPTH, 2, D_MODEL), 0.01),
        "ln_b": nrm(ks[6], (DEPTH, 2, D_MODEL), 0.01),
        "w_in_ab": nrm(ks[7], (n_even, D_MODEL, IN_AB), D_MODEL ** -0.5),
        "diff_lambda": nrm(ks[8], (n_even, 4, HEAD_DIM), 0.1),
        "diff_subln_g": 1.0 + nrm(ks[9], (n_even, DV_A), 0.01),
        "w_in_cd": nrm(ks[10], (n_odd, D_MODEL, IN_CD), D_MODEL ** -0.5),
        "forget_b": FORGET_BIAS_INIT + nrm(ks[11], (n_odd, N_HEADS_D), 0.5),
        "w_o": nrm(ks[12], (DEPTH, MIX_WIDTH, D_MODEL), DEEPNORM_BETA * MIX_WIDTH ** -0.5),
        "w_ffn_in": nrm(ks[13], (DEPTH, D_MODEL, 2 * D_FF), D_MODEL ** -0.5),
        "w_ffn_out": nrm(ks[14], (DEPTH, D_FF, D_MODEL), DEEPNORM_BETA * D_FF ** -0.5),
    }


def reference(x, c, rel_bias, w_ada, b_ada, ln_g, ln_b, w_in_ab, diff_lambda, diff_subln_g,
              w_in_cd, forget_b, w_o, w_ffn_in, w_ffn_out):
    for l in range(DEPTH):
        ada = jax.nn.silu(c) @ w_ada[l] + b_ada[l]
        sh1, sc1, g1, sh2, sc2, g2 = [a[:, None, :] for a in jnp.split(ada, 6, axis=-1)]
        h = x * (1.0 + sc1) + sh1
        if l % 2 == 0:
            i = l // 2
            lambda_init = 0.8 - 0.6 * math.exp(-0.3 * l)
            y = mixer_ab(h, w_in_ab[i], w_o[l], diff_lambda[i], diff_subln_g[i], rel_bias,
                         lambda_init)
        else:
            i = l // 2
            y = mixer_cd(h, w_in_cd[i], w_o[l], forget_b[i], rel_bias)
        x = layer_norm(DEEPNORM_ALPHA * x + (1.0 + g1) * y, ln_g[l, 0], ln_b[l, 0])
        h = x * (1.0 + sc2) + sh2
        y = swiglu(h, w_ffn_in[l], w_ffn_out[l])
        x = layer_norm(DEEPNORM_ALPHA * x + (1.0 + g2) * y, ln_g[l, 1], ln_b[l, 1])
    return x
```

```python
import math
import numpy as np
from contextlib import ExitStack
import concourse.bass as bass
import concourse.mybir as mybir
from concourse.bass_utils import run_bass_kernel_spmd

F32 = mybir.dt.float32
BF16 = mybir.dt.bfloat16
AF = mybir.ActivationFunctionType
ALU = mybir.AluOpType
AX = mybir.AxisListType

T = 2048
D = 1024
NT = 16
DFF = 2816
NF = 22
NB = 2
ALPHA = 4 ** 0.25
EPS = 1e-5
NEG = -30000.0
N_ETILES = 53

ENGS = ("pe", "act", "dve", "pool", "sp")
N_DMA_SEMS = 24


def run_interleaved(gens):
    live = list(gens)
    while live:
        nxt = []
        for g in live:
            try:
                next(g)
                nxt.append(g)
            except StopIteration:
                pass
        live = nxt


class Prog:
    def __init__(self, nc, es):
        self.nc = nc
        self.eng_obj = {"pe": nc.tensor, "act": nc.scalar, "dve": nc.vector,
                        "pool": nc.gpsimd, "sp": nc.sync}
        self.count = {e: 0 for e in ENGS}
        self.known = {e: {} for e in ENGS}
        self.last_w = {}
        self.readers = {}
        self.dma_rr = 0
        self.dma_cnt = [0] * N_DMA_SEMS
        self.sems = {}
        self.n_ins = {e: 0 for e in ENGS}
        for e in ENGS:
            self.sems[e] = es.enter_context(nc.semaphore("s_" + e))
        for j in range(N_DMA_SEMS):
            self.sems[("d", j)] = es.enter_context(nc.semaphore("s_d%d" % j))

    def _deps(self, reads, writes):
        toks = []
        for r in reads:
            t = self.last_w.get(r)
            if t is not None:
                toks.append(t)
        for w in writes:
            t = self.last_w.get(w)
            if t is not None:
                toks.append(t)
            toks.extend(self.readers.get(w, ()))
        return toks

    def _commit(self, tok, reads, writes):
        for r in reads:
            self.readers.setdefault(r, []).append(tok)
        for w in writes:
            self.last_w[w] = tok
            self.readers[w] = []

    def _waits(self, eng, toks):
        need = {}
        kn = self.known[eng]
        for (k, v) in toks:
            if eng == "pe" and k == "pe":
                continue
            if kn.get(k, 0) >= v:
                continue
            if need.get(k, 0) < v:
                need[k] = v
        for k, v in need.items():
            kn[k] = v
        return list(need.items())

    def _emit(self, eng, waits, fn, inc):
        e = self.eng_obj[eng]
        for k, v in waits:
            e.wait_ge(self.sems[k], v)
        if fn is not None:
            ins = fn(e)
            self.n_ins[eng] += 1
            if inc is not None:
                ins.then_inc(self.sems[inc[0]], inc[1])

    def op(self, eng, fn, reads=(), writes=(), inc=True):
        toks = self._deps(reads, writes)
        waits = self._waits(eng, toks)
        if inc:
            self.count[eng] += 1
            tok = (eng, self.count[eng])
            self._emit(eng, waits, fn, (eng, 1))
        else:
            tok = (eng, self.count[eng] + 1)
            self._emit(eng, waits, fn, None)
        self._commit(tok, reads, writes)
        return tok

    def dma(self, q, out, in_, reads=(), writes=(), slow=False):
        toks = self._deps(reads, writes)
        j = self.dma_rr
        self.dma_rr = (self.dma_rr + 1) % N_DMA_SEMS
        if self.dma_cnt[j] > 0:
            toks.append((("d", j), 16 * self.dma_cnt[j]))
        waits = self._waits(q, toks)
        self.dma_cnt[j] += 1
        tok = (("d", j), 16 * self.dma_cnt[j])
        if slow:
            fn = lambda e: e.dma_start(out=out, in_=in_, allow_slow_non_contiguous=True)
        else:
            fn = lambda e: e.dma_start(out=out, in_=in_)
        self._emit(q, waits, fn, (("d", j), 16))
        self._commit(tok, reads, writes)
        return tok

    def all_tokens(self):
        toks = [(e, self.count[e]) for e in ENGS if self.count[e] > 0]
        toks += [(("d", j), 16 * c) for j, c in enumerate(self.dma_cnt) if c > 0]
        return toks

    def barrier(self, engs=ENGS):
        toks = self.all_tokens()
        for e in engs:
            self._emit(e, self._waits(e, list(toks)), None, None)
        self.last_w = {}
        self.readers = {}


def t5_bucket_np(n):
    n = np.maximum(n, 0)
    nf = np.maximum(n, 1).astype(np.float32)
    large = 16 + (np.log(nf / np.float32(16)) / np.float32(math.log(128 / 16)) * np.float32(16)).astype(np.int32)
    large = np.minimum(large, 31)
    return np.where(n < 16, n, large)


def make_bias_tiles(rel_bias):
    k = np.arange(128)[:, None]
    q = np.arange(128)[None, :]
    tiles = np.full((N_ETILES, 128, 128), NEG, np.float32)
    cols = np.zeros((N_ETILES,), np.int64)

    def fill(idx, dist, valid, col):
        b = t5_bucket_np(np.where(valid, dist, 0))
        tiles[idx] = np.where(valid, rel_bias[b, col], np.float32(NEG))
        cols[idx] = col

    for col in range(12):
        fill(col, q - k, q >= k, col)
    for h in range(8):
        fill(12 + h, q - k + 128, np.ones((128, 128), bool), h)
    for hb in range(8):
        col = 4 + hb
        fill(20 + hb, q - k + 128, q <= k, col)
        fill(28 + hb, 4 * (q - k), q >= k, col)
        fill(36 + hb, 4 * (q - k + 128), q <= k, col)
        fill(44 + hb, 16 * (q - k), q >= k, col)
    tiles[52] = np.where(q >= k, np.float32(0.0), np.float32(NEG))
    cols[52] = 12
    return tiles, cols


ECOLS = None


def build_program(stage=99):
    nc = bass.Bass("TRN2", target_bir_lowering=False)
    dram = lambda name, shape, dt, kind: nc.dram_tensor(name, shape, dt, kind=kind).ap()
    x_d = dram("x", [NB, T, D], F32, "ExternalInput")
    c_d = dram("c", [NB, D], F32, "ExternalInput")
    relb_d = dram("rel_bias", [32, 12], F32, "ExternalInput")
    wada_d = dram("w_ada", [2, D, 6 * D], F32, "ExternalInput")
    bada_d = dram("b_ada", [2, 6 * D], F32, "ExternalInput")
    lng_d = dram("ln_g", [2, 2, D], F32, "ExternalInput")
    lnb_d = dram("ln_b", [2, 2, D], F32, "ExternalInput")
    wab_d = dram("w_in_ab", [D, 3072], F32, "ExternalInput")
    lam_d = dram("diff_lambda", [256], F32, "ExternalInput")
    subg_d = dram("diff_subln_g", [128], F32, "ExternalInput")
    wcd_d = dram("w_in_cd", [D, 3080], F32, "ExternalInput")
    fb_d = dram("forget_b", [8], F32, "ExternalInput")
    wo_d = dram("w_o", [2, D, D], F32, "ExternalInput")
    wfi_d = dram("w_ffn_in", [2, D, 2 * DFF], F32, "ExternalInput")
    wfo_d = dram("w_ffn_out", [2, DFF, D], F32, "ExternalInput")
    bt_d = dram("btiles", [N_ETILES, 128, 128], F32, "ExternalInput")
    ident_d = dram("ident", [128, 128], F32, "ExternalInput")
    tri_d = dram("tri", [128, 128], F32, "ExternalInput")
    kaug_d = dram("kaug", [8, T], F32, "ExternalInput")
    out_d = dram("out", [NB, T, D], F32, "ExternalOutput")
    wg16 = dram("wg16", [2, 24, 128, 8, 128], BF16, "Internal")
    kaug16 = dram("kaug16", [8, T], BF16, "Internal")
    wo16 = dram("wo16", [2, 128, 8, D], BF16, "Internal")
    wfi16 = dram("wfi16", [2, NF, 128, 8, 256], BF16, "Internal")
    wfo16 = dram("wfo16", [2, NF, 128, D], BF16, "Internal")
    ada_s = dram("ada_s", [2, NB, 6 * D], F32, "Internal")
    E_d = dram("E_d", [N_ETILES, 128, 128], BF16, "Internal")
    dbg = {}
    if stage < 99:
        dbg["mixT"] = dram("dbg_mixT", [128, 8, T], BF16, "ExternalOutput")
        dbg["X"] = dram("dbg_X", [T, D], F32, "ExternalOutput")

    with ExitStack() as es:
        P = Prog(nc, es)
        ctr = {"ev": 0, "bank": 0, "pt": 0}

        def sb(st, name, shape, dt):
            ctr["uid"] = ctr.get("uid", 0) + 1
            return st.enter_context(nc.sbuf_tensor("%s_u%d" % (name, ctr["uid"]), shape, dt))

        X = sb(es, "X", [128, NT, D], F32)
        hT = sb(es, "hT", [128, 8, T], BF16)
        ident = sb(es, "ident", [128, 128], F32)
        ident16 = sb(es, "ident16", [128, 128], BF16)
        tri = sb(es, "tri", [128, 128], F32)
        ones32 = sb(es, "ones32", [128, 128], F32)
        ones16 = sb(es, "ones16", [128, 128], BF16)
        gmask = sb(es, "gmask", [128, NT, 8], F32)
        modT = sb(es, "modT", [128, 4, 8], F32)
        neglam = sb(es, "neglam", [128, 1], F32)
        subg = sb(es, "subg", [128, 1], F32)
        fbb = sb(es, "fbb", [128, 8], F32)
        epsc = sb(es, "epsc", [128, 1], F32)
        stt = sb(es, "stt", [128, 2, 2, 6], F32)
        mv = sb(es, "mv", [128, 2, 8], F32)
        banks = [es.enter_context(nc.psum_tensor("ps%d" % i, [128, 512], F32)) for i in range(8)]

        def bank(group=None):
            lst = group if group is not None else list(range(8))
            key = "bank" + str(lst)
            i = ctr.get(key, 0)
            ctr[key] = i + 1
            b = lst[i % len(lst)]
            return banks[b], "ps%d" % b

        def evac(out, in_, reads, writes, eng=None):
            if eng is None:
                eng = "act" if ctr["ev"] % 2 == 0 else "dve"
                ctr["ev"] += 1
            if eng == "act":
                P.op("act", lambda e: e.activation(out=out, in_=in_, func=AF.Copy), reads, writes)
            else:
                P.op(eng, lambda e: e.tensor_copy(out=out, in_=in_), reads, writes)

        def mm(out, lhsT, rhs, start, stop, reads, writes, inc=None):
            if inc is None:
                inc = stop
            P.op("pe", lambda e: e.matmul(out, lhsT=lhsT, rhs=rhs, start=start, stop=stop), reads, writes, inc=inc)

        P.dma("sp", ident[:], ident_d, writes=["ident"])
        P.dma("sp", tri[:], tri_d, writes=["tri"])
        P.op("dve", lambda e: e.memset(ones32[:], 1.0), writes=["ones32"])
        P.op("dve", lambda e: e.memset(ones16[:], 1.0), writes=["ones16"])
        P.op("dve", lambda e: e.memset(epsc[:], EPS), writes=["epsc"])
        P.op("dve", lambda e: e.memset(gmask[:], -1e30), writes=["gmask"])
        for ti in range(NT):
            own = ti // 2
            if own > 0:
                P.op("dve", lambda e, ti=ti, own=own: e.memset(gmask[:, ti, 0:own], 0.0), reads=["gmask"], writes=["gmask"])
            P.op("dve", lambda e, ti=ti, own=own: e.memset(gmask[:, ti, own:own + 1], 1e30), reads=["gmask"], writes=["gmask"])
        P.dma("sp", subg[:], subg_d.rearrange("(p o) -> p o", o=1), writes=["subg"])
        P.op("dve", lambda e: e.tensor_scalar(out=subg[:], in0=subg[:], scalar1=0.8, scalar2=None, op0=ALU.mult),
             reads=["subg"], writes=["subg"])
        P.dma("sp", fbb[:], fb_d.partition_broadcast(128), writes=["fbb"])

        modall = sb(es, "modall", [128, 2, 48, NB], F32)
        wfs = sb(es, "wfs", [128, 8, 8], BF16)
        with ExitStack() as ps_:
            ps1 = ExitStack()
            negfar = sb(ps1, "negfar", [128, 16], F32)
            lam = sb(ps1, "lam", [128, 256], F32)
            lamp = sb(ps1, "lamp", [128, 128], F32)
            ls = sb(ps1, "ls", [128, 4], F32)
            btl = [sb(ps1, "btl%d" % i, [128, 4, 128], F32) for i in range(2)]
            etl = [sb(ps1, "etl%d" % i, [128, 4, 128], BF16) for i in range(2)]
            kg32 = sb(ps1, "kg32", [8, T], F32)
            kg16 = sb(ps1, "kg16", [8, T], BF16)

            P.op("dve", lambda e: e.tensor_copy(out=ident16[:], in_=ident[:]), reads=["ident"], writes=["ident16"])
            P.dma("sp", kg32[:], kaug_d, writes=["kg32"])
            P.op("dve", lambda e: e.tensor_copy(out=kg16[:], in_=kg32[:]), reads=["kg32"], writes=["kg16"])
            P.dma("sp", kaug16, kg16[:], reads=["kg16"], writes=["kaug16"])

            P.dma("sp", lam[:], lam_d.partition_broadcast(128), writes=["lam"])
            P.op("dve", lambda e: e.tensor_tensor(out=lamp[:, 0:64], in0=lam[:, 0:64], in1=lam[:, 64:128], op=ALU.mult),
                 reads=["lam"], writes=["lamp"])
            P.op("dve", lambda e: e.tensor_tensor(out=lamp[:, 64:128], in0=lam[:, 128:192], in1=lam[:, 192:256], op=ALU.mult),
                 reads=["lam", "lamp"], writes=["lamp"])
            P.op("dve", lambda e: e.reduce_sum(out=ls[:, 0:1], in_=lamp[:, 0:64], axis=AX.X), reads=["lamp"], writes=["ls"])
            P.op("dve", lambda e: e.reduce_sum(out=ls[:, 1:2], in_=lamp[:, 64:128], axis=AX.X), reads=["lamp", "ls"], writes=["ls"])
            P.op("act", lambda e: e.activation(out=ls[:, 2:4], in_=ls[:, 0:2], func=AF.Exp), reads=["ls"], writes=["ls"])
            P.op("dve", lambda e: e.tensor_tensor(out=neglam[:], in0=ls[:, 3:4], in1=ls[:, 2:3], op=ALU.subtract),
                 reads=["ls"], writes=["neglam"])
            P.op("dve", lambda e: e.tensor_scalar(out=neglam[:], in0=neglam[:], scalar1=-0.2, scalar2=None, op0=ALU.add),
                 reads=["neglam"], writes=["neglam"])

            P.op("dve", lambda e: e.memset(negfar[:], 0.0), writes=["negfar"])
            P.dma("sp", negfar[:, 0:12], relb_d[31, :].partition_broadcast(128), reads=["negfar"], writes=["negfar"])
            P.op("dve", lambda e: e.tensor_scalar(out=negfar[:], in0=negfar[:], scalar1=-1.0, scalar2=None, op0=ALU.mult),
                 reads=["negfar"], writes=["negfar"])
            for gi, t0 in enumerate(range(0, N_ETILES, 4)):
                n = min(4, N_ETILES - t0)
                bb, ee = btl[gi % 2], etl[gi % 2]
                bn, en = "btl%d" % (gi % 2), "etl%d" % (gi % 2)
                P.dma("sp", bb[:, 0:n, :], bt_d[t0:t0 + n].rearrange("t k q -> k t q"), writes=[bn])
                for i in range(n):
                    col = int(ECOLS[t0 + i])
                    P.op("act", lambda e, ee=ee, bb=bb, i=i, col=col: e.activation(
                        out=ee[:, i, :], in_=bb[:, i, :], func=AF.Exp, bias=negfar[:, col:col + 1], scale=1.0),
                        reads=[bn, "negfar"], writes=[en])
                P.dma("sp", E_d[t0:t0 + n].rearrange("t k q -> k t q"), ee[:, 0:n, :], reads=[en], writes=["E_d"])

            P.barrier()
            ps1.close()
            csb = sb(ps_, "csb", [NB, D], F32)
            cT32 = sb(ps_, "cT32", [128, 8, NB], F32)
            badas = [sb(ps_, "bada%d" % i, [NB, 512], F32) for i in range(2)]
            adasb = sb(ps_, "adasb", [NB, 6 * D], F32)
            NSTG = 3
            stg32 = [sb(ps_, "stg32_%d" % i, [128, 8 * 520], F32) for i in range(NSTG)]
            stg16 = [sb(ps_, "stg16_%d" % i, [128, 8 * 512], BF16) for i in range(NSTG)]
            P.dma("sp", csb[:], c_d, writes=["csb"])
            P.op("act", lambda e: e.activation(out=csb[:], in_=csb[:], func=AF.Silu), reads=["csb"], writes=["csb"])
            pb, pbn = bank()
            for k in range(8):
                P.op("pe", lambda e, k=k: e.transpose(pb[:, k * NB:(k + 1) * NB], csb[0:NB, k * 128:(k + 1) * 128], ident[0:NB, 0:NB]),
                     reads=["csb", "ident"], writes=[pbn], inc=(k == 7))
            P.op("dve", lambda e: e.tensor_copy(out=cT32[:].rearrange("p k b -> p (k b)"), in_=pb[:, 0:8 * NB]), reads=[pbn], writes=["cT32"])
            it = 0
            for l in range(2):
                for n in range(12):
                    bada, bdn = badas[it % 2], "bada%d" % (it % 2)
                    P.dma("sp", bada[:], bada_d[l, n * 512:(n + 1) * 512].partition_broadcast(NB), writes=[bdn])
                    si = it % NSTG
                    it += 1
                    wv = stg32[si][:, 0:8 * 512].rearrange("p (c n) -> p c n", c=8)
                    P.dma("sp", wv, wada_d[l, :, n * 512:(n + 1) * 512].rearrange("(c p) n -> p c n", p=128), writes=["stg32_%d" % si])
                    pb, pbn = bank()
                    for k in range(8):
                        mm(pb[0:NB, :], cT32[:, k, :], wv[:, k, :], k == 0, k == 7, ["cT32", "stg32_%d" % si], [pbn])
                    P.op("dve", lambda e, pb=pb, n=n: e.tensor_tensor(
                        out=adasb[:, n * 512:(n + 1) * 512], in0=pb[0:NB, :], in1=bada[:], op=ALU.add),
                        reads=[pbn, bdn], writes=["adasb"])
                P.dma("sp", ada_s[l], adasb[:], reads=["adasb"], writes=["ada_s"])
                pb, pbn = bank()
                for ch in range(48):
                    P.op("pe", lambda e, ch=ch: e.transpose(pb[:, ch * NB:(ch + 1) * NB], adasb[0:NB, ch * 128:(ch + 1) * 128], ident[0:NB, 0:NB]),
                         reads=["adasb", "ident"], writes=[pbn], inc=(ch == 47))
                P.op("dve", lambda e, l=l, pb=pb: e.tensor_copy(out=modall[:, l].rearrange("p c b -> p (c b)"), in_=pb[:, 0:48 * NB]),
                     reads=[pbn], writes=["modall"])
            for c0 in (8, 32):
                P.op("dve", lambda e, c0=c0: e.tensor_scalar(out=modall[:, :, c0:c0 + 8, :], in0=modall[:, :, c0:c0 + 8, :], scalar1=1.0,
                                                               scalar2=None, op0=ALU.add), reads=["modall"], writes=["modall"])

            cast_engs = ["dve", "act", "pool"]
            pieces = []

            def do_cast(eng, ov, iv, n32, n16):
                if eng == "act":
                    P.op("act", lambda e: e.activation(out=ov, in_=iv, func=AF.Copy), reads=[n32], writes=[n16])
                else:
                    P.op(eng, lambda e: e.tensor_copy(out=ov, in_=iv), reads=[n32], writes=[n16])

            def add_piece(loads, casts, stores):
                k = len(pieces)
                eng = cast_engs[k % 3]

                def ld(si):
                    for vf, src in loads:
                        P.dma("sp", vf(stg32[si]), src, reads=["stg32_%d" % si], writes=["stg32_%d" % si])

                def cs(si):
                    for of, inf in casts:
                        do_cast(eng, of(stg16[si]), inf(stg32[si]), "stg32_%d" % si, "stg16_%d" % si)

                def st(si):
                    for dst, vf, names in stores:
                        P.dma("sp", dst, vf(stg16[si]), reads=["stg16_%d" % si], writes=names)
                pieces.append((ld, cs, st))

            for l, src in ((0, wab_d), (1, wcd_d)):
                for g in range(6):
                    if l == 1 and g < 5:
                        continue
                    ncol = 520 if (l == 1 and g == 5) else 512
                    casts = [(lambda t: t[:, 0:4096].rearrange("p (s c n) -> p c s n", s=4, c=8),
                              lambda t, ncol=ncol: t[:, 0:8 * ncol].rearrange("p (c n) -> p c n", c=8)[:, :, 0:512].rearrange(
                                  "p c (s n) -> p c s n", s=4))]
                    if ncol == 520:
                        casts.append((lambda t: wfs[:], lambda t: t[:, 0:8 * 520].rearrange("p (c n) -> p c n", c=8)[:, :, 512:520]))
                    add_piece([(lambda t, ncol=ncol: t[:, 0:8 * ncol].rearrange("p (c n) -> p c n", c=8),
                                src[:, g * 512:g * 512 + ncol].rearrange("(c p) n -> p c n", p=128))],
                              casts,
                              [(wg16[l, g * 4:(g + 1) * 4].rearrange("s p c n -> p s (c n)"),
                                lambda t: t[:, 0:4096].rearrange("p (s x) -> p s x", s=4),
                                ["wg16:%d:%d" % (l, g * 4 + i) for i in range(4)])])
            npc = len(pieces)
            for i in range(npc + 1):
                if i < npc:
                    pieces[i][0](i % NSTG)
                if i >= 1:
                    pieces[i - 1][1]((i - 1) % NSTG)
                    pieces[i - 1][2]((i - 1) % NSTG)
            P.barrier()

        class BgCast:
            def __init__(self, mixT, pieces, every):
                base = mixT[:, 4:8, :].rearrange("p a n -> p (a n)")
                self.s32 = [base[:, i * 2048:(i + 1) * 2048].bitcast(F32) for i in range(3)]
                self.s16 = [base[:, 6144 + i * 1024:6144 + (i + 1) * 1024] for i in range(2)]
                self.pieces = pieces
                self.t = 0
                self.n = 0
                self.every = every

            def view(self, ap, like):
                if len(like.shape) == 3:
                    return ap.rearrange("p (c n) -> p c n", c=like.shape[1])
                return ap

            def step(self):
                t, pcs = self.t, self.pieces
                if t < len(pcs):
                    src, dst, names = pcs[t]
                    P.dma("sp", self.view(self.s32[t % 3], src), src, reads=["bg32_%d" % (t % 3)], writes=["bg32_%d" % (t % 3)])
                u = t - 2
                if 0 <= u < len(pcs):
                    src, dst, names = pcs[u]
                    i32, i16 = u % 3, u % 2
                    eng = "dve"
                    P.op(eng, lambda e: e.tensor_copy(out=self.s16[i16], in_=self.s32[i32]), reads=["bg32_%d" % i32], writes=["bg16_%d" % i16])
                    P.dma("sp", dst, self.view(self.s16[i16], dst), reads=["bg16_%d" % i16], writes=names)
                self.t += 1

            def tick(self):
                self.n += 1
                if self.n % self.every == 0 and self.t < len(self.pieces) + 2:
                    self.step()

            def flush(self):
                while self.t < len(self.pieces) + 2:
                    self.step()

        def col_piece(src2d, dst3d, names):
            return (src2d.rearrange("(c p) n -> p c n", p=128), dst3d, names)

        bg_l0 = []
        for n0 in range(8):
            bg_l0.append(col_piece(wo_d[0, :, n0 * 128:(n0 + 1) * 128], wo16[0, :, :, n0 * 128:(n0 + 1) * 128], ["wo16:0"]))
        for f in range(NF):
            bg_l0.append(col_piece(wfi_d[0, :, f * 128:(f + 1) * 128], wfi16[0, f, :, :, 0:128], ["wfi16:0"]))
            bg_l0.append(col_piece(wfi_d[0, :, DFF + f * 128:DFF + (f + 1) * 128], wfi16[0, f, :, :, 128:256], ["wfi16:0"]))
            bg_l0.append((wfo_d[0, f * 128:(f + 1) * 128, :], wfo16[0, f], ["wfo16:0"]))
        for sl in range(20):
            bg_l0.append(col_piece(wcd_d[:, sl * 128:(sl + 1) * 128], wg16[1, sl], ["wg16:1:%d" % sl]))
        for n0 in range(8):
            bg_l0.append(col_piece(wo_d[1, :, n0 * 128:(n0 + 1) * 128], wo16[1, :, :, n0 * 128:(n0 + 1) * 128], ["wo16:1"]))
        bg_l1 = []
        for f in range(NF):
            bg_l1.append(col_piece(wfi_d[1, :, f * 128:(f + 1) * 128], wfi16[1, f, :, :, 0:128], ["wfi16:1"]))
            bg_l1.append(col_piece(wfi_d[1, :, DFF + f * 128:DFF + (f + 1) * 128], wfi16[1, f, :, :, 128:256], ["wfi16:1"]))
            bg_l1.append((wfo_d[1, f * 128:(f + 1) * 128, :], wfo16[1, f], ["wfo16:1"]))
        bgs = {"st": None}

        def load_mod(l, b):
            for j, c0 in enumerate((0, 8, 24, 32)):
                P.op("dve", lambda e, j=j, c0=c0: e.tensor_copy(out=modT[:, j, :], in_=modall[:, l, c0:c0 + 8, b]),
                     reads=["modall"], writes=["modT%d" % j])

        def transposes(sub):
            jsh, jsc = (0, 1) if sub == 0 else (2, 3)
            for tg in range(4):
                for c in range(8):
                    pb, pbn = bank()
                    for j in range(4):
                        ti = tg * 4 + j
                        P.op("pe", lambda e, pb=pb, j=j, ti=ti, c=c: e.transpose(
                            pb[:, j * 128:(j + 1) * 128], X[:, ti, c * 128:(c + 1) * 128], ident[:]),
                            reads=["X:%d" % ti, "ident"], writes=[pbn], inc=(j == 3))
                    rd = [pbn, "modT%d" % jsh, "modT%d" % jsc]
                    wr = ["hT:%d:%d" % (c, tg)]
                    if ctr["ev"] % 2 == 0:
                        P.op("act", lambda e, pb=pb, c=c, tg=tg: e.activation(
                            out=hT[:, c, tg * 512:(tg + 1) * 512], in_=pb[:], func=AF.Identity,
                            bias=modT[:, jsh, c:c + 1], scale=modT[:, jsc, c:c + 1]), rd, wr)
                    else:
                        P.op("dve", lambda e, pb=pb, c=c, tg=tg: e.tensor_scalar(
                            out=hT[:, c, tg * 512:(tg + 1) * 512], in0=pb[:], scalar1=modT[:, jsc, c:c + 1],
                            scalar2=modT[:, jsh, c:c + 1], op0=ALU.mult, op1=ALU.add), rd, wr)
                    ctr["ev"] += 1

        def hT_reads(tgs):
            return ["hT:%d:%d" % (c, tg) for c in range(8) for tg in tgs]

        def ln_residual(ti, zts, LNG, LNB, psrc):
            p = ti % 2
            zt = zts[p]
            zn = ["zt%d_%d" % (p, hf) for hf in range(2)]
            for hf in range(2):
                pb, pbn = psrc[hf]
                sl = slice(hf * 512, (hf + 1) * 512)
                P.op("dve", lambda e, pb=pb, sl=sl: e.scalar_tensor_tensor(out=zt[:, sl], in0=X[:, ti, sl], scalar=ALPHA, in1=pb[:],
                                                                            op0=ALU.mult, op1=ALU.add),
                     reads=["X:%d" % ti, pbn], writes=[zn[hf]])
                P.op("dve", lambda e, sl=sl, hf=hf: e.bn_stats(out=stt[:, p, hf, :], in_=zt[:, sl]), reads=[zn[hf]], writes=["stt%d_%d" % (p, hf)])
            P.op("dve", lambda e: e.bn_aggr(out=mv[:, p, 0:2], in_=stt[:, p].rearrange("p a b -> p (a b)")),
                 reads=["stt%d_0" % p, "stt%d_1" % p], writes=["mv%d" % p])
            P.op("act", lambda e: e.activation(out=mv[:, p, 2:3], in_=mv[:, p, 1:2], func=AF.Ln, bias=epsc[:], scale=1.0),
                 reads=["mv%d" % p, "epsc"], writes=["mvb%d" % p])
            P.op("act", lambda e: e.activation(out=mv[:, p, 3:4], in_=mv[:, p, 2:3], func=AF.Exp, scale=-0.5),
                 reads=["mvb%d" % p], writes=["mvc%d" % p])
            P.op("dve", lambda e: e.tensor_scalar(out=mv[:, p, 4:5], in0=mv[:, p, 0:1], scalar1=-1.0, scalar2=mv[:, p, 3:4],
                                                  op0=ALU.mult, op1=ALU.mult), reads=["mv%d" % p, "mvc%d" % p], writes=["mvd%d" % p])
            P.op("act", lambda e: e.activation(out=zt[:], in_=zt[:], func=AF.Identity, bias=mv[:, p, 4:5], scale=mv[:, p, 3:4]),
                 reads=zn + ["mvc%d" % p, "mvd%d" % p], writes=zn)
            P.op("pool", lambda e: e.tensor_tensor(out=zt[:], in0=zt[:], in1=LNG[:], op=ALU.mult),
                 reads=zn + ["LNG"], writes=zn)

            def stage_b():
                P.op("dve", lambda e: e.tensor_tensor(out=X[:, ti, :], in0=zt[:], in1=LNB[:], op=ALU.add),
                     reads=zn + ["LNB"], writes=["X:%d" % ti])
            return stage_b

        def load_ln(l, sub, b, LNG, LNB, GB):
            off = 2048 if sub == 0 else 5120
            P.dma("sp", GB[:], ada_s[l, b, off:off + 1024].partition_broadcast(128), writes=["GB"])
            P.op("dve", lambda e: e.tensor_scalar(out=GB[:], in0=GB[:], scalar1=1.0, scalar2=None, op0=ALU.add),
                 reads=["GB"], writes=["GB"])
            P.dma("sp", LNG[:], lng_d[l, sub].partition_broadcast(128), writes=["LNG"])
            P.dma("sp", LNB[:], lnb_d[l, sub].partition_broadcast(128), writes=["LNB"])

        def mixer(l, b):
            with ExitStack() as ms:
                mixT = sb(ms, "mixT", [128, 8, T], BF16)
                with ExitStack() as gs:
                    QT = sb(gs, "QT", [128, 2, T], BF16)
                    KT = sb(gs, "KT", [128, 2, T], BF16)
                    Vaug = sb(gs, "Vaug", [128, NT, 2, 128], BF16)
                    wgr = [sb(gs, "wgr%d" % i, [128, 3, 8, 128], BF16) for i in range(2)]
                    PTs = [sb(gs, "PT%d" % i, [128, 512], BF16) for i in range(4)]
                    Eg = [sb(gs, "Eg%d" % i, [128, 10, 128], BF16) for i in range(2)]
                    rec = sb(gs, "rec", [128, 512], F32)
                    rec2 = sb(gs, "rec2", [128, 512], F32)
                    gate = sb(gs, "gate", [128, NT, 8], F32)
                    top8 = sb(gs, "top8", [128, NT, 8], F32)
                    sel = sb(gs, "sel", [128, NT, 8], F32)
                    negpad = sb(gs, "negpad", [128, NT, 72], BF16)
                    gts = [(gate, top8, sel, negpad)]
                    if l == 1:
                        gts.append((sb(gs, "gate2", [128, NT, 8], F32), sb(gs, "top82", [128, NT, 8], F32),
                                    sb(gs, "sel2", [128, NT, 8], F32), sb(gs, "negpad2", [128, NT, 72], BF16)))
                        P.op("pool", lambda e: e.memset(gts[1][3][:], 0.0), writes=["negpad1"])
                    ksum = sb(gs, "ksum", [128, 2, 8], F32)
                    kmb = sb(gs, "kmb", [128, 2, 8], BF16)
                    zf = sb(gs, "zf", [128, NT, 8], F32)
                    cwt = sb(gs, "cwt", [128, 2, NT, 8], F32)
                    offn = sb(gs, "offn", [128, NT + 1, 8], F32)
                    ncum = sb(gs, "ncum", [128, NT, 8], F32)
                    bfox = sb(gs, "bfox", [128, 4, NT, 8], F32)
                    P.op("pool", lambda e: e.memset(Vaug[:, :, :, 64:128], 1.0), writes=["Vaug"])
                    P.op("pool", lambda e: e.memset(negpad[:], 0.0), writes=["negpad0"])

                    def pt_next():
                        i = ctr["pt"] % 4
                        ctr["pt"] += 1
                        return PTs[i], "PT%d" % i

                    SB = [0, 1, 2]
                    OB = [3, 4, 5, 6]

                    def load_group_w(gi, slices):
                        w = wgr[gi % 2]
                        wn = "wgr%d" % (gi % 2)
                        for j, s in enumerate(slices):
                            P.dma("sp", w[:, j], wg16[l, s], reads=["wg16:%d:%d" % (l, s)], writes=[wn + ":%d" % j])
                        return w, wn

                    def proj_T(dst_fn, w, wn, j):
                        for tg in range(4):
                            pb, pbn = bank()
                            for k in range(8):
                                mm(pb[:], w[:, j, k, :], hT[:, k, tg * 512:(tg + 1) * 512], k == 0, k == 7,
                                   hT_reads([tg]) + [wn + ":%d" % j], [pbn])
                            dst_fn(tg, pb, pbn)

                    def proj_V(w, wn, tok_ap_fn, pair):
                        for tg in range(4):
                            pb, pbn = bank()
                            for j in range(4):
                                slot = tg * 4 + j
                                sl, tgs = tok_ap_fn(slot)
                                for k in range(8):
                                    mm(pb[:, j * 128:(j + 1) * 128], hT[:, k, sl], w[:, 2, k, :], k == 0, k == 7,
                                       hT_reads(tgs) + [wn + ":2"], [pbn], inc=(k == 7 and j == 3))
                            if pair:
                                evac(Vaug[:, tg * 4:(tg + 1) * 4, :, 0:64],
                                     pb[:].rearrange("p (t h d) -> p t h d", t=4, h=2), [pbn], ["Vaug"])
                            else:
                                evac(Vaug[:, tg * 4:(tg + 1) * 4, 0, :],
                                     pb[:].rearrange("p (t d) -> p t d", t=4), [pbn], ["Vaug"])

                    contig = lambda slot: (slice(slot * 128, (slot + 1) * 128), [slot // 4])

                    def load_E(gi, idxs):
                        e_ = Eg[gi % 2]
                        en = "Eg%d" % (gi % 2)
                        for j, ix in enumerate(idxs):
                            P.dma("sp", e_[:, j, :], E_d[ix], reads=["E_d"], writes=[en])
                        return e_, en

                    def dense_attn(units_for_chunk, finish_chunk, LOOK=2, SBK=(0, 1, 2), PAIR=False, MENG="pool"):
                        for c in range(4):
                            units = units_for_chunk(c)
                            SBK = list(SBK)

                            def issue_S(u):
                                pb, pbn = bank(SBK)
                                qlo = u["qlo"]
                                mm(pb[:, qlo:512], u["kT"], u["qT"], True, True, u["sreads"], [pbn])
                                u["sb"], u["sbn"] = pb, pbn

                            last_idx = {}
                            for i, u in enumerate(units):
                                for (_l, _r, _ob, obn) in u["pv"]:
                                    last_idx[obn] = i
                            for i in range(min(LOOK, len(units))):
                                issue_S(units[i])
                            for i, u in enumerate(units):
                                if PAIR:
                                    if i % 2 == 0:
                                        for k2 in (i + LOOK, i + LOOK + 1):
                                            if k2 < len(units):
                                                issue_S(units[k2])
                                elif i + LOOK < len(units):
                                    issue_S(units[i + LOOK])
                                if bgs["st"] is not None:
                                    bgs["st"].tick()
                                qlo = u["qlo"]
                                pt, ptn = pt_next()
                                bias = u["bias"]
                                if bias is None:
                                    P.op("act", lambda e, pt=pt, u=u, qlo=qlo: e.activation(
                                        out=pt[:, qlo:512], in_=u["sb"][:, qlo:512], func=AF.Exp, scale=0.125),
                                        reads=[u["sbn"]], writes=[ptn])
                                else:
                                    P.op("act", lambda e, pt=pt, u=u, qlo=qlo, bias=bias: e.activation(
                                        out=pt[:, qlo:512], in_=u["sb"][:, qlo:512], func=AF.Exp, scale=0.125, bias=bias),
                                        reads=[u["sbn"], "bfox"], writes=[ptn])
                                for (ii, et, en) in u["masks"]:
                                    P.op(MENG, lambda e, pt=pt, ii=ii, et=et: e.tensor_tensor(
                                        out=pt[:, ii * 128:(ii + 1) * 128], in0=pt[:, ii * 128:(ii + 1) * 128], in1=et, op=ALU.mult),
                                        reads=[ptn, en], writes=[ptn])
                                for (lhsT, lreads, ob, obn) in u["pv"]:
                                    lastu = (last_idx[obn] == i)
                                    if u["diag"] and qlo > 0:
                                        jj = qlo // 128
                                        for ii in range(jj, 4):
                                            mm(ob[:, ii * 128:(ii + 1) * 128], lhsT, pt[:, ii * 128:(ii + 1) * 128],
                                               u["first"], lastu and ii == 3, [ptn] + lreads, [obn], inc=(ii == 3))
                                    else:
                                        mm(ob[:], lhsT, pt[:], u["first"], lastu, [ptn] + lreads, [obn], inc=True)
                            finish_chunk(c)

                    gi = 0
                    if l == 0:
                        with ExitStack() as at:
                            r0 = sb(at, "r0", [128, 512], F32)
                            r1 = sb(at, "r1", [128, 512], F32)
                            oo = sb(at, "oo", [128, 512], F32)
                            t1 = sb(at, "t1", [128, 512], F32)
                            sq = sb(at, "sq", [128, 512], F32)
                            if b == 0:
                                bgs["st"] = BgCast(mixT, bg_l0, 3)
                            for h in range(4):
                                w, wn = load_group_w(gi, [h, 4 + h, 8 + h])
                                e_, en = load_E(gi, [h, 12 + h])
                                gi += 1
                                proj_T(lambda tg, pb, pbn: evac(QT[:, 0, tg * 512:(tg + 1) * 512], pb[:], [pbn], ["QT:%d" % tg]), w, wn, 0)
                                proj_T(lambda tg, pb, pbn: evac(KT[:, 0, tg * 512:(tg + 1) * 512], pb[:], [pbn], ["KT:%d" % tg]), w, wn, 1)
                                proj_V(w, wn, contig, False)
                                obs = [(banks[3], "ps3"), (banks[4], "ps4"), (banks[5], "ps5"), (banks[6], "ps6")]

                                def units_A(c, e_=e_, en=en):
                                    us = []
                                    for j in range(4 * c + 4):
                                        for m in range(2):
                                            jj = j - 4 * c
                                            qlo = max(jj, 0) * 128
                                            masks = []
                                            for ii in range(4):
                                                i = 4 * c + ii
                                                if j == i:
                                                    masks.append((ii, e_[:, 0, :], en))
                                                elif j == i - 1:
                                                    masks.append((ii, e_[:, 1, :], en))
                                            rb = m * 64
                                            us.append(dict(
                                                qlo=qlo, diag=(jj >= 0), first=(j == 0),
                                                kT=KT[rb:rb + 64, 0, j * 128:(j + 1) * 128],
                                                qT=QT[rb:rb + 64, 0, c * 512 + qlo:(c + 1) * 512],
                                                sreads=["KT:%d" % (j // 4), "QT:%d" % c], bias=None, masks=masks,
                                                pv=[(Vaug[:, j, 0, :], ["Vaug"], obs[2 * m][0], obs[2 * m][1]),
                                                    (ones16[:], ["ones16"], obs[2 * m + 1][0], obs[2 * m + 1][1])]))
                                    return us

                                def finish_A(c, h=h):
                                    cs = slice(c * 512, (c + 1) * 512)
                                    P.op("act", lambda e: e.activation(out=r0[:], in_=banks[4][:], func=AF.Ln), reads=["ps4"], writes=["r0"])
                                    P.op("dve", lambda e: e.tensor_copy(out=oo[:], in_=banks[3][:]), reads=["ps3"], writes=["oo"])
                                    P.op("act", lambda e: e.activation(out=r1[:], in_=banks[6][:], func=AF.Ln), reads=["ps6"], writes=["r1"])
                                    P.op("dve", lambda e: e.tensor_copy(out=t1[:], in_=banks[5][:]), reads=["ps5"], writes=["t1"])
                                    P.op("act", lambda e: e.activation(out=r0[:], in_=r0[:], func=AF.Exp, scale=-1.0), reads=["r0"], writes=["r0"])
                                    P.op("act", lambda e: e.activation(out=r1[:], in_=r1[:], func=AF.Exp, scale=-1.0), reads=["r1"], writes=["r1"])
                                    P.op("pool", lambda e: e.tensor_tensor(out=oo[:], in0=oo[:], in1=r0[:], op=ALU.mult),
                                         reads=["oo", "r0"], writes=["oo"])
                                    P.op("pool", lambda e: e.tensor_tensor(out=t1[:], in0=t1[:], in1=r1[:], op=ALU.mult),
                                         reads=["t1", "r1"], writes=["t1"])
                                    P.op("dve", lambda e: e.scalar_tensor_tensor(out=oo[:], in0=t1[:], scalar=neglam[:, 0:1], in1=oo[:],
                                                                                  op0=ALU.mult, op1=ALU.add),
                                         reads=["t1", "oo", "neglam"], writes=["oo"])
                                    P.op("act", lambda e: e.activation(out=sq[:], in_=oo[:], func=AF.Square), reads=["oo"], writes=["sq"])
                                    pm, pmn = bank([0, 1, 2, 7])
                                    mm(pm[:], ones32[:], sq[:], True, True, ["sq", "ones32"], [pmn])
                                    P.op("act", lambda e: e.activation(out=sq[:], in_=pm[:], func=AF.Ln, bias=epsc[:], scale=1.0 / 128.0),
                                         reads=[pmn, "epsc"], writes=["sq"])
                                    P.op("act", lambda e: e.activation(out=sq[:], in_=sq[:], func=AF.Exp, scale=-0.5), reads=["sq"], writes=["sq"])
                                    P.op("dve", lambda e: e.scalar_tensor_tensor(out=mixT[:, h, cs], in0=oo[:], scalar=subg[:, 0:1], in1=sq[:],
                                                                                  op0=ALU.mult, op1=ALU.mult),
                                         reads=["oo", "sq", "subg"], writes=["mixT:%d" % h])

                                dense_attn(units_A, finish_A, LOOK=2, SBK=(0, 1, 2, 7), PAIR=True, MENG="dve")
                            if bgs["st"] is not None:
                                bgs["st"].flush()
                                bgs["st"] = None
                            P.barrier()
                        P.op("pool", lambda e: e.memset(Vaug[:, :, :, 64:128], 1.0), writes=["Vaug"])
                        bt_ = ExitStack()
                        accs = [sb(bt_, "acc%d" % i, [128, T], F32) for i in range(2)]

                        for j in range(4):
                            w, wn = load_group_w(gi, [12 + j, 16 + j, 20 + j])
                            eidx = []
                            for s in range(2):
                                hb = 2 * j + s
                                eidx += [4 + hb, 20 + hb, 28 + hb, 36 + hb, 44 + hb]
                            e_, en = load_E(gi, eidx)
                            gi += 1
                            proj_T(lambda tg, pb, pbn: evac(QT[:, 0, tg * 512:(tg + 1) * 512], pb[:], [pbn], ["QT:%d" % tg]), w, wn, 0)
                            proj_T(lambda tg, pb, pbn: evac(KT[:, 0, tg * 512:(tg + 1) * 512], pb[:], [pbn], ["KT:%d" % tg]), w, wn, 1)
                            QA = ["QT:%d" % i for i in range(4)]
                            KA = ["KT:%d" % i for i in range(4)]

                            def pth_next():
                                i = ctr.get("pth", 0) % 8
                                ctr["pth"] = ctr.get("pth", 0) + 1
                                return PTs[i // 2][:, (i % 2) * 256:(i % 2) * 256 + 256], "PTh%d" % i

                            def window_pattern(s, nset, kset_ap, qset_ap, e2, vslot, obank_of, flush):
                                rb = s * 64
                                pts = {}

                                def issue(i):
                                    ncol = 256 if i + 1 < nset else 128
                                    pb, pbn = bank(SB)
                                    mm(pb[:, 0:ncol], kset_ap(rb, i), qset_ap(rb, i, ncol), True, True, QA + KA, [pbn])
                                    pt, ptn = pth_next()
                                    P.op("act", lambda e: e.activation(out=pt[:, 0:ncol], in_=pb[:, 0:ncol], func=AF.Exp, scale=0.125),
                                         reads=[pbn], writes=[ptn])
                                    P.op("dve", lambda e: e.tensor_tensor(out=pt[:, 0:ncol], in0=pt[:, 0:ncol], in1=e2[:, 0:ncol], op=ALU.mult),
                                         reads=[ptn, en], writes=[ptn])
                                    pts[i] = (pt, ptn)

                                issue(0)
                                if nset > 1:
                                    issue(1)
                                yield
                                for i in range(nset):
                                    if i + 2 < nset:
                                        issue(i + 2)
                                    ob, obn, col = obank_of(i)
                                    if i > 0:
                                        pt, ptn = pts[i - 1]
                                        mm(ob[:, col:col + 128], Vaug[:, vslot(i - 1), s, :], pt[:, 128:256], True, False,
                                           [ptn, "Vaug"], [obn], inc=False)
                                    pt, ptn = pts[i]
                                    mm(ob[:, col:col + 128], Vaug[:, vslot(i), s, :], pt[:, 0:128], i == 0, True,
                                       [ptn, "Vaug"], [obn], inc=True)
                                    flush(i, ob, obn)
                                    yield

                            SB = [0, 1, 2, 7]
                            proj_V(w, wn, contig, True)

                            def pat1(s):
                                acc, an = accs[s], "acc%d" % s
                                e2 = e_[:, s * 5:s * 5 + 2, :].rearrange("p a q -> p (a q)")
                                cur = {}

                                def ob1(i):
                                    if i % 4 == 0:
                                        cur["b"] = bank(OB)
                                    return cur["b"][0], cur["b"][1], (i % 4) * 128

                                def fl1(i, ob, obn):
                                    if i % 4 == 3:
                                        n = i // 4
                                        evac(acc[:, n * 512:(n + 1) * 512], ob[:], [obn], [an])

                                return window_pattern(s, 16,
                                                      lambda rb, i: KT[rb:rb + 64, 0, i * 128:(i + 1) * 128],
                                                      lambda rb, i, ncol: QT[rb:rb + 64, 0, i * 128:i * 128 + ncol],
                                                      e2, lambda i: i, ob1, fl1)

                            run_interleaved([pat1(0), pat1(1)])
                            proj_V(w, wn, lambda slot: (slice(512 * (slot % 4) + slot // 4, 512 * (slot % 4) + 512, 4), [slot % 4]), True)

                            def pat2(s, r):
                                acc, an = accs[s], "acc%d" % s
                                e2 = e_[:, s * 5 + 2:s * 5 + 4, :].rearrange("p a q -> p (a q)")
                                cur = {"b": bank(OB)}

                                def fl2(i, ob, obn):
                                    if i == 3:
                                        av = acc[:, :].rearrange("p (n u f) -> p n u f", n=4, u=128, f=4)[:, :, :, r]
                                        P.op("dve", lambda e: e.tensor_tensor(out=av, in0=av, in1=ob[:].rearrange("p (n u) -> p n u", n=4),
                                                                              op=ALU.add), reads=[obn, an], writes=[an])

                                return window_pattern(s, 4,
                                                      lambda rb, n: KT[rb:rb + 64, 0, 512 * n + r:512 * n + 512:4],
                                                      lambda rb, n, ncol: QT[rb:rb + 64, 0, 512 * n + r:512 * n + 4 * ncol:4],
                                                      e2, lambda n: r * 4 + n,
                                                      lambda n: (cur["b"][0], cur["b"][1], n * 128), fl2)

                            for r in range(4):
                                run_interleaved([pat2(0, r), pat2(1, r)])
                            proj_V(w, wn, lambda slot: (slice(slot, T, 16), [0, 1, 2, 3]), True)
                            p3u = [(r16, s) for r16 in range(16) for s in range(2)]
                            ob3 = {}
                            pts3 = {}
                            L3 = 4

                            def issue3(idx):
                                r16, s = p3u[idx]
                                rb = s * 64
                                e3 = e_[:, s * 5 + 4, :]
                                pb, pbn = bank(SB)
                                mm(pb[:, 0:128], KT[rb:rb + 64, 0, r16:T:16], QT[rb:rb + 64, 0, r16:T:16], True, True, QA + KA, [pbn])
                                pt, ptn = pth_next()
                                P.op("act", lambda e: e.activation(out=pt[:, 0:128], in_=pb[:, 0:128], func=AF.Exp, scale=0.125),
                                     reads=[pbn], writes=[ptn])
                                P.op("dve", lambda e: e.tensor_tensor(out=pt[:, 0:128], in0=pt[:, 0:128], in1=e3, op=ALU.mult),
                                     reads=[ptn, en], writes=[ptn])
                                pts3[idx] = (pt, ptn)

                            for idx in range(min(L3, len(p3u))):
                                issue3(idx)
                            for idx, (r16, s) in enumerate(p3u):
                                if idx + L3 < len(p3u):
                                    issue3(idx + L3)
                                rr = r16 % 4
                                if rr == 0:
                                    ob3[s] = bank(OB)
                                ob, obn = ob3[s]
                                pt, ptn = pts3.pop(idx)
                                mm(ob[:, rr * 128:(rr + 1) * 128], Vaug[:, r16, s, :], pt[:, 0:128], True, True, [ptn, "Vaug"], [obn],
                                   inc=True)
                                if rr == 3:
                                    r0_ = r16 - 3
                                    acc, an = accs[s], "acc%d" % s
                                    av = acc[:, :].rearrange("p (u f) -> p f u", f=16)[:, r0_:r0_ + 4, :]
                                    P.op("dve", lambda e: e.tensor_tensor(out=av, in0=av, in1=ob[:].rearrange("p (f u) -> p f u", f=4),
                                                                          op=ALU.add), reads=[obn, an], writes=[an])
                            for s in range(2):
                                acc, an = accs[s], "acc%d" % s
                                for c in range(4):
                                    cs = slice(c * 512, (c + 1) * 512)
                                    rc, rcn = (rec, "rec") if c % 2 == 0 else (rec2, "rec2")
                                    P.op("act", lambda e, acc=acc, cs=cs, rc=rc: e.activation(out=rc[0:64, :], in_=acc[64:128, cs], func=AF.Ln),
                                         reads=[an, rcn], writes=[rcn])
                                    P.op("act", lambda e, rc=rc: e.activation(out=rc[0:64, :], in_=rc[0:64, :], func=AF.Exp, scale=-1.0),
                                         reads=[rcn], writes=[rcn])
                                    P.op("pool", lambda e, acc=acc, cs=cs, s=s, j=j, rc=rc: e.tensor_tensor(
                                        out=mixT[s * 64:(s + 1) * 64, 4 + j, cs], in0=acc[0:64, cs], in1=rc[0:64, :], op=ALU.mult),
                                        reads=[an, rcn], writes=["mixT:%d" % (4 + j)])
                        P.barrier()
                        bt_.close()
                    else:
                        def finish_pair(s, mc):
                            def fin(c, s=s, mc=mc):
                                ob, obn = cur_o["b%d" % s]
                                cs = slice(c * 512, (c + 1) * 512)
                                rc, rcn = (rec, "rec") if s == 0 else (rec2, "rec2")
                                P.op("act", lambda e: e.activation(out=rc[0:64, :], in_=ob[64:128, :], func=AF.Ln), reads=[obn, rcn], writes=[rcn])
                                P.op("act", lambda e: e.activation(out=rc[0:64, :], in_=rc[0:64, :], func=AF.Exp, scale=-1.0), reads=[rcn], writes=[rcn])
                                P.op("dve", lambda e: e.tensor_tensor(out=mixT[s * 64:(s + 1) * 64, mc, cs], in0=ob[0:64, :], in1=rc[0:64, :],
                                                                      op=ALU.mult), reads=[obn, rcn], writes=["mixT:%d" % mc])
                            return fin

                        cur_o = {}

                        def merge_units(ufs):
                            def mu(c):
                                lists = [uf(c) for uf in ufs]
                                out = []
                                for i in range(max(len(x) for x in lists)):
                                    for x in lists:
                                        if i < len(x):
                                            out.append(x[i])
                                return out
                            return mu

                        def merged_finish(mc):
                            def mf(c):
                                for s in range(2):
                                    finish_pair(s, mc)(c)
                            return mf

                        if b == 0:
                            bgs["st"] = BgCast(mixT, bg_l1, 4)
                        for j in range(4):
                            w, wn = load_group_w(gi, [j, 4 + j, 8 + j])
                            e_, en = load_E(gi, [2 * j, 12 + 2 * j, 2 * j + 1, 12 + 2 * j + 1])
                            gi += 1
                            for s in range(2):
                                P.dma("sp", KT[64:72, s, :], kaug16, reads=["KTaug%d" % s], writes=["KTaug%d" % s])

                            def dq(tg, pb, pbn):
                                for s in range(2):
                                    evac(QT[0:64, s, tg * 512:(tg + 1) * 512], pb[s * 64:(s + 1) * 64, :], [pbn], ["QT%d:%d" % (s, tg)])

                            def dk(tg, pb, pbn):
                                for s in range(2):
                                    evac(KT[0:64, s, tg * 512:(tg + 1) * 512], pb[s * 64:(s + 1) * 64, :], [pbn], ["KT%d:%d" % (s, tg)])

                            proj_T(dq, w, wn, 0)
                            proj_T(dk, w, wn, 1)
                            proj_V(w, wn, contig, True)
                            ufs = []
                            ggens = []
                            for s in range(2):
                                QAs = ["QT%d:%d" % (s, i) for i in range(4)]
                                KAs = ["KT%d:%d" % (s, i) for i in range(4)]
                                def gate_gen(s=s, QAs=QAs, KAs=KAs):
                                    gate_, top8_, sel_, negpad_ = gts[s]
                                    gn, tn, sn_, nn = "gate%d" % s, "top8%d" % s, "sel%d" % s, "negpad%d" % s
                                    P.op("dve", lambda e: e.tensor_reduce(out=ksum[0:64, s, :],
                                                                          in_=KT[0:64, s, :].rearrange("p (n t) -> p n t", t=256),
                                                                          axis=AX.X, op=ALU.add), reads=KAs, writes=["ksum%d" % s])
                                    P.op("dve", lambda e: e.tensor_copy(out=kmb[0:64, s, :], in_=ksum[0:64, s, :]),
                                         reads=["ksum%d" % s], writes=["kmb%d" % s])
                                    yield
                                    pg, pgn = bank()
                                    for ti in range(NT):
                                        mm(pg[:, ti * 8:(ti + 1) * 8], QT[0:64, s, ti * 128:(ti + 1) * 128], kmb[0:64, s, :], True, True,
                                           QAs + ["kmb%d" % s], [pgn], inc=(ti == NT - 1))
                                    yield
                                    P.op("dve", lambda e: e.tensor_tensor(out=gate_[:], in0=pg[:, 0:128].rearrange("p (t n) -> p t n", n=8),
                                                                          in1=gmask[:], op=ALU.add), reads=[pgn, "gmask"], writes=[gn])
                                    for ti in range(NT):
                                        P.op("dve", lambda e, ti=ti: e.max(out=top8_[:, ti, :], in_=gate_[:, ti, :]), reads=[gn], writes=[tn])
                                    P.op("dve", lambda e: e.tensor_tensor(out=sel_[:], in0=gate_[:], in1=top8_[:, :, 3:4].to_broadcast([128, NT, 8]),
                                                                          op=ALU.is_ge), reads=[gn, tn], writes=[sn_])
                                    P.op("dve", lambda e: e.tensor_scalar(out=negpad_[:, :, 64:72], in0=sel_[:], scalar1=1.0, scalar2=-NEG,
                                                                          op0=ALU.subtract, op1=ALU.mult), reads=[sn_], writes=[nn])
                                    yield
                                    for tg in range(4):
                                        pa, pan = bank()
                                        for jj in range(4):
                                            ti = tg * 4 + jj
                                            mm(pa[0:72, jj * 128:(jj + 1) * 128], negpad_[:, ti, :], ident16[:], True, True,
                                               [nn, "ident16"], [pan], inc=(jj == 3))
                                        evac(QT[64:72, s, tg * 512:(tg + 1) * 512], pa[64:72, :], [pan], ["QTaug%d:%d" % (s, tg)])

                                ggens.append(gate_gen())

                                def units_C(c, s=s, e_=e_, en=en, QAs=QAs, KAs=KAs):
                                    if cur_o.get("c%d" % s) != (s, c, "C", j):
                                        cur_o["b%d" % s] = bank(OB)
                                        cur_o["c%d" % s] = (s, c, "C", j)
                                    ob, obn = cur_o["b%d" % s]
                                    us = []
                                    for kt in range(4 * c + 4):
                                        jj = kt - 4 * c
                                        qlo = max(jj, 0) * 128
                                        masks = []
                                        for ii in range(4):
                                            i = 4 * c + ii
                                            if kt == i:
                                                masks.append((ii, e_[:, 2 * s, :], en))
                                            elif kt == i - 1:
                                                masks.append((ii, e_[:, 2 * s + 1, :], en))
                                        us.append(dict(
                                            qlo=qlo, diag=(jj >= 0), first=(kt == 0),
                                            kT=KT[0:72, s, kt * 128:(kt + 1) * 128],
                                            qT=QT[0:72, s, c * 512 + qlo:(c + 1) * 512],
                                            sreads=["KT%d:%d" % (s, kt // 4), "KTaug%d" % s, "QT%d:%d" % (s, c), "QTaug%d:%d" % (s, c)],
                                            bias=None, masks=masks,
                                            pv=[(Vaug[:, kt, s, :], ["Vaug"], ob, obn)]))
                                    return us

                                ufs.append(units_C)
                            run_interleaved(ggens)
                            dense_attn(merge_units(ufs), merged_finish(j), LOOK=3, SBK=(0, 1, 2, 7), MENG="dve")

                        if bgs["st"] is not None:
                            bgs["st"].flush()
                            bgs["st"] = None
                            P.barrier()
                        pf, pfn = bank()
                        for ti in range(NT):
                            for k in range(8):
                                mm(pf[:, ti * 8:(ti + 1) * 8], hT[:, k, ti * 128:(ti + 1) * 128], wfs[:, k, :], k == 0, k == 7,
                                   hT_reads([ti // 4]) + ["wfs"], [pfn], inc=(k == 7 and ti == NT - 1))
                        P.op("dve", lambda e: e.tensor_tensor(out=zf[:], in0=pf[:, 0:128].rearrange("p (t n) -> p t n", n=8),
                                                              in1=fbb[:, :].unsqueeze(1).to_broadcast([128, NT, 8]), op=ALU.add),
                             reads=[pfn, "fbb"], writes=["zf"])
                        P.op("act", lambda e: e.activation(out=zf[:], in_=zf[:], func=AF.Exp, scale=-1.0), reads=["zf"], writes=["zf"])
                        P.op("act", lambda e: e.activation(out=zf[:], in_=zf[:], func=AF.Ln, bias=1.0, scale=1.0), reads=["zf"], writes=["zf"])
                        pc, pcn = bank()
                        zf2 = zf[:].rearrange("p t n -> p (t n)")
                        mm(pc[:, 0:128], tri[:], zf2, True, True, ["tri", "zf"], [pcn], inc=False)
                        mm(pc[:, 128:256], ones32[:], zf2, True, True, ["ones32", "zf"], [pcn], inc=True)
                        P.op("dve", lambda e: e.tensor_copy(out=cwt[:].rearrange("p a t n -> p (a t n)"), in_=pc[:, 0:256]), reads=[pcn], writes=["cwt"])
                        P.op("dve", lambda e: e.memset(offn[:, 0, :], 0.0), writes=["offn"])
                        for ti in range(NT):
                            P.op("dve", lambda e, ti=ti: e.tensor_tensor(out=offn[:, ti + 1, :], in0=offn[:, ti, :], in1=cwt[:, 1, ti, :], op=ALU.add),
                                 reads=["offn", "cwt"], writes=["offn"])
                        P.op("dve", lambda e: e.tensor_tensor(out=ncum[:], in0=offn[:, 0:NT, :], in1=cwt[:, 0, :, :], op=ALU.add),
                             reads=["offn", "cwt"], writes=["ncum"])
                        for c in range(4):
                            P.op("dve", lambda e, c=c: e.tensor_tensor(out=bfox[:, c, :, :], in0=ncum[:],
                                                                       in1=offn[:, 4 * c + 2:4 * c + 3, :].to_broadcast([128, NT, 8]),
                                                                       op=ALU.subtract), reads=["ncum", "offn"], writes=["bfox"])

                        for j in range(4):
                            w, wn = load_group_w(gi, [12 + j, 16 + j, 20 + j])
                            e_, en = load_E(gi, [52])
                            gi += 1
                            proj_T(lambda tg, pb, pbn: evac(QT[:, 0, tg * 512:(tg + 1) * 512], pb[:], [pbn],
                                                            ["QT0:%d" % tg, "QTaug0:%d" % tg]), w, wn, 0)
                            proj_T(lambda tg, pb, pbn: evac(KT[:, 0, tg * 512:(tg + 1) * 512], pb[:], [pbn],
                                                            ["KT0:%d" % tg, "KTaug0"]), w, wn, 1)
                            proj_V(w, wn, contig, True)
                            ufs = []
                            for s in range(2):
                                hd = 2 * j + s

                                def units_D(c, s=s, hd=hd, e_=e_, en=en):
                                    if cur_o.get("c%d" % s) != (s, c, "D", j):
                                        cur_o["b%d" % s] = bank(OB)
                                        cur_o["c%d" % s] = (s, c, "D", j)
                                    ob, obn = cur_o["b%d" % s]
                                    us = []
                                    rb = s * 64
                                    for kt in range(4 * c + 4):
                                        jj = kt - 4 * c
                                        qlo = max(jj, 0) * 128
                                        masks = [(jj, e_[:, 0, :], en)] if jj >= 0 else []
                                        us.append(dict(
                                            qlo=qlo, diag=(jj >= 0), first=(kt == 0),
                                            kT=KT[rb:rb + 64, 0, kt * 128:(kt + 1) * 128],
                                            qT=QT[rb:rb + 64, 0, c * 512 + qlo:(c + 1) * 512],
                                            sreads=["KT0:%d" % (kt // 4), "QT0:%d" % c],
                                            bias=bfox[:, c, kt, hd:hd + 1], masks=masks,
                                            pv=[(Vaug[:, kt, s, :], ["Vaug"], ob, obn)]))
                                    return us

                                ufs.append(units_D)
                            dense_attn(merge_units(ufs), merged_finish(4 + j), LOOK=2, SBK=(0, 1, 2, 7), PAIR=True, MENG="dve")
                    P.barrier()
                if stage < 99 and stage == 2 * (2 * b + l):
                    P.dma("sp", dbg["mixT"], mixT[:], reads=["mixT:%d" % i for i in range(8)])
                with ExitStack() as ws:
                    wos = sb(ws, "wos", [128, 8, D], BF16)
                    LNG = sb(ws, "LNG", [128, D], F32)
                    LNB = sb(ws, "LNB", [128, D], F32)
                    GB = sb(ws, "GB", [128, D], F32)
                    zts = [sb(ws, "zt%d" % i, [128, D], F32) for i in range(2)]
                    load_ln(l, 0, b, LNG, LNB, GB)
                    for k in range(8):
                        P.dma("sp", wos[:, k, :], wo16[l, :, k, :], reads=["wo16:%d" % l], writes=["wos:%d" % k])
                        P.op("dve" if k % 2 == 0 else "pool", lambda e, k=k: e.tensor_tensor(out=wos[:, k, :], in0=wos[:, k, :], in1=GB[:], op=ALU.mult),
                             reads=["wos:%d" % k, "GB"], writes=["wos:%d" % k])
                    pend = None
                    for ti in range(NT):
                        psrc = []
                        for hf in range(2):
                            pb, pbn = bank()
                            for k in range(8):
                                mm(pb[:], mixT[:, k, ti * 128:(ti + 1) * 128], wos[:, k, hf * 512:(hf + 1) * 512], k == 0, k == 7,
                                   ["mixT:%d" % k, "wos:%d" % k], [pbn])
                            psrc.append((pb, pbn))
                        fin = ln_residual(ti, zts, LNG, LNB, psrc)
                        if pend is not None:
                            pend()
                        pend = fin
                    pend()
                    P.barrier()

        def ffn(l, b):
            with ExitStack() as fs:
                aT = sb(fs, "aT", [128, NF, 512], BF16)
                wout = sb(fs, "wout", [128, NF, D], BF16)
                wgu = [sb(fs, "wgu%d" % i, [128, 8, 256], BF16) for i in range(3)]
                sg = [sb(fs, "sg%d" % i, [128, 512], F32) for i in range(2)]
                LNG = sb(fs, "LNG", [128, D], F32)
                LNB = sb(fs, "LNB", [128, D], F32)
                GB = sb(fs, "GB", [128, D], F32)
                zts = [sb(fs, "zt%d" % i, [128, D], F32) for i in range(2)]
                load_ln(l, 1, b, LNG, LNB, GB)
                for f in range(NF):
                    P.dma("sp", wout[:, f, :], wfo16[l, f], reads=["wfo16:%d" % l], writes=["wout:%d" % f])
                    P.op("pool", lambda e, f=f: e.tensor_tensor(out=wout[:, f, :], in0=wout[:, f, :], in1=GB[:], op=ALU.mult),
                         reads=["wout:%d" % f, "GB"], writes=["wout:%d" % f])
                it = 0
                for tc in range(4):
                    for f in range(NF):
                        wi = it % 3
                        it += 1
                        P.dma("sp", wgu[wi][:], wfi16[l, f], reads=["wfi16:%d" % l], writes=["wgu%d" % wi])
                        pg, pgn = bank()
                        pu, pun = bank()
                        for k in range(8):
                            mm(pg[:], wgu[wi][:, k, 0:128], hT[:, k, tc * 512:(tc + 1) * 512], k == 0, k == 7, hT_reads([tc]) + ["wgu%d" % wi], [pgn])
                        for k in range(8):
                            mm(pu[:], wgu[wi][:, k, 128:256], hT[:, k, tc * 512:(tc + 1) * 512], k == 0, k == 7, hT_reads([tc]) + ["wgu%d" % wi], [pun])
                        s_ = sg[f % 2]
                        sn = "sg%d" % (f % 2)
                        P.op("act", lambda e, s_=s_, pg=pg: e.activation(out=s_[:], in_=pg[:], func=AF.Silu), reads=[pgn], writes=[sn])
                        P.op("dve", lambda e, s_=s_, pu=pu, f=f: e.tensor_tensor(out=aT[:, f, :], in0=pu[:], in1=s_[:], op=ALU.mult),
                             reads=[pun, sn], writes=["aT:%d" % f])
                    pend = None
                    for jt in range(4):
                        ti = tc * 4 + jt
                        psrc = []
                        for hf in range(2):
                            pb, pbn = bank()
                            for f in range(NF):
                                mm(pb[:], aT[:, f, jt * 128:(jt + 1) * 128], wout[:, f, hf * 512:(hf + 1) * 512], f == 0, f == NF - 1,
                                   ["aT:%d" % f, "wout:%d" % f], [pbn])
                            psrc.append((pb, pbn))
                        fin = ln_residual(ti, zts, LNG, LNB, psrc)
                        if pend is not None:
                            pend()
                        pend = fin
                    pend()
                    pend = None
                P.barrier()

        done = False
        for b in range(NB):
            for tg in range(4):
                P.dma("sp", X[:, tg * 4:(tg + 1) * 4, :], x_d[b, tg * 512:(tg + 1) * 512, :].rearrange("(t p) d -> p t d", p=128),
                      writes=["X:%d" % (tg * 4 + i) for i in range(4)])
            if stage == -1:
                done = True
            for l in range(2):
                if done:
                    break
                load_mod(l, b)
                transposes(0)
                mixer(l, b)
                if stage == 2 * (2 * b + l):
                    done = True
                    break
                transposes(1)
                ffn(l, b)
                if stage == 2 * (2 * b + l) + 1:
                    done = True
                    break
            if done:
                P.dma("sp", dbg["X"].rearrange("(t p) d -> p t d", p=128), X[:], reads=["X:%d" % i for i in range(NT)])
                break
            for tg in range(4):
                P.dma("sp", out_d[b, tg * 512:(tg + 1) * 512, :].rearrange("(t p) d -> p t d", p=128), X[:, tg * 4:(tg + 1) * 4, :],
                      reads=["X:%d" % (tg * 4 + i) for i in range(4)])
        P.barrier()
        build_program.stats = dict(n_ins=dict(P.n_ins), count=dict(P.count), dma=list(P.dma_cnt))
    return nc


def host_inputs(inputs):
    global ECOLS
    rel_bias = np.asarray(inputs["rel_bias"], np.float32)
    tiles, cols = make_bias_tiles(rel_bias)
    ECOLS = cols
    k = np.arange(128)
    tri = (k[:, None] <= k[None, :]).astype(np.float32)
    kaug = np.zeros((8, T), np.float32)
    for n in range(8):
        kaug[n, n * 256:(n + 1) * 256] = 1.0
    shared = {
        "rel_bias": rel_bias,
        "w_ada": np.ascontiguousarray(inputs["w_ada"], np.float32),
        "b_ada": np.ascontiguousarray(inputs["b_ada"], np.float32),
        "ln_g": np.ascontiguousarray(inputs["ln_g"], np.float32),
        "ln_b": np.ascontiguousarray(inputs["ln_b"], np.float32),
        "w_in_ab": np.ascontiguousarray(np.asarray(inputs["w_in_ab"], np.float32)[0]),
        "diff_lambda": np.ascontiguousarray(np.asarray(inputs["diff_lambda"], np.float32).reshape(256)),
        "diff_subln_g": np.ascontiguousarray(np.asarray(inputs["diff_subln_g"], np.float32).reshape(128)),
        "w_in_cd": np.ascontiguousarray(np.asarray(inputs["w_in_cd"], np.float32)[0]),
        "forget_b": np.ascontiguousarray(np.asarray(inputs["forget_b"], np.float32).reshape(8)),
        "w_o": np.ascontiguousarray(inputs["w_o"], np.float32),
        "w_ffn_in": np.ascontiguousarray(inputs["w_ffn_in"], np.float32),
        "w_ffn_out": np.ascontiguousarray(inputs["w_ffn_out"], np.float32),
        "btiles": tiles,
        "ident": np.eye(128, dtype=np.float32),
        "tri": tri,
        "kaug": kaug,
    }
    return shared


def kernel(**inputs):
    shared = host_inputs(inputs)
    x = np.asarray(inputs["x"], np.float32)
    c = np.asarray(inputs["c"], np.float32)
    n = 8
    nc = build_program()
    in_maps = []
    for i in range(n):
        m = dict(shared)
        m["x"] = np.ascontiguousarray(x[NB * i:NB * (i + 1)])
        m["c"] = np.ascontiguousarray(c[NB * i:NB * (i + 1)])
        in_maps.append(m)
    res = run_bass_kernel_spmd(nc, in_maps, core_ids=list(range(n)))
    return np.concatenate([np.asarray(r["out"], np.float32) for r in res.results], axis=0)
```

```python
import math
import numpy as np
from contextlib import ExitStack
import concourse.bass as bass
import concourse.mybir as mybir
from concourse.bass_utils import run_bass_kernel_spmd

F32 = mybir.dt.float32
BF16 = mybir.dt.bfloat16
AF = mybir.ActivationFunctionType
ALU = mybir.AluOpType
AX = mybir.AxisListType

T = 2048
D = 1024
NT = 16
DFF = 2816
NF = 22
NB = 2
ALPHA = 4 ** 0.25
EPS = 1e-5
NEG = -30000.0
N_ETILES = 53

ENGS = ("pe", "act", "dve", "pool", "sp")
N_DMA_SEMS = 24


def run_interleaved(gens):
    live = list(gens)
    while live:
        nxt = []
        for g in live:
            try:
                next(g)
                nxt.append(g)
            except StopIteration:
                pass
        live = nxt


class Prog:
    def __init__(self, nc, es):
        self.nc = nc
        self.eng_obj = {"pe": nc.tensor, "act": nc.scalar, "dve": nc.vector,
                        "pool": nc.gpsimd, "sp": nc.sync}
        self.count = {e: 0 for e in ENGS}
        self.known = {e: {} for e in ENGS}
        self.last_w = {}
        self.readers = {}
        self.dma_rr = 0
        self.dma_cnt = [0] * N_DMA_SEMS
        self.sems = {}
        self.n_ins = {e: 0 for e in ENGS}
        for e in ENGS:
            self.sems[e] = es.enter_context(nc.semaphore("s_" + e))
        for j in range(N_DMA_SEMS):
            self.sems[("d", j)] = es.enter_context(nc.semaphore("s_d%d" % j))

    def _deps(self, reads, writes):
        toks = []
        for r in reads:
            t = self.last_w.get(r)
            if t is not None:
                toks.append(t)
        for w in writes:
            t = self.last_w.get(w)
            if t is not None:
                toks.append(t)
            toks.extend(self.readers.get(w, ()))
        return toks

    def _commit(self, tok, reads, writes):
        for r in reads:
            self.readers.setdefault(r, []).append(tok)
        for w in writes:
            self.last_w[w] = tok
            self.readers[w] = []

    def _waits(self, eng, toks):
        need = {}
        kn = self.known[eng]
        for (k, v) in toks:
            if eng == "pe" and k == "pe":
                continue
            if kn.get(k, 0) >= v:
                continue
            if need.get(k, 0) < v:
                need[k] = v
        for k, v in need.items():
            kn[k] = v
        return list(need.items())

    def _emit(self, eng, waits, fn, inc):
        e = self.eng_obj[eng]
        for k, v in waits:
            e.wait_ge(self.sems[k], v)
        if fn is not None:
            ins = fn(e)
            self.n_ins[eng] += 1
            if inc is not None:
                ins.then_inc(self.sems[inc[0]], inc[1])

    def op(self, eng, fn, reads=(), writes=(), inc=True):
        toks = self._deps(reads, writes)
        waits = self._waits(eng, toks)
        if inc:
            self.count[eng] += 1
            tok = (eng, self.count[eng])
            self._emit(eng, waits, fn, (eng, 1))
        else:
            tok = (eng, self.count[eng] + 1)
            self._emit(eng, waits, fn, None)
        self._commit(tok, reads, writes)
        return tok

    def dma(self, q, out, in_, reads=(), writes=(), slow=False):
        toks = self._deps(reads, writes)
        j = self.dma_rr
        self.dma_rr = (self.dma_rr + 1) % N_DMA_SEMS
        if self.dma_cnt[j] > 0:
            toks.append((("d", j), 16 * self.dma_cnt[j]))
        waits = self._waits(q, toks)
        self.dma_cnt[j] += 1
        tok = (("d", j), 16 * self.dma_cnt[j])
        if slow:
            fn = lambda e: e.dma_start(out=out, in_=in_, allow_slow_non_contiguous=True)
        else:
            fn = lambda e: e.dma_start(out=out, in_=in_)
        self._emit(q, waits, fn, (("d", j), 16))
        self._commit(tok, reads, writes)
        return tok

    def all_tokens(self):
        toks = [(e, self.count[e]) for e in ENGS if self.count[e] > 0]
        toks += [(("d", j), 16 * c) for j, c in enumerate(self.dma_cnt) if c > 0]
        return toks

    def barrier(self, engs=ENGS):
        toks = self.all_tokens()
        for e in engs:
            self._emit(e, self._waits(e, list(toks)), None, None)
        self.last_w = {}
        self.readers = {}


def t5_bucket_np(n):
    n = np.maximum(n, 0)
    nf = np.maximum(n, 1).astype(np.float32)
    large = 16 + (np.log(nf / np.float32(16)) / np.float32(math.log(128 / 16)) * np.float32(16)).astype(np.int32)
    large = np.minimum(large, 31)
    return np.where(n < 16, n, large)


def make_bias_tiles(rel_bias):
    k = np.arange(128)[:, None]
    q = np.arange(128)[None, :]
    tiles = np.full((N_ETILES, 128, 128), NEG, np.float32)
    cols = np.zeros((N_ETILES,), np.int64)

    def fill(idx, dist, valid, col):
        b = t5_bucket_np(np.where(valid, dist, 0))
        tiles[idx] = np.where(valid, rel_bias[b, col], np.float32(NEG))
        cols[idx] = col

    for col in range(12):
        fill(col, q - k, q >= k, col)
    for h in range(8):
        fill(12 + h, q - k + 128, np.ones((128, 128), bool), h)
    for hb in range(8):
        col = 4 + hb
        fill(20 + hb, q - k + 128, q <= k, col)
        fill(28 + hb, 4 * (q - k), q >= k, col)
        fill(36 + hb, 4 * (q - k + 128), q <= k, col)
        fill(44 + hb, 16 * (q - k), q >= k, col)
    tiles[52] = np.where(q >= k, np.float32(0.0), np.float32(NEG))
    cols[52] = 12
    return tiles, cols


ECOLS = None


def build_program(stage=99):
    nc = bass.Bass("TRN2", target_bir_lowering=False)
    dram = lambda name, shape, dt, kind: nc.dram_tensor(name, shape, dt, kind=kind).ap()
    x_d = dram("x", [NB, T, D], F32, "ExternalInput")
    c_d = dram("c", [NB, D], F32, "ExternalInput")
    relb_d = dram("rel_bias", [32, 12], F32, "ExternalInput")
    wada_d = dram("w_ada", [2, D, 6 * D], F32, "ExternalInput")
    bada_d = dram("b_ada", [2, 6 * D], F32, "ExternalInput")
    lng_d = dram("ln_g", [2, 2, D], F32, "ExternalInput")
    lnb_d = dram("ln_b", [2, 2, D], F32, "ExternalInput")
    wab_d = dram("w_in_ab", [D, 3072], F32, "ExternalInput")
    lam_d = dram("diff_lambda", [256], F32, "ExternalInput")
    subg_d = dram("diff_subln_g", [128], F32, "ExternalInput")
    wcd_d = dram("w_in_cd", [D, 3080], F32, "ExternalInput")
    fb_d = dram("forget_b", [8], F32, "ExternalInput")
    wo_d = dram("w_o", [2, D, D], F32, "ExternalInput")
    wfi_d = dram("w_ffn_in", [2, D, 2 * DFF], F32, "ExternalInput")
    wfo_d = dram("w_ffn_out", [2, DFF, D], F32, "ExternalInput")
    bt_d = dram("btiles", [N_ETILES, 128, 128], F32, "ExternalInput")
    ident_d = dram("ident", [128, 128], F32, "ExternalInput")
    tri_d = dram("tri", [128, 128], F32, "ExternalInput")
    kaug_d = dram("kaug", [8, T], F32, "ExternalInput")
    out_d = dram("out", [NB, T, D], F32, "ExternalOutput")
    wg16 = dram("wg16", [2, 24, 128, 8, 128], BF16, "Internal")
    kaug16 = dram("kaug16", [8, T], BF16, "Internal")
    wo16 = dram("wo16", [2, 128, 8, D], BF16, "Internal")
    wfi16 = dram("wfi16", [2, NF, 128, 8, 256], BF16, "Internal")
    wfo16 = dram("wfo16", [2, NF, 128, D], BF16, "Internal")
    ada_s = dram("ada_s", [2, NB, 6 * D], F32, "Internal")
    E_d = dram("E_d", [N_ETILES, 128, 128], BF16, "Internal")
    dbg = {}
    if stage < 99:
        dbg["mixT"] = dram("dbg_mixT", [128, 8, T], BF16, "ExternalOutput")
        dbg["X"] = dram("dbg_X", [T, D], F32, "ExternalOutput")

    with ExitStack() as es:
        P = Prog(nc, es)
        ctr = {"ev": 0, "bank": 0, "pt": 0}

        def sb(st, name, shape, dt):
            ctr["uid"] = ctr.get("uid", 0) + 1
            return st.enter_context(nc.sbuf_tensor("%s_u%d" % (name, ctr["uid"]), shape, dt))

        X = sb(es, "X", [128, NT, D], F32)
        hT = sb(es, "hT", [128, 8, T], BF16)
        ident = sb(es, "ident", [128, 128], F32)
        ident16 = sb(es, "ident16", [128, 128], BF16)
        tri = sb(es, "tri", [128, 128], F32)
        ones32 = sb(es, "ones32", [128, 128], F32)
        ones16 = sb(es, "ones16", [128, 128], BF16)
        gmask = sb(es, "gmask", [128, NT, 8], F32)
        modT = sb(es, "modT", [128, 4, 8], F32)
        neglam = sb(es, "neglam", [128, 1], F32)
        subg = sb(es, "subg", [128, 1], F32)
        fbb = sb(es, "fbb", [128, 8], F32)
        epsc = sb(es, "epsc", [128, 1], F32)
        stt = sb(es, "stt", [128, 2, 2, 6], F32)
        mv = sb(es, "mv", [128, 2, 8], F32)
        banks = [es.enter_context(nc.psum_tensor("ps%d" % i, [128, 512], F32)) for i in range(8)]

        def bank(group=None):
            lst = group if group is not None else list(range(8))
            key = "bank" + str(lst)
            i = ctr.get(key, 0)
            ctr[key] = i + 1
            b = lst[i % len(lst)]
            return banks[b], "ps%d" % b

        def evac(out, in_, reads, writes, eng=None):
            if eng is None:
                eng = "act" if ctr["ev"] % 2 == 0 else "dve"
                ctr["ev"] += 1
            if eng == "act":
                P.op("act", lambda e: e.activation(out=out, in_=in_, func=AF.Copy), reads, writes)
            else:
                P.op(eng, lambda e: e.tensor_copy(out=out, in_=in_), reads, writes)

        def mm(out, lhsT, rhs, start, stop, reads, writes, inc=None):
            if inc is None:
                inc = stop
            P.op("pe", lambda e: e.matmul(out, lhsT=lhsT, rhs=rhs, start=start, stop=stop), reads, writes, inc=inc)

        P.dma("sp", ident[:], ident_d, writes=["ident"])
        P.dma("sp", tri[:], tri_d, writes=["tri"])
        P.op("dve", lambda e: e.memset(ones32[:], 1.0), writes=["ones32"])
        P.op("dve", lambda e: e.memset(ones16[:], 1.0), writes=["ones16"])
        P.op("dve", lambda e: e.memset(epsc[:], EPS), writes=["epsc"])
        P.op("dve", lambda e: e.memset(gmask[:], -1e30), writes=["gmask"])
        for ti in range(NT):
            own = ti // 2
            if own > 0:
                P.op("dve", lambda e, ti=ti, own=own: e.memset(gmask[:, ti, 0:own], 0.0), reads=["gmask"], writes=["gmask"])
            P.op("dve", lambda e, ti=ti, own=own: e.memset(gmask[:, ti, own:own + 1], 1e30), reads=["gmask"], writes=["gmask"])
        P.dma("sp", subg[:], subg_d.rearrange("(p o) -> p o", o=1), writes=["subg"])
        P.op("dve", lambda e: e.tensor_scalar(out=subg[:], in0=subg[:], scalar1=0.8, scalar2=None, op0=ALU.mult),
             reads=["subg"], writes=["subg"])
        P.dma("sp", fbb[:], fb_d.partition_broadcast(128), writes=["fbb"])

        modall = sb(es, "modall", [128, 2, 48, NB], F32)
        wfs = sb(es, "wfs", [128, 8, 8], BF16)
        with ExitStack() as ps_:
            ps1 = ExitStack()
            negfar = sb(ps1, "negfar", [128, 16], F32)
            lam = sb(ps1, "lam", [128, 256], F32)
            lamp = sb(ps1, "lamp", [128, 128], F32)
            ls = sb(ps1, "ls", [128, 4], F32)
            btl = [sb(ps1, "btl%d" % i, [128, 4, 128], F32) for i in range(2)]
            etl = [sb(ps1, "etl%d" % i, [128, 4, 128], BF16) for i in range(2)]
            kg32 = sb(ps1, "kg32", [8, T], F32)
            kg16 = sb(ps1, "kg16", [8, T], BF16)

            P.op("dve", lambda e: e.tensor_copy(out=ident16[:], in_=ident[:]), reads=["ident"], writes=["ident16"])
            P.dma("sp", kg32[:], kaug_d, writes=["kg32"])
            P.op("dve", lambda e: e.tensor_copy(out=kg16[:], in_=kg32[:]), reads=["kg32"], writes=["kg16"])
            P.dma("sp", kaug16, kg16[:], reads=["kg16"], writes=["kaug16"])

            P.dma("sp", lam[:], lam_d.partition_broadcast(128), writes=["lam"])
            P.op("dve", lambda e: e.tensor_tensor(out=lamp[:, 0:64], in0=lam[:, 0:64], in1=lam[:, 64:128], op=ALU.mult),
                 reads=["lam"], writes=["lamp"])
            P.op("dve", lambda e: e.tensor_tensor(out=lamp[:, 64:128], in0=lam[:, 128:192], in1=lam[:, 192:256], op=ALU.mult),
                 reads=["lam", "lamp"], writes=["lamp"])
            P.op("dve", lambda e: e.reduce_sum(out=ls[:, 0:1], in_=lamp[:, 0:64], axis=AX.X), reads=["lamp"], writes=["ls"])
            P.op("dve", lambda e: e.reduce_sum(out=ls[:, 1:2], in_=lamp[:, 64:128], axis=AX.X), reads=["lamp", "ls"], writes=["ls"])
            P.op("act", lambda e: e.activation(out=ls[:, 2:4], in_=ls[:, 0:2], func=AF.Exp), reads=["ls"], writes=["ls"])
            P.op("dve", lambda e: e.tensor_tensor(out=neglam[:], in0=ls[:, 3:4], in1=ls[:, 2:3], op=ALU.subtract),
                 reads=["ls"], writes=["neglam"])
            P.op("dve", lambda e: e.tensor_scalar(out=neglam[:], in0=neglam[:], scalar1=-0.2, scalar2=None, op0=ALU.add),
                 reads=["neglam"], writes=["neglam"])

            P.op("dve", lambda e: e.memset(negfar[:], 0.0), writes=["negfar"])
            P.dma("sp", negfar[:, 0:12], relb_d[31, :].partition_broadcast(128), reads=["negfar"], writes=["negfar"])
            P.op("dve", lambda e: e.tensor_scalar(out=negfar[:], in0=negfar[:], scalar1=-1.0, scalar2=None, op0=ALU.mult),
                 reads=["negfar"], writes=["negfar"])
            for gi, t0 in enumerate(range(0, N_ETILES, 4)):
                n = min(4, N_ETILES - t0)
                bb, ee = btl[gi % 2], etl[gi % 2]
                bn, en = "btl%d" % (gi % 2), "etl%d" % (gi % 2)
                P.dma("sp", bb[:, 0:n, :], bt_d[t0:t0 + n].rearrange("t k q -> k t q"), writes=[bn])
                for i in range(n):
                    col = int(ECOLS[t0 + i])
                    P.op("act", lambda e, ee=ee, bb=bb, i=i, col=col: e.activation(
                        out=ee[:, i, :], in_=bb[:, i, :], func=AF.Exp, bias=negfar[:, col:col + 1], scale=1.0),
                        reads=[bn, "negfar"], writes=[en])
                P.dma("sp", E_d[t0:t0 + n].rearrange("t k q -> k t q"), ee[:, 0:n, :], reads=[en], writes=["E_d"])

            P.barrier()
            ps1.close()
            csb = sb(ps_, "csb", [NB, D], F32)
            cT32 = sb(ps_, "cT32", [128, 8, NB], F32)
            badas = [sb(ps_, "bada%d" % i, [NB, 512], F32) for i in range(2)]
            adasb = sb(ps_, "adasb", [NB, 6 * D], F32)
            NSTG = 3
            stg32 = [sb(ps_, "stg32_%d" % i, [128, 8 * 520], F32) for i in range(NSTG)]
            stg16 = [sb(ps_, "stg16_%d" % i, [128, 8 * 512], BF16) for i in range(NSTG)]
            P.dma("sp", csb[:], c_d, writes=["csb"])
            P.op("act", lambda e: e.activation(out=csb[:], in_=csb[:], func=AF.Silu), reads=["csb"], writes=["csb"])
            pb, pbn = bank()
            for k in range(8):
                P.op("pe", lambda e, k=k: e.transpose(pb[:, k * NB:(k + 1) * NB], csb[0:NB, k * 128:(k + 1) * 128], ident[0:NB, 0:NB]),
                     reads=["csb", "ident"], writes=[pbn], inc=(k == 7))
            P.op("dve", lambda e: e.tensor_copy(out=cT32[:].rearrange("p k b -> p (k b)"), in_=pb[:, 0:8 * NB]), reads=[pbn], writes=["cT32"])
            it = 0
            for l in range(2):
                for n in range(12):
                    bada, bdn = badas[it % 2], "bada%d" % (it % 2)
                    P.dma("sp", bada[:], bada_d[l, n * 512:(n + 1) * 512].partition_broadcast(NB), writes=[bdn])
                    si = it % NSTG
                    it += 1
                    wv = stg32[si][:, 0:8 * 512].rearrange("p (c n) -> p c n", c=8)
                    P.dma("sp", wv, wada_d[l, :, n * 512:(n + 1) * 512].rearrange("(c p) n -> p c n", p=128), writes=["stg32_%d" % si])
                    pb, pbn = bank()
                    for k in range(8):
                        mm(pb[0:NB, :], cT32[:, k, :], wv[:, k, :], k == 0, k == 7, ["cT32", "stg32_%d" % si], [pbn])
                    P.op("dve", lambda e, pb=pb, n=n: e.tensor_tensor(
                        out=adasb[:, n * 512:(n + 1) * 512], in0=pb[0:NB, :], in1=bada[:], op=ALU.add),
                        reads=[pbn, bdn], writes=["adasb"])
                P.dma("sp", ada_s[l], adasb[:], reads=["adasb"], writes=["ada_s"])
                pb, pbn = bank()
                for ch in range(48):
                    P.op("pe", lambda e, ch=ch: e.transpose(pb[:, ch * NB:(ch + 1) * NB], adasb[0:NB, ch * 128:(ch + 1) * 128], ident[0:NB, 0:NB]),
                         reads=["adasb", "ident"], writes=[pbn], inc=(ch == 47))
                P.op("dve", lambda e, l=l, pb=pb: e.tensor_copy(out=modall[:, l].rearrange("p c b -> p (c b)"), in_=pb[:, 0:48 * NB]),
                     reads=[pbn], writes=["modall"])
            for c0 in (8, 32):
                P.op("dve", lambda e, c0=c0: e.tensor_scalar(out=modall[:, :, c0:c0 + 8, :], in0=modall[:, :, c0:c0 + 8, :], scalar1=1.0,
                                                               scalar2=None, op0=ALU.add), reads=["modall"], writes=["modall"])

            cast_engs = ["dve", "act", "pool"]
            pieces = []

            def do_cast(eng, ov, iv, n32, n16):
                if eng == "act":
                    P.op("act", lambda e: e.activation(out=ov, in_=iv, func=AF.Copy), reads=[n32], writes=[n16])
                else:
                    P.op(eng, lambda e: e.tensor_copy(out=ov, in_=iv), reads=[n32], writes=[n16])

            def add_piece(loads, casts, stores):
                k = len(pieces)
                eng = cast_engs[k % 3]

                def ld(si):
                    for vf, src in loads:
                        P.dma("sp", vf(stg32[si]), src, reads=["stg32_%d" % si], writes=["stg32_%d" % si])

                def cs(si):
                    for of, inf in casts:
                        do_cast(eng, of(stg16[si]), inf(stg32[si]), "stg32_%d" % si, "stg16_%d" % si)

                def st(si):
                    for dst, vf, names in stores:
                        P.dma("sp", dst, vf(stg16[si]), reads=["stg16_%d" % si], writes=names)
                pieces.append((ld, cs, st))

            for l, src in ((0, wab_d), (1, wcd_d)):
                for g in range(6):
                    if l == 1 and g < 5:
                        continue
                    ncol = 520 if (l == 1 and g == 5) else 512
                    casts = [(lambda t: t[:, 0:4096].rearrange("p (s c n) -> p c s n", s=4, c=8),
                              lambda t, ncol=ncol: t[:, 0:8 * ncol].rearrange("p (c n) -> p c n", c=8)[:, :, 0:512].rearrange(
                                  "p c (s n) -> p c s n", s=4))]
                    if ncol == 520:
                        casts.append((lambda t: wfs[:], lambda t: t[:, 0:8 * 520].rearrange("p (c n) -> p c n", c=8)[:, :, 512:520]))
                    add_piece([(lambda t, ncol=ncol: t[:, 0:8 * ncol].rearrange("p (c n) -> p c n", c=8),
                                src[:, g * 512:g * 512 + ncol].rearrange("(c p) n -> p c n", p=128))],
                              casts,
                              [(wg16[l, g * 4:(g + 1) * 4].rearrange("s p c n -> p s (c n)"),
                                lambda t: t[:, 0:4096].rearrange("p (s x) -> p s x", s=4),
                                ["wg16:%d:%d" % (l, g * 4 + i) for i in range(4)])])
            npc = len(pieces)
            for i in range(npc + 1):
                if i < npc:
                    pieces[i][0](i % NSTG)
                if i >= 1:
                    pieces[i - 1][1]((i - 1) % NSTG)
                    pieces[i - 1][2]((i - 1) % NSTG)
            P.barrier()

        class BgCast:
            def __init__(self, mixT, pieces, every):
                base = mixT[:, 4:8, :].rearrange("p a n -> p (a n)")
                self.s32 = [base[:, i * 2048:(i + 1) * 2048].bitcast(F32) for i in range(3)]
                self.s16 = [base[:, 6144 + i * 1024:6144 + (i + 1) * 1024] for i in range(2)]
                self.pieces = pieces
                self.t = 0
                self.n = 0
                self.every = every

            def view(self, ap, like):
                if len(like.shape) == 3:
                    return ap.rearrange("p (c n) -> p c n", c=like.shape[1])
                return ap

            def step(self):
                t, pcs = self.t, self.pieces
                if t < len(pcs):
                    src, dst, names = pcs[t]
                    P.dma("sp", self.view(self.s32[t % 3], src), src, reads=["bg32_%d" % (t % 3)], writes=["bg32_%d" % (t % 3)])
                u = t - 2
                if 0 <= u < len(pcs):
                    src, dst, names = pcs[u]
                    i32, i16 = u % 3, u % 2
                    eng = "dve"
                    P.op(eng, lambda e: e.tensor_copy(out=self.s16[i16], in_=self.s32[i32]), reads=["bg32_%d" % i32], writes=["bg16_%d" % i16])
                    P.dma("sp", dst, self.view(self.s16[i16], dst), reads=["bg16_%d" % i16], writes=names)
                self.t += 1

            def tick(self):
                self.n += 1
                if self.n % self.every == 0 and self.t < len(self.pieces) + 2:
                    self.step()

            def flush(self):
                while self.t < len(self.pieces) + 2:
                    self.step()

        def col_piece(src2d, dst3d, names):
            return (src2d.rearrange("(c p) n -> p c n", p=128), dst3d, names)

        bg_l0 = []
        for n0 in range(8):
            bg_l0.append(col_piece(wo_d[0, :, n0 * 128:(n0 + 1) * 128], wo16[0, :, :, n0 * 128:(n0 + 1) * 128], ["wo16:0"]))
        for f in range(NF):
            bg_l0.append(col_piece(wfi_d[0, :, f * 128:(f + 1) * 128], wfi16[0, f, :, :, 0:128], ["wfi16:0"]))
            bg_l0.append(col_piece(wfi_d[0, :, DFF + f * 128:DFF + (f + 1) * 128], wfi16[0, f, :, :, 128:256], ["wfi16:0"]))
            bg_l0.append((wfo_d[0, f * 128:(f + 1) * 128, :], wfo16[0, f], ["wfo16:0"]))
        for sl in range(20):
            bg_l0.append(col_piece(wcd_d[:, sl * 128:(sl + 1) * 128], wg16[1, sl], ["wg16:1:%d" % sl]))
        for n0 in range(8):
            bg_l0.append(col_piece(wo_d[1, :, n0 * 128:(n0 + 1) * 128], wo16[1, :, :, n0 * 128:(n0 + 1) * 128], ["wo16:1"]))
        bg_l1 = []
        for f in range(NF):
            bg_l1.append(col_piece(wfi_d[1, :, f * 128:(f + 1) * 128], wfi16[1, f, :, :, 0:128], ["wfi16:1"]))
            bg_l1.append(col_piece(wfi_d[1, :, DFF + f * 128:DFF + (f + 1) * 128], wfi16[1, f, :, :, 128:256], ["wfi16:1"]))
            bg_l1.append((wfo_d[1, f * 128:(f + 1) * 128, :], wfo16[1, f], ["wfo16:1"]))
        bgs = {"st": None}

        def load_mod(l, b):
            for j, c0 in enumerate((0, 8, 24, 32)):
                P.op("dve", lambda e, j=j, c0=c0: e.tensor_copy(out=modT[:, j, :], in_=modall[:, l, c0:c0 + 8, b]),
                     reads=["modall"], writes=["modT%d" % j])

        def transposes(sub):
            jsh, jsc = (0, 1) if sub == 0 else (2, 3)
            for tg in range(4):
                for c in range(8):
                    pb, pbn = bank()
                    for j in range(4):
                        ti = tg * 4 + j
                        P.op("pe", lambda e, pb=pb, j=j, ti=ti, c=c: e.transpose(
                            pb[:, j * 128:(j + 1) * 128], X[:, ti, c * 128:(c + 1) * 128], ident[:]),
                            reads=["X:%d" % ti, "ident"], writes=[pbn], inc=(j == 3))
                    rd = [pbn, "modT%d" % jsh, "modT%d" % jsc]
                    wr = ["hT:%d:%d" % (c, tg)]
                    if ctr["ev"] % 2 == 0:
                        P.op("act", lambda e, pb=pb, c=c, tg=tg: e.activation(
                            out=hT[:, c, tg * 512:(tg + 1) * 512], in_=pb[:], func=AF.Identity,
                            bias=modT[:, jsh, c:c + 1], scale=modT[:, jsc, c:c + 1]), rd, wr)
                    else:
                        P.op("dve", lambda e, pb=pb, c=c, tg=tg: e.tensor_scalar(
                            out=hT[:, c, tg * 512:(tg + 1) * 512], in0=pb[:], scalar1=modT[:, jsc, c:c + 1],
                            scalar2=modT[:, jsh, c:c + 1], op0=ALU.mult, op1=ALU.add), rd, wr)
                    ctr["ev"] += 1

        def hT_reads(tgs):
            return ["hT:%d:%d" % (c, tg) for c in range(8) for tg in tgs]

        def ln_residual(ti, zts, LNG, LNB, psrc):
            p = ti % 2
            zt = zts[p]
            zn = ["zt%d_%d" % (p, hf) for hf in range(2)]
            for hf in range(2):
                pb, pbn = psrc[hf]
                sl = slice(hf * 512, (hf + 1) * 512)
                P.op("dve", lambda e, pb=pb, sl=sl: e.scalar_tensor_tensor(out=zt[:, sl], in0=X[:, ti, sl], scalar=ALPHA, in1=pb[:],
                                                                            op0=ALU.mult, op1=ALU.add),
                     reads=["X:%d" % ti, pbn], writes=[zn[hf]])
                P.op("dve", lambda e, sl=sl, hf=hf: e.bn_stats(out=stt[:, p, hf, :], in_=zt[:, sl]), reads=[zn[hf]], writes=["stt%d_%d" % (p, hf)])
            P.op("dve", lambda e: e.bn_aggr(out=mv[:, p, 0:2], in_=stt[:, p].rearrange("p a b -> p (a b)")),
                 reads=["stt%d_0" % p, "stt%d_1" % p], writes=["mv%d" % p])
            P.op("act", lambda e: e.activation(out=mv[:, p, 2:3], in_=mv[:, p, 1:2], func=AF.Ln, bias=epsc[:], scale=1.0),
                 reads=["mv%d" % p, "epsc"], writes=["mvb%d" % p])
            P.op("act", lambda e: e.activation(out=mv[:, p, 3:4], in_=mv[:, p, 2:3], func=AF.Exp, scale=-0.5),
                 reads=["mvb%d" % p], writes=["mvc%d" % p])
            P.op("dve", lambda e: e.tensor_scalar(out=mv[:, p, 4:5], in0=mv[:, p, 0:1], scalar1=-1.0, scalar2=mv[:, p, 3:4],
                                                  op0=ALU.mult, op1=ALU.mult), reads=["mv%d" % p, "mvc%d" % p], writes=["mvd%d" % p])
            P.op("act", lambda e: e.activation(out=zt[:], in_=zt[:], func=AF.Identity, bias=mv[:, p, 4:5], scale=mv[:, p, 3:4]),
                 reads=zn + ["mvc%d" % p, "mvd%d" % p], writes=zn)
            P.op("pool", lambda e: e.tensor_tensor(out=zt[:], in0=zt[:], in1=LNG[:], op=ALU.mult),
                 reads=zn + ["LNG"], writes=zn)

            P.op("pool", lambda e: e.tensor_tensor(out=X[:, ti, 0:512], in0=zt[:, 0:512], in1=LNB[:, 0:512], op=ALU.add),
                 reads=zn + ["LNB"], writes=["X:%d" % ti])

            def stage_b():
                P.op("dve", lambda e: e.tensor_tensor(out=X[:, ti, 512:1024], in0=zt[:, 512:1024], in1=LNB[:, 512:1024], op=ALU.add),
                     reads=zn + ["LNB"], writes=["X:%d" % ti])
            return stage_b

        def load_ln(l, sub, b, LNG, LNB, GB):
            off = 2048 if sub == 0 else 5120
            P.dma("sp", GB[:], ada_s[l, b, off:off + 1024].partition_broadcast(128), writes=["GB"])
            P.op("dve", lambda e: e.tensor_scalar(out=GB[:], in0=GB[:], scalar1=1.0, scalar2=None, op0=ALU.add),
                 reads=["GB"], writes=["GB"])
            P.dma("sp", LNG[:], lng_d[l, sub].partition_broadcast(128), writes=["LNG"])
            P.dma("sp", LNB[:], lnb_d[l, sub].partition_broadcast(128), writes=["LNB"])

        def mixer(l, b):
            with ExitStack() as ms:
                mixT = sb(ms, "mixT", [128, 8, T], BF16)
                with ExitStack() as gs:
                    QT = sb(gs, "QT", [128, 2, T], BF16)
                    KT = sb(gs, "KT", [128, 2, T], BF16)
                    Vaug = sb(gs, "Vaug", [128, NT, 2, 128], BF16)
                    wgr = [sb(gs, "wgr%d" % i, [128, 3, 8, 128], BF16) for i in range(2)]
                    PTs = [sb(gs, "PT%d" % i, [128, 512], BF16) for i in range(4)]
                    Eg = [sb(gs, "Eg%d" % i, [128, 10, 128], BF16) for i in range(2)]
                    rec = sb(gs, "rec", [128, 512], F32)
                    rec2 = sb(gs, "rec2", [128, 512], F32)
                    gate = sb(gs, "gate", [128, NT, 8], F32)
                    top8 = sb(gs, "top8", [128, NT, 8], F32)
                    sel = sb(gs, "sel", [128, NT, 8], F32)
                    negpad = sb(gs, "negpad", [128, NT, 72], BF16)
                    gts = [(gate, top8, sel, negpad)]
                    if l == 1:
                        gts.append((sb(gs, "gate2", [128, NT, 8], F32), sb(gs, "top82", [128, NT, 8], F32),
                                    sb(gs, "sel2", [128, NT, 8], F32), sb(gs, "negpad2", [128, NT, 72], BF16)))
                        P.op("pool", lambda e: e.memset(gts[1][3][:], 0.0), writes=["negpad1"])
                    ksum = sb(gs, "ksum", [128, 2, 8], F32)
                    kmb = sb(gs, "kmb", [128, 2, 8], BF16)
                    zf = sb(gs, "zf", [128, NT, 8], F32)
                    cwt = sb(gs, "cwt", [128, 2, NT, 8], F32)
                    offn = sb(gs, "offn", [128, NT + 1, 8], F32)
                    ncum = sb(gs, "ncum", [128, NT, 8], F32)
                    bfox = sb(gs, "bfox", [128, 4, NT, 8], F32)
                    P.op("pool", lambda e: e.memset(Vaug[:, :, :, 64:128], 1.0), writes=["Vaug"])
                    P.op("pool", lambda e: e.memset(negpad[:], 0.0), writes=["negpad0"])

                    def pt_next():
                        i = ctr["pt"] % 4
                        ctr["pt"] += 1
                        return PTs[i], "PT%d" % i

                    SB = [0, 1, 2]
                    OB = [3, 4, 5, 6]

                    def load_group_w(gi, slices):
                        w = wgr[gi % 2]
                        wn = "wgr%d" % (gi % 2)
                        for j, s in enumerate(slices):
                            P.dma("sp", w[:, j], wg16[l, s], reads=["wg16:%d:%d" % (l, s)], writes=[wn + ":%d" % j])
                        return w, wn

                    def proj_T(dst_fn, w, wn, j):
                        for tg in range(4):
                            pb, pbn = bank()
                            for k in range(8):
                                mm(pb[:], w[:, j, k, :], hT[:, k, tg * 512:(tg + 1) * 512], k == 0, k == 7,
                                   hT_reads([tg]) + [wn + ":%d" % j], [pbn])
                            dst_fn(tg, pb, pbn)

                    def proj_V(w, wn, tok_ap_fn, pair):
                        for tg in range(4):
                            pb, pbn = bank()
                            for j in range(4):
                                slot = tg * 4 + j
                                sl, tgs = tok_ap_fn(slot)
                                for k in range(8):
                                    mm(pb[:, j * 128:(j + 1) * 128], hT[:, k, sl], w[:, 2, k, :], k == 0, k == 7,
                                       hT_reads(tgs) + [wn + ":2"], [pbn], inc=(k == 7 and j == 3))
                            if pair:
                                evac(Vaug[:, tg * 4:(tg + 1) * 4, :, 0:64],
                                     pb[:].rearrange("p (t h d) -> p t h d", t=4, h=2), [pbn], ["Vaug"])
                            else:
                                evac(Vaug[:, tg * 4:(tg + 1) * 4, 0, :],
                                     pb[:].rearrange("p (t d) -> p t d", t=4), [pbn], ["Vaug"])

                    contig = lambda slot: (slice(slot * 128, (slot + 1) * 128), [slot // 4])

                    def load_E(gi, idxs):
                        e_ = Eg[gi % 2]
                        en = "Eg%d" % (gi % 2)
                        for j, ix in enumerate(idxs):
                            P.dma("sp", e_[:, j, :], E_d[ix], reads=["E_d"], writes=[en])
                        return e_, en

                    def dense_attn(units_for_chunk, finish_chunk, LOOK=2, SBK=(0, 1, 2), PAIR=False, MENG="pool"):
                        for c in range(4):
                            units = units_for_chunk(c)
                            SBK = list(SBK)

                            def issue_S(u):
                                pb, pbn = bank(SBK)
                                qlo = u["qlo"]
                                mm(pb[:, qlo:512], u["kT"], u["qT"], True, True, u["sreads"], [pbn])
                                u["sb"], u["sbn"] = pb, pbn

                            last_idx = {}
                            for i, u in enumerate(units):
                                for (_l, _r, _ob, obn) in u["pv"]:
                                    last_idx[obn] = i
                            for i in range(min(LOOK, len(units))):
                                issue_S(units[i])
                            for i, u in enumerate(units):
                                if PAIR:
                                    if i % 2 == 0:
                                        for k2 in (i + LOOK, i + LOOK + 1):
                                            if k2 < len(units):
                                                issue_S(units[k2])
                                elif i + LOOK < len(units):
                                    issue_S(units[i + LOOK])
                                if bgs["st"] is not None:
                                    bgs["st"].tick()
                                qlo = u["qlo"]
                                pt, ptn = pt_next()
                                bias = u["bias"]
                                if bias is None:
                                    P.op("act", lambda e, pt=pt, u=u, qlo=qlo: e.activation(
                                        out=pt[:, qlo:512], in_=u["sb"][:, qlo:512], func=AF.Exp, scale=0.125),
                                        reads=[u["sbn"]], writes=[ptn])
                                else:
                                    P.op("act", lambda e, pt=pt, u=u, qlo=qlo, bias=bias: e.activation(
                                        out=pt[:, qlo:512], in_=u["sb"][:, qlo:512], func=AF.Exp, scale=0.125, bias=bias),
                                        reads=[u["sbn"], "bfox"], writes=[ptn])
                                for (ii, et, en) in u["masks"]:
                                    P.op(MENG, lambda e, pt=pt, ii=ii, et=et: e.tensor_tensor(
                                        out=pt[:, ii * 128:(ii + 1) * 128], in0=pt[:, ii * 128:(ii + 1) * 128], in1=et, op=ALU.mult),
                                        reads=[ptn, en], writes=[ptn])
                                for (lhsT, lreads, ob, obn) in u["pv"]:
                                    lastu = (last_idx[obn] == i)
                                    if u["diag"] and qlo > 0:
                                        jj = qlo // 128
                                        for ii in range(jj, 4):
                                            mm(ob[:, ii * 128:(ii + 1) * 128], lhsT, pt[:, ii * 128:(ii + 1) * 128],
                                               u["first"], lastu and ii == 3, [ptn] + lreads, [obn], inc=(ii == 3))
                                    else:
                                        mm(ob[:], lhsT, pt[:], u["first"], lastu, [ptn] + lreads, [obn], inc=True)
                            finish_chunk(c)

                    gi = 0
                    if l == 0:
                        with ExitStack() as at:
                            r0 = sb(at, "r0", [128, 512], F32)
                            r1 = sb(at, "r1", [128, 512], F32)
                            oo = sb(at, "oo", [128, 512], F32)
                            t1 = sb(at, "t1", [128, 512], F32)
                            sq = sb(at, "sq", [128, 512], F32)
                            if b == 0:
                                bgs["st"] = BgCast(mixT, bg_l0, 3)
                            for h in range(4):
                                w, wn = load_group_w(gi, [h, 4 + h, 8 + h])
                                e_, en = load_E(gi, [h, 12 + h])
                                gi += 1
                                proj_T(lambda tg, pb, pbn: evac(QT[:, 0, tg * 512:(tg + 1) * 512], pb[:], [pbn], ["QT:%d" % tg]), w, wn, 0)
                                proj_T(lambda tg, pb, pbn: evac(KT[:, 0, tg * 512:(tg + 1) * 512], pb[:], [pbn], ["KT:%d" % tg]), w, wn, 1)
                                proj_V(w, wn, contig, False)
                                obs = [(banks[3], "ps3"), (banks[4], "ps4"), (banks[5], "ps5"), (banks[6], "ps6")]

                                def units_A(c, e_=e_, en=en):
                                    us = []
                                    for j in range(4 * c + 4):
                                        for m in range(2):
                                            jj = j - 4 * c
                                            qlo = max(jj, 0) * 128
                                            masks = []
                                            for ii in range(4):
                                                i = 4 * c + ii
                                                if j == i:
                                                    masks.append((ii, e_[:, 0, :], en))
                                                elif j == i - 1:
                                                    masks.append((ii, e_[:, 1, :], en))
                                            rb = m * 64
                                            us.append(dict(
                                                qlo=qlo, diag=(jj >= 0), first=(j == 0),
                                                kT=KT[rb:rb + 64, 0, j * 128:(j + 1) * 128],
                                                qT=QT[rb:rb + 64, 0, c * 512 + qlo:(c + 1) * 512],
                                                sreads=["KT:%d" % (j // 4), "QT:%d" % c], bias=None, masks=masks,
                                                pv=[(Vaug[:, j, 0, :], ["Vaug"], obs[2 * m][0], obs[2 * m][1]),
                                                    (ones16[:], ["ones16"], obs[2 * m + 1][0], obs[2 * m + 1][1])]))
                                    return us

                                def finish_A(c, h=h):
                                    cs = slice(c * 512, (c + 1) * 512)
                                    P.op("act", lambda e: e.activation(out=r0[:], in_=banks[4][:], func=AF.Ln), reads=["ps4"], writes=["r0"])
                                    P.op("dve", lambda e: e.tensor_copy(out=oo[:], in_=banks[3][:]), reads=["ps3"], writes=["oo"])
                                    P.op("act", lambda e: e.activation(out=r1[:], in_=banks[6][:], func=AF.Ln), reads=["ps6"], writes=["r1"])
                                    P.op("dve", lambda e: e.tensor_copy(out=t1[:], in_=banks[5][:]), reads=["ps5"], writes=["t1"])
                                    P.op("act", lambda e: e.activation(out=r0[:], in_=r0[:], func=AF.Exp, scale=-1.0), reads=["r0"], writes=["r0"])
                                    P.op("act", lambda e: e.activation(out=r1[:], in_=r1[:], func=AF.Exp, scale=-1.0), reads=["r1"], writes=["r1"])
                                    P.op("pool", lambda e: e.tensor_tensor(out=oo[:], in0=oo[:], in1=r0[:], op=ALU.mult),
                                         reads=["oo", "r0"], writes=["oo"])
                                    P.op("pool", lambda e: e.tensor_tensor(out=t1[:], in0=t1[:], in1=r1[:], op=ALU.mult),
                                         reads=["t1", "r1"], writes=["t1"])
                                    P.op("dve", lambda e: e.scalar_tensor_tensor(out=oo[:], in0=t1[:], scalar=neglam[:, 0:1], in1=oo[:],
                                                                                  op0=ALU.mult, op1=ALU.add),
                                         reads=["t1", "oo", "neglam"], writes=["oo"])
                                    P.op("act", lambda e: e.activation(out=sq[:], in_=oo[:], func=AF.Square), reads=["oo"], writes=["sq"])
                                    pm, pmn = bank([0, 1, 2, 7])
                                    mm(pm[:], ones32[:], sq[:], True, True, ["sq", "ones32"], [pmn])
                                    P.op("act", lambda e: e.activation(out=sq[:], in_=pm[:], func=AF.Ln, bias=epsc[:], scale=1.0 / 128.0),
                                         reads=[pmn, "epsc"], writes=["sq"])
                                    P.op("act", lambda e: e.activation(out=sq[:], in_=sq[:], func=AF.Exp, scale=-0.5), reads=["sq"], writes=["sq"])
                                    P.op("dve", lambda e: e.scalar_tensor_tensor(out=mixT[:, h, cs], in0=oo[:], scalar=subg[:, 0:1], in1=sq[:],
                                                                                  op0=ALU.mult, op1=ALU.mult),
                                         reads=["oo", "sq", "subg"], writes=["mixT:%d" % h])

                                dense_attn(units_A, finish_A, LOOK=2, SBK=(0, 1, 2, 7), PAIR=True, MENG="dve")
                            if bgs["st"] is not None:
                                bgs["st"].flush()
                                bgs["st"] = None
                            P.barrier()
                        P.op("pool", lambda e: e.memset(Vaug[:, :, :, 64:128], 1.0), writes=["Vaug"])
                        bt_ = ExitStack()
                        accs = [sb(bt_, "acc%d" % i, [128, T], F32) for i in range(2)]

                        for j in range(4):
                            w, wn = load_group_w(gi, [12 + j, 16 + j, 20 + j])
                            eidx = []
                            for s in range(2):
                                hb = 2 * j + s
                                eidx += [4 + hb, 20 + hb, 28 + hb, 36 + hb, 44 + hb]
                            e_, en = load_E(gi, eidx)
                            gi += 1
                            proj_T(lambda tg, pb, pbn: evac(QT[:, 0, tg * 512:(tg + 1) * 512], pb[:], [pbn], ["QT:%d" % tg]), w, wn, 0)
                            proj_T(lambda tg, pb, pbn: evac(KT[:, 0, tg * 512:(tg + 1) * 512], pb[:], [pbn], ["KT:%d" % tg]), w, wn, 1)
                            QA = ["QT:%d" % i for i in range(4)]
                            KA = ["KT:%d" % i for i in range(4)]

                            def pth_next():
                                i = ctr.get("pth", 0) % 8
                                ctr["pth"] = ctr.get("pth", 0) + 1
                                return PTs[i // 2][:, (i % 2) * 256:(i % 2) * 256 + 256], "PTh%d" % i

                            def window_pattern(s, nset, kset_ap, qset_ap, e2, vslot, obank_of, flush):
                                rb = s * 64
                                pts = {}

                                def issue(i):
                                    ncol = 256 if i + 1 < nset else 128
                                    pb, pbn = bank(SB)
                                    mm(pb[:, 0:ncol], kset_ap(rb, i), qset_ap(rb, i, ncol), True, True, QA + KA, [pbn])
                                    pt, ptn = pth_next()
                                    P.op("act", lambda e: e.activation(out=pt[:, 0:ncol], in_=pb[:, 0:ncol], func=AF.Exp, scale=0.125),
                                         reads=[pbn], writes=[ptn])
                                    P.op("dve", lambda e: e.tensor_tensor(out=pt[:, 0:ncol], in0=pt[:, 0:ncol], in1=e2[:, 0:ncol], op=ALU.mult),
                                         reads=[ptn, en], writes=[ptn])
                                    pts[i] = (pt, ptn)

                                issue(0)
                                if nset > 1:
                                    issue(1)
                                yield
                                for i in range(nset):
                                    if i + 2 < nset:
                                        issue(i + 2)
                                    ob, obn, col = obank_of(i)
                                    if i > 0:
                                        pt, ptn = pts[i - 1]
                                        mm(ob[:, col:col + 128], Vaug[:, vslot(i - 1), s, :], pt[:, 128:256], True, False,
                                           [ptn, "Vaug"], [obn], inc=False)
                                    pt, ptn = pts[i]
                                    mm(ob[:, col:col + 128], Vaug[:, vslot(i), s, :], pt[:, 0:128], i == 0, True,
                                       [ptn, "Vaug"], [obn], inc=True)
                                    flush(i, ob, obn)
                                    yield

                            SB = [0, 1, 2, 7]
                            proj_V(w, wn, contig, True)

                            def pat1(s):
                                acc, an = accs[s], "acc%d" % s
                                e2 = e_[:, s * 5:s * 5 + 2, :].rearrange("p a q -> p (a q)")
                                cur = {}

                                def ob1(i):
                                    if i % 4 == 0:
                                        cur["b"] = bank(OB)
                                    return cur["b"][0], cur["b"][1], (i % 4) * 128

                                def fl1(i, ob, obn):
                                    if i % 4 == 3:
                                        n = i // 4
                                        evac(acc[:, n * 512:(n + 1) * 512], ob[:], [obn], [an])

                                return window_pattern(s, 16,
                                                      lambda rb, i: KT[rb:rb + 64, 0, i * 128:(i + 1) * 128],
                                                      lambda rb, i, ncol: QT[rb:rb + 64, 0, i * 128:i * 128 + ncol],
                                                      e2, lambda i: i, ob1, fl1)

                            run_interleaved([pat1(0), pat1(1)])
                            proj_V(w, wn, lambda slot: (slice(512 * (slot % 4) + slot // 4, 512 * (slot % 4) + 512, 4), [slot % 4]), True)

                            def pat2(s, r):
                                acc, an = accs[s], "acc%d" % s
                                e2 = e_[:, s * 5 + 2:s * 5 + 4, :].rearrange("p a q -> p (a q)")
                                cur = {"b": bank(OB)}

                                def fl2(i, ob, obn):
                                    if i == 3:
                                        av = acc[:, :].rearrange("p (n u f) -> p n u f", n=4, u=128, f=4)[:, :, :, r]
                                        P.op("dve", lambda e: e.tensor_tensor(out=av, in0=av, in1=ob[:].rearrange("p (n u) -> p n u", n=4),
                                                                              op=ALU.add), reads=[obn, an], writes=[an])

                                return window_pattern(s, 4,
                                                      lambda rb, n: KT[rb:rb + 64, 0, 512 * n + r:512 * n + 512:4],
                                                      lambda rb, n, ncol: QT[rb:rb + 64, 0, 512 * n + r:512 * n + 4 * ncol:4],
                                                      e2, lambda n: r * 4 + n,
                                                      lambda n: (cur["b"][0], cur["b"][1], n * 128), fl2)

                            for r in range(4):
                                run_interleaved([pat2(0, r), pat2(1, r)])
                            proj_V(w, wn, lambda slot: (slice(slot, T, 16), [0, 1, 2, 3]), True)
                            p3u = [(r16, s) for r16 in range(16) for s in range(2)]
                            ob3 = {}
                            pts3 = {}
                            L3 = 4

                            def issue3(idx):
                                r16, s = p3u[idx]
                                rb = s * 64
                                e3 = e_[:, s * 5 + 4, :]
                                pb, pbn = bank(SB)
                                mm(pb[:, 0:128], KT[rb:rb + 64, 0, r16:T:16], QT[rb:rb + 64, 0, r16:T:16], True, True, QA + KA, [pbn])
                                pt, ptn = pth_next()
                                P.op("act", lambda e: e.activation(out=pt[:, 0:128], in_=pb[:, 0:128], func=AF.Exp, scale=0.125),
                                     reads=[pbn], writes=[ptn])
                                P.op("dve", lambda e: e.tensor_tensor(out=pt[:, 0:128], in0=pt[:, 0:128], in1=e3, op=ALU.mult),
                                     reads=[ptn, en], writes=[ptn])
                                pts3[idx] = (pt, ptn)

                            for idx in range(min(L3, len(p3u))):
                                issue3(idx)
                            for idx, (r16, s) in enumerate(p3u):
                                if idx + L3 < len(p3u):
                                    issue3(idx + L3)
                                rr = r16 % 4
                                if rr == 0:
                                    ob3[s] = bank(OB)
                                ob, obn = ob3[s]
                                pt, ptn = pts3.pop(idx)
                                mm(ob[:, rr * 128:(rr + 1) * 128], Vaug[:, r16, s, :], pt[:, 0:128], True, True, [ptn, "Vaug"], [obn],
                                   inc=True)
                                if rr == 3:
                                    r0_ = r16 - 3
                                    acc, an = accs[s], "acc%d" % s
                                    av = acc[:, :].rearrange("p (u f) -> p f u", f=16)[:, r0_:r0_ + 4, :]
                                    P.op("dve", lambda e: e.tensor_tensor(out=av, in0=av, in1=ob[:].rearrange("p (f u) -> p f u", f=4),
                                                                          op=ALU.add), reads=[obn, an], writes=[an])
                            for s in range(2):
                                acc, an = accs[s], "acc%d" % s
                                for c in range(4):
                                    cs = slice(c * 512, (c + 1) * 512)
                                    rc, rcn = (rec, "rec") if c % 2 == 0 else (rec2, "rec2")
                                    P.op("act", lambda e, acc=acc, cs=cs, rc=rc: e.activation(out=rc[0:64, :], in_=acc[64:128, cs], func=AF.Ln),
                                         reads=[an, rcn], writes=[rcn])
                                    P.op("act", lambda e, rc=rc: e.activation(out=rc[0:64, :], in_=rc[0:64, :], func=AF.Exp, scale=-1.0),
                                         reads=[rcn], writes=[rcn])
                                    P.op("pool", lambda e, acc=acc, cs=cs, s=s, j=j, rc=rc: e.tensor_tensor(
                                        out=mixT[s * 64:(s + 1) * 64, 4 + j, cs], in0=acc[0:64, cs], in1=rc[0:64, :], op=ALU.mult),
                                        reads=[an, rcn], writes=["mixT:%d" % (4 + j)])
                        P.barrier()
                        bt_.close()
                    else:
                        def finish_pair(s, mc):
                            def fin(c, s=s, mc=mc):
                                ob, obn = cur_o["b%d" % s]
                                cs = slice(c * 512, (c + 1) * 512)
                                rc, rcn = (rec, "rec") if s == 0 else (rec2, "rec2")
                                P.op("act", lambda e: e.activation(out=rc[0:64, :], in_=ob[64:128, :], func=AF.Ln), reads=[obn, rcn], writes=[rcn])
                                P.op("act", lambda e: e.activation(out=rc[0:64, :], in_=rc[0:64, :], func=AF.Exp, scale=-1.0), reads=[rcn], writes=[rcn])
                                P.op("dve", lambda e: e.tensor_tensor(out=mixT[s * 64:(s + 1) * 64, mc, cs], in0=ob[0:64, :], in1=rc[0:64, :],
                                                                      op=ALU.mult), reads=[obn, rcn], writes=["mixT:%d" % mc])
                            return fin

                        cur_o = {}

                        def merge_units(ufs):
                            def mu(c):
                                lists = [uf(c) for uf in ufs]
                                out = []
                                for i in range(max(len(x) for x in lists)):
                                    for x in lists:
                                        if i < len(x):
                                            out.append(x[i])
                                return out
                            return mu

                        def merged_finish(mc):
                            def mf(c):
                                for s in range(2):
                                    finish_pair(s, mc)(c)
                            return mf

                        if b == 0:
                            bgs["st"] = BgCast(mixT, bg_l1, 4)
                        for j in range(4):
                            w, wn = load_group_w(gi, [j, 4 + j, 8 + j])
                            e_, en = load_E(gi, [2 * j, 12 + 2 * j, 2 * j + 1, 12 + 2 * j + 1])
                            gi += 1
                            for s in range(2):
                                P.dma("sp", KT[64:72, s, :], kaug16, reads=["KTaug%d" % s], writes=["KTaug%d" % s])

                            def dq(tg, pb, pbn):
                                for s in range(2):
                                    evac(QT[0:64, s, tg * 512:(tg + 1) * 512], pb[s * 64:(s + 1) * 64, :], [pbn], ["QT%d:%d" % (s, tg)])

                            def dk(tg, pb, pbn):
                                for s in range(2):
                                    evac(KT[0:64, s, tg * 512:(tg + 1) * 512], pb[s * 64:(s + 1) * 64, :], [pbn], ["KT%d:%d" % (s, tg)])

                            proj_T(dq, w, wn, 0)
                            proj_T(dk, w, wn, 1)
                            proj_V(w, wn, contig, True)
                            ufs = []
                            ggens = []
                            for s in range(2):
                                QAs = ["QT%d:%d" % (s, i) for i in range(4)]
                                KAs = ["KT%d:%d" % (s, i) for i in range(4)]
                                def gate_gen(s=s, QAs=QAs, KAs=KAs):
                                    gate_, top8_, sel_, negpad_ = gts[s]
                                    gn, tn, sn_, nn = "gate%d" % s, "top8%d" % s, "sel%d" % s, "negpad%d" % s
                                    P.op("dve", lambda e: e.tensor_reduce(out=ksum[0:64, s, :],
                                                                          in_=KT[0:64, s, :].rearrange("p (n t) -> p n t", t=256),
                                                                          axis=AX.X, op=ALU.add), reads=KAs, writes=["ksum%d" % s])
                                    P.op("dve", lambda e: e.tensor_copy(out=kmb[0:64, s, :], in_=ksum[0:64, s, :]),
                                         reads=["ksum%d" % s], writes=["kmb%d" % s])
                                    yield
                                    pg, pgn = bank()
                                    for ti in range(NT):
                                        mm(pg[:, ti * 8:(ti + 1) * 8], QT[0:64, s, ti * 128:(ti + 1) * 128], kmb[0:64, s, :], True, True,
                                           QAs + ["kmb%d" % s], [pgn], inc=(ti == NT - 1))
                                    yield
                                    P.op("dve", lambda e: e.tensor_tensor(out=gate_[:], in0=pg[:, 0:128].rearrange("p (t n) -> p t n", n=8),
                                                                          in1=gmask[:], op=ALU.add), reads=[pgn, "gmask"], writes=[gn])
                                    for ti in range(NT):
                                        P.op("dve", lambda e, ti=ti: e.max(out=top8_[:, ti, :], in_=gate_[:, ti, :]), reads=[gn], writes=[tn])
                                    P.op("dve", lambda e: e.tensor_tensor(out=sel_[:], in0=gate_[:], in1=top8_[:, :, 3:4].to_broadcast([128, NT, 8]),
                                                                          op=ALU.is_ge), reads=[gn, tn], writes=[sn_])
                                    P.op("dve", lambda e: e.tensor_scalar(out=negpad_[:, :, 64:72], in0=sel_[:], scalar1=1.0, scalar2=-NEG,
                                                                          op0=ALU.subtract, op1=ALU.mult), reads=[sn_], writes=[nn])
                                    yield
                                    for tg in range(4):
                                        pa, pan = bank()
                                        for jj in range(4):
                                            ti = tg * 4 + jj
                                            mm(pa[0:72, jj * 128:(jj + 1) * 128], negpad_[:, ti, :], ident16[:], True, True,
                                               [nn, "ident16"], [pan], inc=(jj == 3))
                                        evac(QT[64:72, s, tg * 512:(tg + 1) * 512], pa[64:72, :], [pan], ["QTaug%d:%d" % (s, tg)])

                                ggens.append(gate_gen())

                                def units_C(c, s=s, e_=e_, en=en, QAs=QAs, KAs=KAs):
                                    if cur_o.get("c%d" % s) != (s, c, "C", j):
                                        cur_o["b%d" % s] = bank(OB)
                                        cur_o["c%d" % s] = (s, c, "C", j)
                                    ob, obn = cur_o["b%d" % s]
                                    us = []
                                    for kt in range(4 * c + 4):
                                        jj = kt - 4 * c
                                        qlo = max(jj, 0) * 128
                                        masks = []
                                        for ii in range(4):
                                            i = 4 * c + ii
                                            if kt == i:
                                                masks.append((ii, e_[:, 2 * s, :], en))
                                            elif kt == i - 1:
                                                masks.append((ii, e_[:, 2 * s + 1, :], en))
                                        us.append(dict(
                                            qlo=qlo, diag=(jj >= 0), first=(kt == 0),
                                            kT=KT[0:72, s, kt * 128:(kt + 1) * 128],
                                            qT=QT[0:72, s, c * 512 + qlo:(c + 1) * 512],
                                            sreads=["KT%d:%d" % (s, kt // 4), "KTaug%d" % s, "QT%d:%d" % (s, c), "QTaug%d:%d" % (s, c)],
                                            bias=None, masks=masks,
                                            pv=[(Vaug[:, kt, s, :], ["Vaug"], ob, obn)]))
                                    return us

                                ufs.append(units_C)
                            run_interleaved(ggens)
                            dense_attn(merge_units(ufs), merged_finish(j), LOOK=3, SBK=(0, 1, 2, 7), MENG="dve")

                        if bgs["st"] is not None:
                            bgs["st"].flush()
                            bgs["st"] = None
                            P.barrier()
                        pf, pfn = bank()
                        for ti in range(NT):
                            for k in range(8):
                                mm(pf[:, ti * 8:(ti + 1) * 8], hT[:, k, ti * 128:(ti + 1) * 128], wfs[:, k, :], k == 0, k == 7,
                                   hT_reads([ti // 4]) + ["wfs"], [pfn], inc=(k == 7 and ti == NT - 1))
                        P.op("dve", lambda e: e.tensor_tensor(out=zf[:], in0=pf[:, 0:128].rearrange("p (t n) -> p t n", n=8),
                                                              in1=fbb[:, :].unsqueeze(1).to_broadcast([128, NT, 8]), op=ALU.add),
                             reads=[pfn, "fbb"], writes=["zf"])
                        P.op("act", lambda e: e.activation(out=zf[:], in_=zf[:], func=AF.Exp, scale=-1.0), reads=["zf"], writes=["zf"])
                        P.op("act", lambda e: e.activation(out=zf[:], in_=zf[:], func=AF.Ln, bias=1.0, scale=1.0), reads=["zf"], writes=["zf"])
                        pc, pcn = bank()
                        zf2 = zf[:].rearrange("p t n -> p (t n)")
                        mm(pc[:, 0:128], tri[:], zf2, True, True, ["tri", "zf"], [pcn], inc=False)
                        mm(pc[:, 128:256], ones32[:], zf2, True, True, ["ones32", "zf"], [pcn], inc=True)
                        P.op("dve", lambda e: e.tensor_copy(out=cwt[:].rearrange("p a t n -> p (a t n)"), in_=pc[:, 0:256]), reads=[pcn], writes=["cwt"])
                        P.op("dve", lambda e: e.memset(offn[:, 0, :], 0.0), writes=["offn"])
                        for ti in range(NT):
                            P.op("dve", lambda e, ti=ti: e.tensor_tensor(out=offn[:, ti + 1, :], in0=offn[:, ti, :], in1=cwt[:, 1, ti, :], op=ALU.add),
                                 reads=["offn", "cwt"], writes=["offn"])
                        P.op("dve", lambda e: e.tensor_tensor(out=ncum[:], in0=offn[:, 0:NT, :], in1=cwt[:, 0, :, :], op=ALU.add),
                             reads=["offn", "cwt"], writes=["ncum"])
                        for c in range(4):
                            P.op("dve", lambda e, c=c: e.tensor_tensor(out=bfox[:, c, :, :], in0=ncum[:],
                                                                       in1=offn[:, 4 * c + 2:4 * c + 3, :].to_broadcast([128, NT, 8]),
                                                                       op=ALU.subtract), reads=["ncum", "offn"], writes=["bfox"])

                        for j in range(4):
                            w, wn = load_group_w(gi, [12 + j, 16 + j, 20 + j])
                            e_, en = load_E(gi, [52])
                            gi += 1
                            proj_T(lambda tg, pb, pbn: evac(QT[:, 0, tg * 512:(tg + 1) * 512], pb[:], [pbn],
                                                            ["QT0:%d" % tg, "QTaug0:%d" % tg]), w, wn, 0)
                            proj_T(lambda tg, pb, pbn: evac(KT[:, 0, tg * 512:(tg + 1) * 512], pb[:], [pbn],
                                                            ["KT0:%d" % tg, "KTaug0"]), w, wn, 1)
                            proj_V(w, wn, contig, True)
                            ufs = []
                            for s in range(2):
                                hd = 2 * j + s

                                def units_D(c, s=s, hd=hd, e_=e_, en=en):
                                    if cur_o.get("c%d" % s) != (s, c, "D", j):
                                        cur_o["b%d" % s] = bank(OB)
                                        cur_o["c%d" % s] = (s, c, "D", j)
                                    ob, obn = cur_o["b%d" % s]
                                    us = []
                                    rb = s * 64
                                    for kt in range(4 * c + 4):
                                        jj = kt - 4 * c
                                        qlo = max(jj, 0) * 128
                                        masks = [(jj, e_[:, 0, :], en)] if jj >= 0 else []
                                        us.append(dict(
                                            qlo=qlo, diag=(jj >= 0), first=(kt == 0),
                                            kT=KT[rb:rb + 64, 0, kt * 128:(kt + 1) * 128],
                                            qT=QT[rb:rb + 64, 0, c * 512 + qlo:(c + 1) * 512],
                                            sreads=["KT0:%d" % (kt // 4), "QT0:%d" % c],
                                            bias=bfox[:, c, kt, hd:hd + 1], masks=masks,
                                            pv=[(Vaug[:, kt, s, :], ["Vaug"], ob, obn)]))
                                    return us

                                ufs.append(units_D)
                            dense_attn(merge_units(ufs), merged_finish(4 + j), LOOK=2, SBK=(0, 1, 2, 7), PAIR=True, MENG="dve")
                    P.barrier()
                if stage < 99 and stage == 2 * (2 * b + l):
                    P.dma("sp", dbg["mixT"], mixT[:], reads=["mixT:%d" % i for i in range(8)])
                with ExitStack() as ws:
                    wos = sb(ws, "wos", [128, 8, D], BF16)
                    LNG = sb(ws, "LNG", [128, D], F32)
                    LNB = sb(ws, "LNB", [128, D], F32)
                    GB = sb(ws, "GB", [128, D], F32)
                    zts = [sb(ws, "zt%d" % i, [128, D], F32) for i in range(2)]
                    load_ln(l, 0, b, LNG, LNB, GB)
                    for k in range(8):
                        P.dma("sp", wos[:, k, :], wo16[l, :, k, :], reads=["wo16:%d" % l], writes=["wos:%d" % k])
                        P.op("dve" if k % 2 == 0 else "pool", lambda e, k=k: e.tensor_tensor(out=wos[:, k, :], in0=wos[:, k, :], in1=GB[:], op=ALU.mult),
                             reads=["wos:%d" % k, "GB"], writes=["wos:%d" % k])
                    pend = None
                    for ti in range(NT):
                        psrc = []
                        for hf in range(2):
                            pb, pbn = bank()
                            for k in range(8):
                                mm(pb[:], mixT[:, k, ti * 128:(ti + 1) * 128], wos[:, k, hf * 512:(hf + 1) * 512], k == 0, k == 7,
                                   ["mixT:%d" % k, "wos:%d" % k], [pbn])
                            psrc.append((pb, pbn))
                        fin = ln_residual(ti, zts, LNG, LNB, psrc)
                        if pend is not None:
                            pend()
                        pend = fin
                    pend()
                    P.barrier()

        def ffn(l, b):
            with ExitStack() as fs:
                aT = sb(fs, "aT", [128, NF, 512], BF16)
                wout = sb(fs, "wout", [128, NF, D], BF16)
                wgu = [sb(fs, "wgu%d" % i, [128, 8, 256], BF16) for i in range(3)]
                sg = [sb(fs, "sg%d" % i, [128, 512], F32) for i in range(2)]
                LNG = sb(fs, "LNG", [128, D], F32)
                LNB = sb(fs, "LNB", [128, D], F32)
                GB = sb(fs, "GB", [128, D], F32)
                zts = [sb(fs, "zt%d" % i, [128, D], F32) for i in range(2)]
                load_ln(l, 1, b, LNG, LNB, GB)
                for f in range(NF):
                    P.dma("sp", wout[:, f, :], wfo16[l, f], reads=["wfo16:%d" % l], writes=["wout:%d" % f])
                    P.op("pool", lambda e, f=f: e.tensor_tensor(out=wout[:, f, :], in0=wout[:, f, :], in1=GB[:], op=ALU.mult),
                         reads=["wout:%d" % f, "GB"], writes=["wout:%d" % f])
                it = 0
                for tc in range(4):
                    for f in range(NF):
                        wi = it % 3
                        it += 1
                        P.dma("sp", wgu[wi][:], wfi16[l, f], reads=["wfi16:%d" % l], writes=["wgu%d" % wi])
                        pg, pgn = bank()
                        pu, pun = bank()
                        for k in range(8):
                            mm(pg[:], wgu[wi][:, k, 0:128], hT[:, k, tc * 512:(tc + 1) * 512], k == 0, k == 7, hT_reads([tc]) + ["wgu%d" % wi], [pgn])
                        for k in range(8):
                            mm(pu[:], wgu[wi][:, k, 128:256], hT[:, k, tc * 512:(tc + 1) * 512], k == 0, k == 7, hT_reads([tc]) + ["wgu%d" % wi], [pun])
                        s_ = sg[f % 2]
                        sn = "sg%d" % (f % 2)
                        P.op("act", lambda e, s_=s_, pg=pg: e.activation(out=s_[:], in_=pg[:], func=AF.Silu), reads=[pgn], writes=[sn])
                        P.op("dve", lambda e, s_=s_, pu=pu, f=f: e.tensor_tensor(out=aT[:, f, :], in0=pu[:], in1=s_[:], op=ALU.mult),
                             reads=[pun, sn], writes=["aT:%d" % f])
                    pend = None
                    for jt in range(4):
                        ti = tc * 4 + jt
                        psrc = []
                        for hf in range(2):
                            pb, pbn = bank()
                            for f in range(NF):
                                mm(pb[:], aT[:, f, jt * 128:(jt + 1) * 128], wout[:, f, hf * 512:(hf + 1) * 512], f == 0, f == NF - 1,
                                   ["aT:%d" % f, "wout:%d" % f], [pbn])
                            psrc.append((pb, pbn))
                        fin = ln_residual(ti, zts, LNG, LNB, psrc)
                        if pend is not None:
                            pend()
                        pend = fin
                    pend()
                    pend = None
                P.barrier()

        done = False
        for b in range(NB):
            for tg in range(4):
                P.dma("sp", X[:, tg * 4:(tg + 1) * 4, :], x_d[b, tg * 512:(tg + 1) * 512, :].rearrange("(t p) d -> p t d", p=128),
                      writes=["X:%d" % (tg * 4 + i) for i in range(4)])
            if stage == -1:
                done = True
            for l in range(2):
                if done:
                    break
                load_mod(l, b)
                transposes(0)
                mixer(l, b)
                if stage == 2 * (2 * b + l):
                    done = True
                    break
                transposes(1)
                ffn(l, b)
                if stage == 2 * (2 * b + l) + 1:
                    done = True
                    break
            if done:
                P.dma("sp", dbg["X"].rearrange("(t p) d -> p t d", p=128), X[:], reads=["X:%d" % i for i in range(NT)])
                break
            for tg in range(4):
                P.dma("sp", out_d[b, tg * 512:(tg + 1) * 512, :].rearrange("(t p) d -> p t d", p=128), X[:, tg * 4:(tg + 1) * 4, :],
                      reads=["X:%d" % (tg * 4 + i) for i in range(4)])
        P.barrier()
        build_program.stats = dict(n_ins=dict(P.n_ins), count=dict(P.count), dma=list(P.dma_cnt))
    return nc


def host_inputs(inputs):
    global ECOLS
    rel_bias = np.asarray(inputs["rel_bias"], np.float32)
    tiles, cols = make_bias_tiles(rel_bias)
    ECOLS = cols
    k = np.arange(128)
    tri = (k[:, None] <= k[None, :]).astype(np.float32)
    kaug = np.zeros((8, T), np.float32)
    for n in range(8):
        kaug[n, n * 256:(n + 1) * 256] = 1.0
    shared = {
        "rel_bias": rel_bias,
        "w_ada": np.ascontiguousarray(inputs["w_ada"], np.float32),
        "b_ada": np.ascontiguousarray(inputs["b_ada"], np.float32),
        "ln_g": np.ascontiguousarray(inputs["ln_g"], np.float32),
        "ln_b": np.ascontiguousarray(inputs["ln_b"], np.float32),
        "w_in_ab": np.ascontiguousarray(np.asarray(inputs["w_in_ab"], np.float32)[0]),
        "diff_lambda": np.ascontiguousarray(np.asarray(inputs["diff_lambda"], np.float32).reshape(256)),
        "diff_subln_g": np.ascontiguousarray(np.asarray(inputs["diff_subln_g"], np.float32).reshape(128)),
        "w_in_cd": np.ascontiguousarray(np.asarray(inputs["w_in_cd"], np.float32)[0]),
        "forget_b": np.ascontiguousarray(np.asarray(inputs["forget_b"], np.float32).reshape(8)),
        "w_o": np.ascontiguousarray(inputs["w_o"], np.float32),
        "w_ffn_in": np.ascontiguousarray(inputs["w_ffn_in"], np.float32),
        "w_ffn_out": np.ascontiguousarray(inputs["w_ffn_out"], np.float32),
        "btiles": tiles,
        "ident": np.eye(128, dtype=np.float32),
        "tri": tri,
        "kaug": kaug,
    }
    return shared


def kernel(**inputs):
    shared = host_inputs(inputs)
    x = np.asarray(inputs["x"], np.float32)
    c = np.asarray(inputs["c"], np.float32)
    n = 8
    nc = build_program()
    in_maps = []
    for i in range(n):
        m = dict(shared)
        m["x"] = np.ascontiguousarray(x[NB * i:NB * (i + 1)])
        m["c"] = np.ascontiguousarray(c[NB * i:NB * (i + 1)])
        in_maps.append(m)
    res = run_bass_kernel_spmd(nc, in_maps, core_ids=list(range(n)))
    return np.concatenate([np.asarray(r["out"], np.float32) for r in res.results], axis=0)
```

```python
import math
import numpy as np
from contextlib import ExitStack
import concourse.bass as bass
import concourse.mybir as mybir
from concourse.bass_utils import run_bass_kernel_spmd

F32 = mybir.dt.float32
BF16 = mybir.dt.bfloat16
AF = mybir.ActivationFunctionType
ALU = mybir.AluOpType
AX = mybir.AxisListType

T = 2048
D = 1024
NT = 16
DFF = 2816
NF = 22
NB = 2
ALPHA = 4 ** 0.25
EPS = 1e-5
NEG = -30000.0
N_ETILES = 53

ENGS = ("pe", "act", "dve", "pool", "sp")
N_DMA_SEMS = 24


def run_interleaved(gens):
    live = list(gens)
    while live:
        nxt = []
        for g in live:
            try:
                next(g)
                nxt.append(g)
            except StopIteration:
                pass
        live = nxt


class Prog:
    def __init__(self, nc, es):
        self.nc = nc
        self.eng_obj = {"pe": nc.tensor, "act": nc.scalar, "dve": nc.vector,
                        "pool": nc.gpsimd, "sp": nc.sync}
        self.count = {e: 0 for e in ENGS}
        self.known = {e: {} for e in ENGS}
        self.last_w = {}
        self.readers = {}
        self.dma_rr = 0
        self.dma_cnt = [0] * N_DMA_SEMS
        self.sems = {}
        self.n_ins = {e: 0 for e in ENGS}
        for e in ENGS:
            self.sems[e] = es.enter_context(nc.semaphore("s_" + e))
        for j in range(N_DMA_SEMS):
            self.sems[("d", j)] = es.enter_context(nc.semaphore("s_d%d" % j))

    def _deps(self, reads, writes):
        toks = []
        for r in reads:
            t = self.last_w.get(r)
            if t is not None:
                toks.append(t)
        for w in writes:
            t = self.last_w.get(w)
            if t is not None:
                toks.append(t)
            toks.extend(self.readers.get(w, ()))
        return toks

    def _commit(self, tok, reads, writes):
        for r in reads:
            self.readers.setdefault(r, []).append(tok)
        for w in writes:
            self.last_w[w] = tok
            self.readers[w] = []

    def _waits(self, eng, toks):
        need = {}
        kn = self.known[eng]
        for (k, v) in toks:
            if eng == "pe" and k == "pe":
                continue
            if kn.get(k, 0) >= v:
                continue
            if need.get(k, 0) < v:
                need[k] = v
        for k, v in need.items():
            kn[k] = v
        return list(need.items())

    def _emit(self, eng, waits, fn, inc):
        e = self.eng_obj[eng]
        for k, v in waits:
            e.wait_ge(self.sems[k], v)
        if fn is not None:
            ins = fn(e)
            self.n_ins[eng] += 1
            if inc is not None:
                ins.then_inc(self.sems[inc[0]], inc[1])

    def op(self, eng, fn, reads=(), writes=(), inc=True):
        toks = self._deps(reads, writes)
        waits = self._waits(eng, toks)
        if inc:
            self.count[eng] += 1
            tok = (eng, self.count[eng])
            self._emit(eng, waits, fn, (eng, 1))
        else:
            tok = (eng, self.count[eng] + 1)
            self._emit(eng, waits, fn, None)
        self._commit(tok, reads, writes)
        return tok

    def dma(self, q, out, in_, reads=(), writes=(), slow=False):
        toks = self._deps(reads, writes)
        j = self.dma_rr
        self.dma_rr = (self.dma_rr + 1) % N_DMA_SEMS
        if self.dma_cnt[j] > 0:
            toks.append((("d", j), 16 * self.dma_cnt[j]))
        waits = self._waits(q, toks)
        self.dma_cnt[j] += 1
        tok = (("d", j), 16 * self.dma_cnt[j])
        if slow:
            fn = lambda e: e.dma_start(out=out, in_=in_, allow_slow_non_contiguous=True)
        else:
            fn = lambda e: e.dma_start(out=out, in_=in_)
        self._emit(q, waits, fn, (("d", j), 16))
        self._commit(tok, reads, writes)
        return tok

    def all_tokens(self):
        toks = [(e, self.count[e]) for e in ENGS if self.count[e] > 0]
        toks += [(("d", j), 16 * c) for j, c in enumerate(self.dma_cnt) if c > 0]
        return toks

    def barrier(self, engs=ENGS):
        toks = self.all_tokens()
        for e in engs:
            self._emit(e, self._waits(e, list(toks)), None, None)
        self.last_w = {}
        self.readers = {}


def t5_bucket_np(n):
    n = np.maximum(n, 0)
    nf = np.maximum(n, 1).astype(np.float32)
    large = 16 + (np.log(nf / np.float32(16)) / np.float32(math.log(128 / 16)) * np.float32(16)).astype(np.int32)
    large = np.minimum(large, 31)
    return np.where(n < 16, n, large)


def make_bias_tiles(rel_bias):
    k = np.arange(128)[:, None]
    q = np.arange(128)[None, :]
    tiles = np.full((N_ETILES, 128, 128), NEG, np.float32)
    cols = np.zeros((N_ETILES,), np.int64)

    def fill(idx, dist, valid, col):
        b = t5_bucket_np(np.where(valid, dist, 0))
        tiles[idx] = np.where(valid, rel_bias[b, col], np.float32(NEG))
        cols[idx] = col

    for col in range(12):
        fill(col, q - k, q >= k, col)
    for h in range(8):
        fill(12 + h, q - k + 128, np.ones((128, 128), bool), h)
    for hb in range(8):
        col = 4 + hb
        fill(20 + hb, q - k + 128, q <= k, col)
        fill(28 + hb, 4 * (q - k), q >= k, col)
        fill(36 + hb, 4 * (q - k + 128), q <= k, col)
        fill(44 + hb, 16 * (q - k), q >= k, col)
    tiles[52] = np.where(q >= k, np.float32(0.0), np.float32(NEG))
    cols[52] = 12
    return tiles, cols


ECOLS = None


def build_program(stage=99):
    nc = bass.Bass("TRN2", target_bir_lowering=False)
    dram = lambda name, shape, dt, kind: nc.dram_tensor(name, shape, dt, kind=kind).ap()
    x_d = dram("x", [NB, T, D], F32, "ExternalInput")
    c_d = dram("c", [NB, D], F32, "ExternalInput")
    relb_d = dram("rel_bias", [32, 12], F32, "ExternalInput")
    wada_d = dram("w_ada", [2, D, 6 * D], F32, "ExternalInput")
    bada_d = dram("b_ada", [2, 6 * D], F32, "ExternalInput")
    lng_d = dram("ln_g", [2, 2, D], F32, "ExternalInput")
    lnb_d = dram("ln_b", [2, 2, D], F32, "ExternalInput")
    wab_d = dram("w_in_ab", [D, 3072], F32, "ExternalInput")
    lam_d = dram("diff_lambda", [256], F32, "ExternalInput")
    subg_d = dram("diff_subln_g", [128], F32, "ExternalInput")
    wcd_d = dram("w_in_cd", [D, 3080], F32, "ExternalInput")
    fb_d = dram("forget_b", [8], F32, "ExternalInput")
    wo_d = dram("w_o", [2, D, D], F32, "ExternalInput")
    wfi_d = dram("w_ffn_in", [2, D, 2 * DFF], F32, "ExternalInput")
    wfo_d = dram("w_ffn_out", [2, DFF, D], F32, "ExternalInput")
    bt_d = dram("btiles", [N_ETILES, 128, 128], F32, "ExternalInput")
    ident_d = dram("ident", [128, 128], F32, "ExternalInput")
    tri_d = dram("tri", [128, 128], F32, "ExternalInput")
    kaug_d = dram("kaug", [8, T], F32, "ExternalInput")
    out_d = dram("out", [NB, T, D], F32, "ExternalOutput")
    wg16 = dram("wg16", [2, 24, 128, 8, 128], BF16, "Internal")
    kaug16 = dram("kaug16", [8, T], BF16, "Internal")
    wo16 = dram("wo16", [2, 128, 8, D], BF16, "Internal")
    wfi16 = dram("wfi16", [2, NF, 128, 8, 256], BF16, "Internal")
    wfo16 = dram("wfo16", [2, NF, 128, D], BF16, "Internal")
    ada_s = dram("ada_s", [2, NB, 6 * D], F32, "Internal")
    E_d = dram("E_d", [N_ETILES, 128, 128], BF16, "Internal")
    dbg = {}
    if stage < 99:
        dbg["mixT"] = dram("dbg_mixT", [128, 8, T], BF16, "ExternalOutput")
        dbg["X"] = dram("dbg_X", [T, D], F32, "ExternalOutput")

    with ExitStack() as es:
        P = Prog(nc, es)
        ctr = {"ev": 0, "bank": 0, "pt": 0}

        def sb(st, name, shape, dt):
            ctr["uid"] = ctr.get("uid", 0) + 1
            return st.enter_context(nc.sbuf_tensor("%s_u%d" % (name, ctr["uid"]), shape, dt))

        X = sb(es, "X", [128, NT, D], F32)
        hT = sb(es, "hT", [128, 8, T], BF16)
        ident = sb(es, "ident", [128, 128], F32)
        ident16 = sb(es, "ident16", [128, 128], BF16)
        tri = sb(es, "tri", [128, 128], F32)
        ones32 = sb(es, "ones32", [128, 128], F32)
        ones16 = sb(es, "ones16", [128, 128], BF16)
        gmask = sb(es, "gmask", [128, NT, 8], F32)
        modT = sb(es, "modT", [128, 4, 8], F32)
        neglam = sb(es, "neglam", [128, 1], F32)
        subg = sb(es, "subg", [128, 1], F32)
        fbb = sb(es, "fbb", [128, 8], F32)
        epsc = sb(es, "epsc", [128, 1], F32)
        stt = sb(es, "stt", [128, 2, 2, 6], F32)
        mv = sb(es, "mv", [128, 2, 8], F32)
        banks = [es.enter_context(nc.psum_tensor("ps%d" % i, [128, 512], F32)) for i in range(8)]

        def bank(group=None):
            lst = group if group is not None else list(range(8))
            key = "bank" + str(lst)
            i = ctr.get(key, 0)
            ctr[key] = i + 1
            b = lst[i % len(lst)]
            return banks[b], "ps%d" % b

        def evac(out, in_, reads, writes, eng=None):
            if eng is None:
                eng = "act" if ctr["ev"] % 2 == 0 else "dve"
                ctr["ev"] += 1
            if eng == "act":
                P.op("act", lambda e: e.activation(out=out, in_=in_, func=AF.Copy), reads, writes)
            else:
                P.op(eng, lambda e: e.tensor_copy(out=out, in_=in_), reads, writes)

        def mm(out, lhsT, rhs, start, stop, reads, writes, inc=None):
            if inc is None:
                inc = stop
            P.op("pe", lambda e: e.matmul(out, lhsT=lhsT, rhs=rhs, start=start, stop=stop), reads, writes, inc=inc)

        P.dma("sp", ident[:], ident_d, writes=["ident"])
        P.dma("sp", tri[:], tri_d, writes=["tri"])
        P.op("dve", lambda e: e.memset(ones32[:], 1.0), writes=["ones32"])
        P.op("dve", lambda e: e.memset(ones16[:], 1.0), writes=["ones16"])
        P.op("dve", lambda e: e.memset(epsc[:], EPS), writes=["epsc"])
        P.op("dve", lambda e: e.memset(gmask[:], -1e30), writes=["gmask"])
        for ti in range(NT):
            own = ti // 2
            if own > 0:
                P.op("dve", lambda e, ti=ti, own=own: e.memset(gmask[:, ti, 0:own], 0.0), reads=["gmask"], writes=["gmask"])
            P.op("dve", lambda e, ti=ti, own=own: e.memset(gmask[:, ti, own:own + 1], 1e30), reads=["gmask"], writes=["gmask"])
        P.dma("sp", subg[:], subg_d.rearrange("(p o) -> p o", o=1), writes=["subg"])
        P.op("dve", lambda e: e.tensor_scalar(out=subg[:], in0=subg[:], scalar1=0.8, scalar2=None, op0=ALU.mult),
             reads=["subg"], writes=["subg"])
        P.dma("sp", fbb[:], fb_d.partition_broadcast(128), writes=["fbb"])

        modall = sb(es, "modall", [128, 2, 48, NB], F32)
        wfs = sb(es, "wfs", [128, 8, 8], BF16)
        with ExitStack() as ps_:
            ps1 = ExitStack()
            negfar = sb(ps1, "negfar", [128, 16], F32)
            lam = sb(ps1, "lam", [128, 256], F32)
            lamp = sb(ps1, "lamp", [128, 128], F32)
            ls = sb(ps1, "ls", [128, 4], F32)
            btl = [sb(ps1, "btl%d" % i, [128, 4, 128], F32) for i in range(2)]
            etl = [sb(ps1, "etl%d" % i, [128, 4, 128], BF16) for i in range(2)]
            kg32 = sb(ps1, "kg32", [8, T], F32)
            kg16 = sb(ps1, "kg16", [8, T], BF16)

            P.op("dve", lambda e: e.tensor_copy(out=ident16[:], in_=ident[:]), reads=["ident"], writes=["ident16"])
            P.dma("sp", kg32[:], kaug_d, writes=["kg32"])
            P.op("dve", lambda e: e.tensor_copy(out=kg16[:], in_=kg32[:]), reads=["kg32"], writes=["kg16"])
            P.dma("sp", kaug16, kg16[:], reads=["kg16"], writes=["kaug16"])

            P.dma("sp", lam[:], lam_d.partition_broadcast(128), writes=["lam"])
            P.op("dve", lambda e: e.tensor_tensor(out=lamp[:, 0:64], in0=lam[:, 0:64], in1=lam[:, 64:128], op=ALU.mult),
                 reads=["lam"], writes=["lamp"])
            P.op("dve", lambda e: e.tensor_tensor(out=lamp[:, 64:128], in0=lam[:, 128:192], in1=lam[:, 192:256], op=ALU.mult),
                 reads=["lam", "lamp"], writes=["lamp"])
            P.op("dve", lambda e: e.reduce_sum(out=ls[:, 0:1], in_=lamp[:, 0:64], axis=AX.X), reads=["lamp"], writes=["ls"])
            P.op("dve", lambda e: e.reduce_sum(out=ls[:, 1:2], in_=lamp[:, 64:128], axis=AX.X), reads=["lamp", "ls"], writes=["ls"])
            P.op("act", lambda e: e.activation(out=ls[:, 2:4], in_=ls[:, 0:2], func=AF.Exp), reads=["ls"], writes=["ls"])
            P.op("dve", lambda e: e.tensor_tensor(out=neglam[:], in0=ls[:, 3:4], in1=ls[:, 2:3], op=ALU.subtract),
                 reads=["ls"], writes=["neglam"])
            P.op("dve", lambda e: e.tensor_scalar(out=neglam[:], in0=neglam[:], scalar1=-0.2, scalar2=None, op0=ALU.add),
                 reads=["neglam"], writes=["neglam"])

            P.op("dve", lambda e: e.memset(negfar[:], 0.0), writes=["negfar"])
            P.dma("sp", negfar[:, 0:12], relb_d[31, :].partition_broadcast(128), reads=["negfar"], writes=["negfar"])
            P.op("dve", lambda e: e.tensor_scalar(out=negfar[:], in0=negfar[:], scalar1=-1.0, scalar2=None, op0=ALU.mult),
                 reads=["negfar"], writes=["negfar"])
            for gi, t0 in enumerate(range(0, N_ETILES, 4)):
                n = min(4, N_ETILES - t0)
                bb, ee = btl[gi % 2], etl[gi % 2]
                bn, en = "btl%d" % (gi % 2), "etl%d" % (gi % 2)
                P.dma("sp", bb[:, 0:n, :], bt_d[t0:t0 + n].rearrange("t k q -> k t q"), writes=[bn])
                for i in range(n):
                    col = int(ECOLS[t0 + i])
                    P.op("act", lambda e, ee=ee, bb=bb, i=i, col=col: e.activation(
                        out=ee[:, i, :], in_=bb[:, i, :], func=AF.Exp, bias=negfar[:, col:col + 1], scale=1.0),
                        reads=[bn, "negfar"], writes=[en])
                P.dma("sp", E_d[t0:t0 + n].rearrange("t k q -> k t q"), ee[:, 0:n, :], reads=[en], writes=["E_d"])

            P.barrier()
            ps1.close()
            csb = sb(ps_, "csb", [NB, D], F32)
            cT32 = sb(ps_, "cT32", [128, 8, NB], F32)
            badas = [sb(ps_, "bada%d" % i, [NB, 512], F32) for i in range(2)]
            adasb = sb(ps_, "adasb", [NB, 6 * D], F32)
            NSTG = 3
            stg32 = [sb(ps_, "stg32_%d" % i, [128, 8 * 520], F32) for i in range(NSTG)]
            stg16 = [sb(ps_, "stg16_%d" % i, [128, 8 * 512], BF16) for i in range(NSTG)]
            P.dma("sp", csb[:], c_d, writes=["csb"])
            P.op("act", lambda e: e.activation(out=csb[:], in_=csb[:], func=AF.Silu), reads=["csb"], writes=["csb"])
            pb, pbn = bank()
            for k in range(8):
                P.op("pe", lambda e, k=k: e.transpose(pb[:, k * NB:(k + 1) * NB], csb[0:NB, k * 128:(k + 1) * 128], ident[0:NB, 0:NB]),
                     reads=["csb", "ident"], writes=[pbn], inc=(k == 7))
            P.op("dve", lambda e: e.tensor_copy(out=cT32[:].rearrange("p k b -> p (k b)"), in_=pb[:, 0:8 * NB]), reads=[pbn], writes=["cT32"])
            it = 0
            for l in range(2):
                for n in range(12):
                    bada, bdn = badas[it % 2], "bada%d" % (it % 2)
                    P.dma("sp", bada[:], bada_d[l, n * 512:(n + 1) * 512].partition_broadcast(NB), writes=[bdn])
                    si = it % NSTG
                    it += 1
                    wv = stg32[si][:, 0:8 * 512].rearrange("p (c n) -> p c n", c=8)
                    P.dma("sp", wv, wada_d[l, :, n * 512:(n + 1) * 512].rearrange("(c p) n -> p c n", p=128), writes=["stg32_%d" % si])
                    pb, pbn = bank()
                    for k in range(8):
                        mm(pb[0:NB, :], cT32[:, k, :], wv[:, k, :], k == 0, k == 7, ["cT32", "stg32_%d" % si], [pbn])
                    P.op("dve", lambda e, pb=pb, n=n: e.tensor_tensor(
                        out=adasb[:, n * 512:(n + 1) * 512], in0=pb[0:NB, :], in1=bada[:], op=ALU.add),
                        reads=[pbn, bdn], writes=["adasb"])
                P.dma("sp", ada_s[l], adasb[:], reads=["adasb"], writes=["ada_s"])
                pb, pbn = bank()
                for ch in range(48):
                    P.op("pe", lambda e, ch=ch: e.transpose(pb[:, ch * NB:(ch + 1) * NB], adasb[0:NB, ch * 128:(ch + 1) * 128], ident[0:NB, 0:NB]),
                         reads=["adasb", "ident"], writes=[pbn], inc=(ch == 47))
                P.op("dve", lambda e, l=l, pb=pb: e.tensor_copy(out=modall[:, l].rearrange("p c b -> p (c b)"), in_=pb[:, 0:48 * NB]),
                     reads=[pbn], writes=["modall"])
            for c0 in (8, 32):
                P.op("dve", lambda e, c0=c0: e.tensor_scalar(out=modall[:, :, c0:c0 + 8, :], in0=modall[:, :, c0:c0 + 8, :], scalar1=1.0,
                                                               scalar2=None, op0=ALU.add), reads=["modall"], writes=["modall"])

            cast_engs = ["dve", "act", "pool"]
            pieces = []

            def do_cast(eng, ov, iv, n32, n16):
                if eng == "act":
                    P.op("act", lambda e: e.activation(out=ov, in_=iv, func=AF.Copy), reads=[n32], writes=[n16])
                else:
                    P.op(eng, lambda e: e.tensor_copy(out=ov, in_=iv), reads=[n32], writes=[n16])

            def add_piece(loads, casts, stores):
                k = len(pieces)
                eng = cast_engs[k % 3]

                def ld(si):
                    for vf, src in loads:
                        P.dma("sp", vf(stg32[si]), src, reads=["stg32_%d" % si], writes=["stg32_%d" % si])

                def cs(si):
                    for of, inf in casts:
                        do_cast(eng, of(stg16[si]), inf(stg32[si]), "stg32_%d" % si, "stg16_%d" % si)

                def st(si):
                    for dst, vf, names in stores:
                        P.dma("sp", dst, vf(stg16[si]), reads=["stg16_%d" % si], writes=names)
                pieces.append((ld, cs, st))

            for l, src in ((0, wab_d), (1, wcd_d)):
                for g in range(6):
                    if l == 1 and g < 5:
                        continue
                    ncol = 520 if (l == 1 and g == 5) else 512
                    casts = [(lambda t: t[:, 0:4096].rearrange("p (s c n) -> p c s n", s=4, c=8),
                              lambda t, ncol=ncol: t[:, 0:8 * ncol].rearrange("p (c n) -> p c n", c=8)[:, :, 0:512].rearrange(
                                  "p c (s n) -> p c s n", s=4))]
                    if ncol == 520:
                        casts.append((lambda t: wfs[:], lambda t: t[:, 0:8 * 520].rearrange("p (c n) -> p c n", c=8)[:, :, 512:520]))
                    add_piece([(lambda t, ncol=ncol: t[:, 0:8 * ncol].rearrange("p (c n) -> p c n", c=8),
                                src[:, g * 512:g * 512 + ncol].rearrange("(c p) n -> p c n", p=128))],
                              casts,
                              [(wg16[l, g * 4:(g + 1) * 4].rearrange("s p c n -> p s (c n)"),
                                lambda t: t[:, 0:4096].rearrange("p (s x) -> p s x", s=4),
                                ["wg16:%d:%d" % (l, g * 4 + i) for i in range(4)])])
            npc = len(pieces)
            for i in range(npc + 1):
                if i < npc:
                    pieces[i][0](i % NSTG)
                if i >= 1:
                    pieces[i - 1][1]((i - 1) % NSTG)
                    pieces[i - 1][2]((i - 1) % NSTG)
            P.barrier()

        class BgCast:
            def __init__(self, mixT, pieces, every):
                base = mixT[:, 4:8, :].rearrange("p a n -> p (a n)")
                self.s32 = [base[:, i * 2048:(i + 1) * 2048].bitcast(F32) for i in range(3)]
                self.s16 = [base[:, 6144 + i * 1024:6144 + (i + 1) * 1024] for i in range(2)]
                self.pieces = pieces
                self.t = 0
                self.n = 0
                self.every = every

            def view(self, ap, like):
                if len(like.shape) == 3:
                    return ap.rearrange("p (c n) -> p c n", c=like.shape[1])
                return ap

            def step(self):
                t, pcs = self.t, self.pieces
                if t < len(pcs):
                    src, dst, names = pcs[t]
                    P.dma("sp", self.view(self.s32[t % 3], src), src, reads=["bg32_%d" % (t % 3)], writes=["bg32_%d" % (t % 3)])
                u = t - 2
                if 0 <= u < len(pcs):
                    src, dst, names = pcs[u]
                    i32, i16 = u % 3, u % 2
                    eng = "dve"
                    P.op(eng, lambda e: e.tensor_copy(out=self.s16[i16], in_=self.s32[i32]), reads=["bg32_%d" % i32], writes=["bg16_%d" % i16])
                    P.dma("sp", dst, self.view(self.s16[i16], dst), reads=["bg16_%d" % i16], writes=names)
                self.t += 1

            def tick(self):
                self.n += 1
                if self.n % self.every == 0 and self.t < len(self.pieces) + 2:
                    self.step()

            def flush(self):
                while self.t < len(self.pieces) + 2:
                    self.step()

        def col_piece(src2d, dst3d, names):
            return (src2d.rearrange("(c p) n -> p c n", p=128), dst3d, names)

        bg_l0 = []
        for n0 in range(8):
            bg_l0.append(col_piece(wo_d[0, :, n0 * 128:(n0 + 1) * 128], wo16[0, :, :, n0 * 128:(n0 + 1) * 128], ["wo16:0"]))
        for f in range(NF):
            bg_l0.append(col_piece(wfi_d[0, :, f * 128:(f + 1) * 128], wfi16[0, f, :, :, 0:128], ["wfi16:0"]))
            bg_l0.append(col_piece(wfi_d[0, :, DFF + f * 128:DFF + (f + 1) * 128], wfi16[0, f, :, :, 128:256], ["wfi16:0"]))
            bg_l0.append((wfo_d[0, f * 128:(f + 1) * 128, :], wfo16[0, f], ["wfo16:0"]))
        for sl in range(20):
            bg_l0.append(col_piece(wcd_d[:, sl * 128:(sl + 1) * 128], wg16[1, sl], ["wg16:1:%d" % sl]))
        for n0 in range(8):
            bg_l0.append(col_piece(wo_d[1, :, n0 * 128:(n0 + 1) * 128], wo16[1, :, :, n0 * 128:(n0 + 1) * 128], ["wo16:1"]))
        bg_l1 = []
        for f in range(NF):
            bg_l1.append(col_piece(wfi_d[1, :, f * 128:(f + 1) * 128], wfi16[1, f, :, :, 0:128], ["wfi16:1"]))
            bg_l1.append(col_piece(wfi_d[1, :, DFF + f * 128:DFF + (f + 1) * 128], wfi16[1, f, :, :, 128:256], ["wfi16:1"]))
            bg_l1.append((wfo_d[1, f * 128:(f + 1) * 128, :], wfo16[1, f], ["wfo16:1"]))
        bgs = {"st": None}

        def load_mod(l, b):
            for j, c0 in enumerate((0, 8, 24, 32)):
                P.op("dve", lambda e, j=j, c0=c0: e.tensor_copy(out=modT[:, j, :], in_=modall[:, l, c0:c0 + 8, b]),
                     reads=["modall"], writes=["modT%d" % j])

        def transposes(sub):
            jsh, jsc = (0, 1) if sub == 0 else (2, 3)
            for tg in range(4):
                for c in range(8):
                    pb, pbn = bank()
                    for j in range(4):
                        ti = tg * 4 + j
                        P.op("pe", lambda e, pb=pb, j=j, ti=ti, c=c: e.transpose(
                            pb[:, j * 128:(j + 1) * 128], X[:, ti, c * 128:(c + 1) * 128], ident[:]),
                            reads=["X:%d" % ti, "ident"], writes=[pbn], inc=(j == 3))
                    rd = [pbn, "modT%d" % jsh, "modT%d" % jsc]
                    wr = ["hT:%d:%d" % (c, tg)]
                    if ctr["ev"] % 2 == 0:
                        P.op("act", lambda e, pb=pb, c=c, tg=tg: e.activation(
                            out=hT[:, c, tg * 512:(tg + 1) * 512], in_=pb[:], func=AF.Identity,
                            bias=modT[:, jsh, c:c + 1], scale=modT[:, jsc, c:c + 1]), rd, wr)
                    else:
                        P.op("dve", lambda e, pb=pb, c=c, tg=tg: e.tensor_scalar(
                            out=hT[:, c, tg * 512:(tg + 1) * 512], in0=pb[:], scalar1=modT[:, jsc, c:c + 1],
                            scalar2=modT[:, jsh, c:c + 1], op0=ALU.mult, op1=ALU.add), rd, wr)
                    ctr["ev"] += 1

        def hT_reads(tgs):
            return ["hT:%d:%d" % (c, tg) for c in range(8) for tg in tgs]

        def ln_residual(ti, zts, LNG, LNB, psrc):
            p = ti % 2
            zt = zts[p]
            zn = ["zt%d_%d" % (p, hf) for hf in range(2)]
            for hf in range(2):
                pb, pbn = psrc[hf]
                sl = slice(hf * 512, (hf + 1) * 512)
                P.op("dve", lambda e, pb=pb, sl=sl: e.scalar_tensor_tensor(out=zt[:, sl], in0=X[:, ti, sl], scalar=ALPHA, in1=pb[:],
                                                                            op0=ALU.mult, op1=ALU.add),
                     reads=["X:%d" % ti, pbn], writes=[zn[hf]])
                P.op("dve", lambda e, sl=sl, hf=hf: e.bn_stats(out=stt[:, p, hf, :], in_=zt[:, sl]), reads=[zn[hf]], writes=["stt%d_%d" % (p, hf)])
            P.op("dve", lambda e: e.bn_aggr(out=mv[:, p, 0:2], in_=stt[:, p].rearrange("p a b -> p (a b)")),
                 reads=["stt%d_0" % p, "stt%d_1" % p], writes=["mv%d" % p])
            P.op("act", lambda e: e.activation(out=mv[:, p, 2:3], in_=mv[:, p, 1:2], func=AF.Ln, bias=epsc[:], scale=1.0),
                 reads=["mv%d" % p, "epsc"], writes=["mvb%d" % p])
            P.op("act", lambda e: e.activation(out=mv[:, p, 3:4], in_=mv[:, p, 2:3], func=AF.Exp, scale=-0.5),
                 reads=["mvb%d" % p], writes=["mvc%d" % p])
            P.op("dve", lambda e: e.tensor_scalar(out=mv[:, p, 4:5], in0=mv[:, p, 0:1], scalar1=-1.0, scalar2=mv[:, p, 3:4],
                                                  op0=ALU.mult, op1=ALU.mult), reads=["mv%d" % p, "mvc%d" % p], writes=["mvd%d" % p])
            P.op("act", lambda e: e.activation(out=zt[:], in_=zt[:], func=AF.Identity, bias=mv[:, p, 4:5], scale=mv[:, p, 3:4]),
                 reads=zn + ["mvc%d" % p, "mvd%d" % p], writes=zn)
            P.op("pool", lambda e: e.tensor_tensor(out=zt[:], in0=zt[:], in1=LNG[:], op=ALU.mult),
                 reads=zn + ["LNG"], writes=zn)

            def stage_b():
                P.op("dve", lambda e: e.tensor_tensor(out=X[:, ti, :], in0=zt[:], in1=LNB[:], op=ALU.add),
                     reads=zn + ["LNB"], writes=["X:%d" % ti])
            return stage_b

        def load_ln(l, sub, b, LNG, LNB, GB):
            off = 2048 if sub == 0 else 5120
            P.dma("sp", GB[:], ada_s[l, b, off:off + 1024].partition_broadcast(128), writes=["GB"])
            P.op("dve", lambda e: e.tensor_scalar(out=GB[:], in0=GB[:], scalar1=1.0, scalar2=None, op0=ALU.add),
                 reads=["GB"], writes=["GB"])
            P.dma("sp", LNG[:], lng_d[l, sub].partition_broadcast(128), writes=["LNG"])
            P.dma("sp", LNB[:], lnb_d[l, sub].partition_broadcast(128), writes=["LNB"])

        def mixer(l, b):
            with ExitStack() as ms:
                mixT = sb(ms, "mixT", [128, 8, T], BF16)
                with ExitStack() as gs:
                    QT = sb(gs, "QT", [128, 2, T], BF16)
                    KT = sb(gs, "KT", [128, 2, T], BF16)
                    Vaug = sb(gs, "Vaug", [128, NT, 2, 128], BF16)
                    wgr = [sb(gs, "wgr%d" % i, [128, 3, 8, 128], BF16) for i in range(2)]
                    PTs = [sb(gs, "PT%d" % i, [128, 512], BF16) for i in range(4)]
                    Eg = [sb(gs, "Eg%d" % i, [128, 10, 128], BF16) for i in range(2)]
                    rec = sb(gs, "rec", [128, 512], F32)
                    rec2 = sb(gs, "rec2", [128, 512], F32)
                    gate = sb(gs, "gate", [128, NT, 8], F32)
                    top8 = sb(gs, "top8", [128, NT, 8], F32)
                    sel = sb(gs, "sel", [128, NT, 8], F32)
                    negpad = sb(gs, "negpad", [128, NT, 72], BF16)
                    gts = [(gate, top8, sel, negpad)]
                    if l == 1:
                        gts.append((sb(gs, "gate2", [128, NT, 8], F32), sb(gs, "top82", [128, NT, 8], F32),
                                    sb(gs, "sel2", [128, NT, 8], F32), sb(gs, "negpad2", [128, NT, 72], BF16)))
                        P.op("pool", lambda e: e.memset(gts[1][3][:], 0.0), writes=["negpad1"])
                    ksum = sb(gs, "ksum", [128, 2, 8], F32)
                    kmb = sb(gs, "kmb", [128, 2, 8], BF16)
                    zf = sb(gs, "zf", [128, NT, 8], F32)
                    cwt = sb(gs, "cwt", [128, 2, NT, 8], F32)
                    offn = sb(gs, "offn", [128, NT + 1, 8], F32)
                    ncum = sb(gs, "ncum", [128, NT, 8], F32)
                    bfox = sb(gs, "bfox", [128, 4, NT, 8], F32)
                    P.op("pool", lambda e: e.memset(Vaug[:, :, :, 64:128], 1.0), writes=["Vaug"])
                    P.op("pool", lambda e: e.memset(negpad[:], 0.0), writes=["negpad0"])

                    def pt_next():
                        i = ctr["pt"] % 4
                        ctr["pt"] += 1
                        return PTs[i], "PT%d" % i

                    SB = [0, 1, 2]
                    OB = [3, 4, 5, 6]

                    def load_group_w(gi, slices):
                        w = wgr[gi % 2]
                        wn = "wgr%d" % (gi % 2)
                        for j, s in enumerate(slices):
                            P.dma("sp", w[:, j], wg16[l, s], reads=["wg16:%d:%d" % (l, s)], writes=[wn + ":%d" % j])
                        return w, wn

                    def proj_T(dst_fn, w, wn, j):
                        for tg in range(4):
                            pb, pbn = bank()
                            for k in range(8):
                                mm(pb[:], w[:, j, k, :], hT[:, k, tg * 512:(tg + 1) * 512], k == 0, k == 7,
                                   hT_reads([tg]) + [wn + ":%d" % j], [pbn])
                            dst_fn(tg, pb, pbn)

                    def proj_V(w, wn, tok_ap_fn, pair):
                        for tg in range(4):
                            pb, pbn = bank()
                            for j in range(4):
                                slot = tg * 4 + j
                                sl, tgs = tok_ap_fn(slot)
                                for k in range(8):
                                    mm(pb[:, j * 128:(j + 1) * 128], hT[:, k, sl], w[:, 2, k, :], k == 0, k == 7,
                                       hT_reads(tgs) + [wn + ":2"], [pbn], inc=(k == 7 and j == 3))
                            if pair:
                                evac(Vaug[:, tg * 4:(tg + 1) * 4, :, 0:64],
                                     pb[:].rearrange("p (t h d) -> p t h d", t=4, h=2), [pbn], ["Vaug"])
                            else:
                                evac(Vaug[:, tg * 4:(tg + 1) * 4, 0, :],
                                     pb[:].rearrange("p (t d) -> p t d", t=4), [pbn], ["Vaug"])

                    contig = lambda slot: (slice(slot * 128, (slot + 1) * 128), [slot // 4])

                    def load_E(gi, idxs):
                        e_ = Eg[gi % 2]
                        en = "Eg%d" % (gi % 2)
                        for j, ix in enumerate(idxs):
                            P.dma("sp", e_[:, j, :], E_d[ix], reads=["E_d"], writes=[en])
                        return e_, en

                    def dense_attn(units_for_chunk, finish_chunk, LOOK=2, SBK=(0, 1, 2), PAIR=False, MENG="pool"):
                        for c in range(4):
                            units = units_for_chunk(c)
                            SBK = list(SBK)

                            def issue_S(u):
                                pb, pbn = bank(SBK)
                                qlo = u["qlo"]
                                mm(pb[:, qlo:512], u["kT"], u["qT"], True, True, u["sreads"], [pbn])
                                u["sb"], u["sbn"] = pb, pbn

                            last_idx = {}
                            for i, u in enumerate(units):
                                for (_l, _r, _ob, obn) in u["pv"]:
                                    last_idx[obn] = i
                            for i in range(min(LOOK, len(units))):
                                issue_S(units[i])
                            for i, u in enumerate(units):
                                if PAIR:
                                    if i % 2 == 0:
                                        for k2 in (i + LOOK, i + LOOK + 1):
                                            if k2 < len(units):
                                                issue_S(units[k2])
                                elif i + LOOK < len(units):
                                    issue_S(units[i + LOOK])
                                if bgs["st"] is not None:
                                    bgs["st"].tick()
                                qlo = u["qlo"]
                                pt, ptn = pt_next()
                                bias = u["bias"]
                                if bias is None:
                                    P.op("act", lambda e, pt=pt, u=u, qlo=qlo: e.activation(
                                        out=pt[:, qlo:512], in_=u["sb"][:, qlo:512], func=AF.Exp, scale=0.125),
                                        reads=[u["sbn"]], writes=[ptn])
                                else:
                                    P.op("act", lambda e, pt=pt, u=u, qlo=qlo, bias=bias: e.activation(
                                        out=pt[:, qlo:512], in_=u["sb"][:, qlo:512], func=AF.Exp, scale=0.125, bias=bias),
                                        reads=[u["sbn"], "bfox"], writes=[ptn])
                                for (ii, et, en) in u["masks"]:
                                    P.op(MENG, lambda e, pt=pt, ii=ii, et=et: e.tensor_tensor(
                                        out=pt[:, ii * 128:(ii + 1) * 128], in0=pt[:, ii * 128:(ii + 1) * 128], in1=et, op=ALU.mult),
                                        reads=[ptn, en], writes=[ptn])
                                for (lhsT, lreads, ob, obn) in u["pv"]:
                                    lastu = (last_idx[obn] == i)
                                    if u["diag"] and qlo > 0:
                                        jj = qlo // 128
                                        for ii in range(jj, 4):
                                            mm(ob[:, ii * 128:(ii + 1) * 128], lhsT, pt[:, ii * 128:(ii + 1) * 128],
                                               u["first"], lastu and ii == 3, [ptn] + lreads, [obn], inc=(ii == 3))
                                    else:
                                        mm(ob[:], lhsT, pt[:], u["first"], lastu, [ptn] + lreads, [obn], inc=True)
                            finish_chunk(c)

                    gi = 0
                    if l == 0:
                        with ExitStack() as at:
                            r0 = sb(at, "r0", [128, 512], F32)
                            r1 = sb(at, "r1", [128, 512], F32)
                            oo = sb(at, "oo", [128, 512], F32)
                            t1 = sb(at, "t1", [128, 512], F32)
                            sq = sb(at, "sq", [128, 512], F32)
                            if b == 0:
                                bgs["st"] = BgCast(mixT, bg_l0, 3)
                            for h in range(4):
                                w, wn = load_group_w(gi, [h, 4 + h, 8 + h])
                                e_, en = load_E(gi, [h, 12 + h])
                                gi += 1
                                proj_T(lambda tg, pb, pbn: evac(QT[:, 0, tg * 512:(tg + 1) * 512], pb[:], [pbn], ["QT:%d" % tg]), w, wn, 0)
                                proj_T(lambda tg, pb, pbn: evac(KT[:, 0, tg * 512:(tg + 1) * 512], pb[:], [pbn], ["KT:%d" % tg]), w, wn, 1)
                                proj_V(w, wn, contig, False)
                                obs = [(banks[3], "ps3"), (banks[4], "ps4"), (banks[5], "ps5"), (banks[6], "ps6")]

                                def units_A(c, e_=e_, en=en):
                                    us = []
                                    for j in range(4 * c + 4):
                                        for m in range(2):
                                            jj = j - 4 * c
                                            qlo = max(jj, 0) * 128
                                            masks = []
                                            for ii in range(4):
                                                i = 4 * c + ii
                                                if j == i:
                                                    masks.append((ii, e_[:, 0, :], en))
                                                elif j == i - 1:
                                                    masks.append((ii, e_[:, 1, :], en))
                                            rb = m * 64
                                            us.append(dict(
                                                qlo=qlo, diag=(jj >= 0), first=(j == 0),
                                                kT=KT[rb:rb + 64, 0, j * 128:(j + 1) * 128],
                                                qT=QT[rb:rb + 64, 0, c * 512 + qlo:(c + 1) * 512],
                                                sreads=["KT:%d" % (j // 4), "QT:%d" % c], bias=None, masks=masks,
                                                pv=[(Vaug[:, j, 0, :], ["Vaug"], obs[2 * m][0], obs[2 * m][1]),
                                                    (ones16[:], ["ones16"], obs[2 * m + 1][0], obs[2 * m + 1][1])]))
                                    return us

                                def finish_A(c, h=h):
                                    cs = slice(c * 512, (c + 1) * 512)
                                    P.op("act", lambda e: e.activation(out=r0[:], in_=banks[4][:], func=AF.Ln), reads=["ps4"], writes=["r0"])
                                    P.op("dve", lambda e: e.tensor_copy(out=oo[:], in_=banks[3][:]), reads=["ps3"], writes=["oo"])
                                    P.op("act", lambda e: e.activation(out=r1[:], in_=banks[6][:], func=AF.Ln), reads=["ps6"], writes=["r1"])
                                    P.op("dve", lambda e: e.tensor_copy(out=t1[:], in_=banks[5][:]), reads=["ps5"], writes=["t1"])
                                    P.op("act", lambda e: e.activation(out=r0[:], in_=r0[:], func=AF.Exp, scale=-1.0), reads=["r0"], writes=["r0"])
                                    P.op("act", lambda e: e.activation(out=r1[:], in_=r1[:], func=AF.Exp, scale=-1.0), reads=["r1"], writes=["r1"])
                                    P.op("dve", lambda e: e.tensor_tensor(out=oo[:], in0=oo[:], in1=r0[:], op=ALU.mult),
                                         reads=["oo", "r0"], writes=["oo"])
                                    P.op("dve", lambda e: e.tensor_tensor(out=t1[:], in0=t1[:], in1=r1[:], op=ALU.mult),
                                         reads=["t1", "r1"], writes=["t1"])
                                    P.op("dve", lambda e: e.scalar_tensor_tensor(out=oo[:], in0=t1[:], scalar=neglam[:, 0:1], in1=oo[:],
                                                                                  op0=ALU.mult, op1=ALU.add),
                                         reads=["t1", "oo", "neglam"], writes=["oo"])
                                    P.op("act", lambda e: e.activation(out=sq[:], in_=oo[:], func=AF.Square), reads=["oo"], writes=["sq"])
                                    pm, pmn = bank([0, 1, 2, 7])
                                    mm(pm[:], ones32[:], sq[:], True, True, ["sq", "ones32"], [pmn])
                                    P.op("act", lambda e: e.activation(out=sq[:], in_=pm[:], func=AF.Ln, bias=epsc[:], scale=1.0 / 128.0),
                                         reads=[pmn, "epsc"], writes=["sq"])
                                    P.op("act", lambda e: e.activation(out=sq[:], in_=sq[:], func=AF.Exp, scale=-0.5), reads=["sq"], writes=["sq"])
                                    P.op("dve", lambda e: e.scalar_tensor_tensor(out=mixT[:, h, cs], in0=oo[:], scalar=subg[:, 0:1], in1=sq[:],
                                                                                  op0=ALU.mult, op1=ALU.mult),
                                         reads=["oo", "sq", "subg"], writes=["mixT:%d" % h])

                                dense_attn(units_A, finish_A, LOOK=2, SBK=(0, 1, 2, 7), PAIR=True, MENG="dve")
                            if bgs["st"] is not None:
                                bgs["st"].flush()
                                bgs["st"] = None
                            P.barrier()
                        P.op("pool", lambda e: e.memset(Vaug[:, :, :, 64:128], 1.0), writes=["Vaug"])
                        bt_ = ExitStack()
                        accs = [sb(bt_, "acc%d" % i, [128, T], F32) for i in range(2)]

                        for j in range(4):
                            w, wn = load_group_w(gi, [12 + j, 16 + j, 20 + j])
                            eidx = []
                            for s in range(2):
                                hb = 2 * j + s
                                eidx += [4 + hb, 20 + hb, 28 + hb, 36 + hb, 44 + hb]
                            e_, en = load_E(gi, eidx)
                            gi += 1
                            proj_T(lambda tg, pb, pbn: evac(QT[:, 0, tg * 512:(tg + 1) * 512], pb[:], [pbn], ["QT:%d" % tg]), w, wn, 0)
                            proj_T(lambda tg, pb, pbn: evac(KT[:, 0, tg * 512:(tg + 1) * 512], pb[:], [pbn], ["KT:%d" % tg]), w, wn, 1)
                            QA = ["QT:%d" % i for i in range(4)]
                            KA = ["KT:%d" % i for i in range(4)]

                            def pth_next():
                                i = ctr.get("pth", 0) % 8
                                ctr["pth"] = ctr.get("pth", 0) + 1
                                return PTs[i // 2][:, (i % 2) * 256:(i % 2) * 256 + 256], "PTh%d" % i

                            def window_pattern(s, nset, kset_ap, qset_ap, e2, vslot, obank_of, flush):
                                rb = s * 64
                                pts = {}

                                def issue(i):
                                    ncol = 256 if i + 1 < nset else 128
                                    pb, pbn = bank(SB)
                                    mm(pb[:, 0:ncol], kset_ap(rb, i), qset_ap(rb, i, ncol), True, True, QA + KA, [pbn])
                                    pt, ptn = pth_next()
                                    P.op("act", lambda e: e.activation(out=pt[:, 0:ncol], in_=pb[:, 0:ncol], func=AF.Exp, scale=0.125),
                                         reads=[pbn], writes=[ptn])
                                    P.op("dve", lambda e: e.tensor_tensor(out=pt[:, 0:ncol], in0=pt[:, 0:ncol], in1=e2[:, 0:ncol], op=ALU.mult),
                                         reads=[ptn, en], writes=[ptn])
                                    pts[i] = (pt, ptn)

                                issue(0)
                                if nset > 1:
                                    issue(1)
                                yield
                                for i in range(nset):
                                    if i + 2 < nset:
                                        issue(i + 2)
                                    ob, obn, col = obank_of(i)
                                    if i > 0:
                                        pt, ptn = pts[i - 1]
                                        mm(ob[:, col:col + 128], Vaug[:, vslot(i - 1), s, :], pt[:, 128:256], True, False,
                                           [ptn, "Vaug"], [obn], inc=False)
                                    pt, ptn = pts[i]
                                    mm(ob[:, col:col + 128], Vaug[:, vslot(i), s, :], pt[:, 0:128], i == 0, True,
                                       [ptn, "Vaug"], [obn], inc=True)
                                    flush(i, ob, obn)
                                    yield

                            SB = [0, 1, 2, 7]
                            proj_V(w, wn, contig, True)

                            def pat1(s):
                                acc, an = accs[s], "acc%d" % s
                                e2 = e_[:, s * 5:s * 5 + 2, :].rearrange("p a q -> p (a q)")
                                cur = {}

                                def ob1(i):
                                    if i % 4 == 0:
                                        cur["b"] = bank(OB)
                                    return cur["b"][0], cur["b"][1], (i % 4) * 128

                                def fl1(i, ob, obn):
                                    if i % 4 == 3:
                                        n = i // 4
                                        evac(acc[:, n * 512:(n + 1) * 512], ob[:], [obn], [an])

                                return window_pattern(s, 16,
                                                      lambda rb, i: KT[rb:rb + 64, 0, i * 128:(i + 1) * 128],
                                                      lambda rb, i, ncol: QT[rb:rb + 64, 0, i * 128:i * 128 + ncol],
                                                      e2, lambda i: i, ob1, fl1)

                            run_interleaved([pat1(0), pat1(1)])
                            proj_V(w, wn, lambda slot: (slice(512 * (slot % 4) + slot // 4, 512 * (slot % 4) + 512, 4), [slot % 4]), True)

                            def pat2(s, r):
                                acc, an = accs[s], "acc%d" % s
                                e2 = e_[:, s * 5 + 2:s * 5 + 4, :].rearrange("p a q -> p (a q)")
                                cur = {"b": bank(OB)}

                                def fl2(i, ob, obn):
                                    if i == 3:
                                        av = acc[:, :].rearrange("p (n u f) -> p n u f", n=4, u=128, f=4)[:, :, :, r]
                                        P.op("dve", lambda e: e.tensor_tensor(out=av, in0=av, in1=ob[:].rearrange("p (n u) -> p n u", n=4),
                                                                              op=ALU.add), reads=[obn, an], writes=[an])

                                return window_pattern(s, 4,
                                                      lambda rb, n: KT[rb:rb + 64, 0, 512 * n + r:512 * n + 512:4],
                                                      lambda rb, n, ncol: QT[rb:rb + 64, 0, 512 * n + r:512 * n + 4 * ncol:4],
                                                      e2, lambda n: r * 4 + n,
                                                      lambda n: (cur["b"][0], cur["b"][1], n * 128), fl2)

                            for r in range(4):
                                run_interleaved([pat2(0, r), pat2(1, r)])
                            proj_V(w, wn, lambda slot: (slice(slot, T, 16), [0, 1, 2, 3]), True)
                            p3u = [(r16, s) for r16 in range(16) for s in range(2)]
                            ob3 = {}
                            pts3 = {}
                            L3 = 4

                            def issue3(idx):
                                r16, s = p3u[idx]
                                rb = s * 64
                                e3 = e_[:, s * 5 + 4, :]
                                pb, pbn = bank(SB)
                                mm(pb[:, 0:128], KT[rb:rb + 64, 0, r16:T:16], QT[rb:rb + 64, 0, r16:T:16], True, True, QA + KA, [pbn])
                                pt, ptn = pth_next()
                                P.op("act", lambda e: e.activation(out=pt[:, 0:128], in_=pb[:, 0:128], func=AF.Exp, scale=0.125),
                                     reads=[pbn], writes=[ptn])
                                P.op("dve", lambda e: e.tensor_tensor(out=pt[:, 0:128], in0=pt[:, 0:128], in1=e3, op=ALU.mult),
                                     reads=[ptn, en], writes=[ptn])
                                pts3[idx] = (pt, ptn)

                            for idx in range(min(L3, len(p3u))):
                                issue3(idx)
                            for idx, (r16, s) in enumerate(p3u):
                                if idx + L3 < len(p3u):
                                    issue3(idx + L3)
                                rr = r16 % 4
                                if rr == 0:
                                    ob3[s] = bank(OB)
                                ob, obn = ob3[s]
                                pt, ptn = pts3.pop(idx)
                                mm(ob[:, rr * 128:(rr + 1) * 128], Vaug[:, r16, s, :], pt[:, 0:128], True, True, [ptn, "Vaug"], [obn],
                                   inc=True)
                                if rr == 3:
                                    r0_ = r16 - 3
                                    acc, an = accs[s], "acc%d" % s
                                    av = acc[:, :].rearrange("p (u f) -> p f u", f=16)[:, r0_:r0_ + 4, :]
                                    P.op("dve", lambda e: e.tensor_tensor(out=av, in0=av, in1=ob[:].rearrange("p (f u) -> p f u", f=4),
                                                                          op=ALU.add), reads=[obn, an], writes=[an])
                            for s in range(2):
                                acc, an = accs[s], "acc%d" % s
                                for c in range(4):
                                    cs = slice(c * 512, (c + 1) * 512)
                                    rc, rcn = (rec, "rec") if c % 2 == 0 else (rec2, "rec2")
                                    P.op("act", lambda e, acc=acc, cs=cs, rc=rc: e.activation(out=rc[0:64, :], in_=acc[64:128, cs], func=AF.Ln),
                                         reads=[an, rcn], writes=[rcn])
                                    P.op("act", lambda e, rc=rc: e.activation(out=rc[0:64, :], in_=rc[0:64, :], func=AF.Exp, scale=-1.0),
                                         reads=[rcn], writes=[rcn])
                                    P.op("pool", lambda e, acc=acc, cs=cs, s=s, j=j, rc=rc: e.tensor_tensor(
                                        out=mixT[s * 64:(s + 1) * 64, 4 + j, cs], in0=acc[0:64, cs], in1=rc[0:64, :], op=ALU.mult),
                                        reads=[an, rcn], writes=["mixT:%d" % (4 + j)])
                        P.barrier()
                        bt_.close()
                    else:
                        def finish_pair(s, mc):
                            def fin(c, s=s, mc=mc):
                                ob, obn = cur_o["b%d" % s]
                                cs = slice(c * 512, (c + 1) * 512)
                                rc, rcn = (rec, "rec") if s == 0 else (rec2, "rec2")
                                P.op("act", lambda e: e.activation(out=rc[0:64, :], in_=ob[64:128, :], func=AF.Ln), reads=[obn, rcn], writes=[rcn])
                                P.op("act", lambda e: e.activation(out=rc[0:64, :], in_=rc[0:64, :], func=AF.Exp, scale=-1.0), reads=[rcn], writes=[rcn])
                                P.op("dve", lambda e: e.tensor_tensor(out=mixT[s * 64:(s + 1) * 64, mc, cs], in0=ob[0:64, :], in1=rc[0:64, :],
                                                                      op=ALU.mult), reads=[obn, rcn], writes=["mixT:%d" % mc])
                            return fin

                        cur_o = {}

                        def merge_units(ufs):
                            def mu(c):
                                lists = [uf(c) for uf in ufs]
                                out = []
                                for i in range(max(len(x) for x in lists)):
                                    for x in lists:
                                        if i < len(x):
                                            out.append(x[i])
                                return out
                            return mu

                        def merged_finish(mc):
                            def mf(c):
                                for s in range(2):
                                    finish_pair(s, mc)(c)
                            return mf

                        if b == 0:
                            bgs["st"] = BgCast(mixT, bg_l1, 4)
                        for j in range(4):
                            w, wn = load_group_w(gi, [j, 4 + j, 8 + j])
                            e_, en = load_E(gi, [2 * j, 12 + 2 * j, 2 * j + 1, 12 + 2 * j + 1])
                            gi += 1
                            for s in range(2):
                                P.dma("sp", KT[64:72, s, :], kaug16, reads=["KTaug%d" % s], writes=["KTaug%d" % s])

                            def dq(tg, pb, pbn):
                                for s in range(2):
                                    evac(QT[0:64, s, tg * 512:(tg + 1) * 512], pb[s * 64:(s + 1) * 64, :], [pbn], ["QT%d:%d" % (s, tg)])

                            def dk(tg, pb, pbn):
                                for s in range(2):
                                    evac(KT[0:64, s, tg * 512:(tg + 1) * 512], pb[s * 64:(s + 1) * 64, :], [pbn], ["KT%d:%d" % (s, tg)])

                            proj_T(dq, w, wn, 0)
                            proj_T(dk, w, wn, 1)
                            proj_V(w, wn, contig, True)
                            ufs = []
                            ggens = []
                            for s in range(2):
                                QAs = ["QT%d:%d" % (s, i) for i in range(4)]
                                KAs = ["KT%d:%d" % (s, i) for i in range(4)]
                                def gate_gen(s=s, QAs=QAs, KAs=KAs):
                                    gate_, top8_, sel_, negpad_ = gts[s]
                                    gn, tn, sn_, nn = "gate%d" % s, "top8%d" % s, "sel%d" % s, "negpad%d" % s
                                    P.op("dve", lambda e: e.tensor_reduce(out=ksum[0:64, s, :],
                                                                          in_=KT[0:64, s, :].rearrange("p (n t) -> p n t", t=256),
                                                                          axis=AX.X, op=ALU.add), reads=KAs, writes=["ksum%d" % s])
                                    P.op("dve", lambda e: e.tensor_copy(out=kmb[0:64, s, :], in_=ksum[0:64, s, :]),
                                         reads=["ksum%d" % s], writes=["kmb%d" % s])
                                    yield
                                    pg, pgn = bank()
                                    for ti in range(NT):
                                        mm(pg[:, ti * 8:(ti + 1) * 8], QT[0:64, s, ti * 128:(ti + 1) * 128], kmb[0:64, s, :], True, True,
                                           QAs + ["kmb%d" % s], [pgn], inc=(ti == NT - 1))
                                    yield
                                    P.op("dve", lambda e: e.tensor_tensor(out=gate_[:], in0=pg[:, 0:128].rearrange("p (t n) -> p t n", n=8),
                                                                          in1=gmask[:], op=ALU.add), reads=[pgn, "gmask"], writes=[gn])
                                    for ti in range(NT):
                                        P.op("dve", lambda e, ti=ti: e.max(out=top8_[:, ti, :], in_=gate_[:, ti, :]), reads=[gn], writes=[tn])
                                    P.op("dve", lambda e: e.tensor_tensor(out=sel_[:], in0=gate_[:], in1=top8_[:, :, 3:4].to_broadcast([128, NT, 8]),
                                                                          op=ALU.is_ge), reads=[gn, tn], writes=[sn_])
                                    P.op("dve", lambda e: e.tensor_scalar(out=negpad_[:, :, 64:72], in0=sel_[:], scalar1=1.0, scalar2=-NEG,
                                                                          op0=ALU.subtract, op1=ALU.mult), reads=[sn_], writes=[nn])
                                    yield
                                    for tg in range(4):
                                        pa, pan = bank()
                                        for jj in range(4):
                                            ti = tg * 4 + jj
                                            mm(pa[0:72, jj * 128:(jj + 1) * 128], negpad_[:, ti, :], ident16[:], True, True,
                                               [nn, "ident16"], [pan], inc=(jj == 3))
                                        evac(QT[64:72, s, tg * 512:(tg + 1) * 512], pa[64:72, :], [pan], ["QTaug%d:%d" % (s, tg)])

                                ggens.append(gate_gen())

                                def units_C(c, s=s, e_=e_, en=en, QAs=QAs, KAs=KAs):
                                    if cur_o.get("c%d" % s) != (s, c, "C", j):
                                        cur_o["b%d" % s] = bank(OB)
                                        cur_o["c%d" % s] = (s, c, "C", j)
                                    ob, obn = cur_o["b%d" % s]
                                    us = []
                                    for kt in range(4 * c + 4):
                                        jj = kt - 4 * c
                                        qlo = max(jj, 0) * 128
                                        masks = []
                                        for ii in range(4):
                                            i = 4 * c + ii
                                            if kt == i:
                                                masks.append((ii, e_[:, 2 * s, :], en))
                                            elif kt == i - 1:
                                                masks.append((ii, e_[:, 2 * s + 1, :], en))
                                        us.append(dict(
                                            qlo=qlo, diag=(jj >= 0), first=(kt == 0),
                                            kT=KT[0:72, s, kt * 128:(kt + 1) * 128],
                                            qT=QT[0:72, s, c * 512 + qlo:(c + 1) * 512],
                                            sreads=["KT%d:%d" % (s, kt // 4), "KTaug%d" % s, "QT%d:%d" % (s, c), "QTaug%d:%d" % (s, c)],
                                            bias=None, masks=masks,
                                            pv=[(Vaug[:, kt, s, :], ["Vaug"], ob, obn)]))
                                    return us

                                ufs.append(units_C)
                            run_interleaved(ggens)
                            dense_attn(merge_units(ufs), merged_finish(j), LOOK=3, SBK=(0, 1, 2, 7), MENG="dve")

                        if bgs["st"] is not None:
                            bgs["st"].flush()
                            bgs["st"] = None
                            P.barrier()
                        pf, pfn = bank()
                        for ti in range(NT):
                            for k in range(8):
                                mm(pf[:, ti * 8:(ti + 1) * 8], hT[:, k, ti * 128:(ti + 1) * 128], wfs[:, k, :], k == 0, k == 7,
                                   hT_reads([ti // 4]) + ["wfs"], [pfn], inc=(k == 7 and ti == NT - 1))
                        P.op("dve", lambda e: e.tensor_tensor(out=zf[:], in0=pf[:, 0:128].rearrange("p (t n) -> p t n", n=8),
                                                              in1=fbb[:, :].unsqueeze(1).to_broadcast([128, NT, 8]), op=ALU.add),
                             reads=[pfn, "fbb"], writes=["zf"])
                        P.op("act", lambda e: e.activation(out=zf[:], in_=zf[:], func=AF.Exp, scale=-1.0), reads=["zf"], writes=["zf"])
                        P.op("act", lambda e: e.activation(out=zf[:], in_=zf[:], func=AF.Ln, bias=1.0, scale=1.0), reads=["zf"], writes=["zf"])
                        pc, pcn = bank()
                        zf2 = zf[:].rearrange("p t n -> p (t n)")
                        mm(pc[:, 0:128], tri[:], zf2, True, True, ["tri", "zf"], [pcn], inc=False)
                        mm(pc[:, 128:256], ones32[:], zf2, True, True, ["ones32", "zf"], [pcn], inc=True)
                        P.op("dve", lambda e: e.tensor_copy(out=cwt[:].rearrange("p a t n -> p (a t n)"), in_=pc[:, 0:256]), reads=[pcn], writes=["cwt"])
                        P.op("dve", lambda e: e.memset(offn[:, 0, :], 0.0), writes=["offn"])
                        for ti in range(NT):
                            P.op("dve", lambda e, ti=ti: e.tensor_tensor(out=offn[:, ti + 1, :], in0=offn[:, ti, :], in1=cwt[:, 1, ti, :], op=ALU.add),
                                 reads=["offn", "cwt"], writes=["offn"])
                        P.op("dve", lambda e: e.tensor_tensor(out=ncum[:], in0=offn[:, 0:NT, :], in1=cwt[:, 0, :, :], op=ALU.add),
                             reads=["offn", "cwt"], writes=["ncum"])
                        for c in range(4):
                            P.op("dve", lambda e, c=c: e.tensor_tensor(out=bfox[:, c, :, :], in0=ncum[:],
                                                                       in1=offn[:, 4 * c + 2:4 * c + 3, :].to_broadcast([128, NT, 8]),
                                                                       op=ALU.subtract), reads=["ncum", "offn"], writes=["bfox"])

                        for j in range(4):
                            w, wn = load_group_w(gi, [12 + j, 16 + j, 20 + j])
                            e_, en = load_E(gi, [52])
                            gi += 1
                            proj_T(lambda tg, pb, pbn: evac(QT[:, 0, tg * 512:(tg + 1) * 512], pb[:], [pbn],
                                                            ["QT0:%d" % tg, "QTaug0:%d" % tg]), w, wn, 0)
                            proj_T(lambda tg, pb, pbn: evac(KT[:, 0, tg * 512:(tg + 1) * 512], pb[:], [pbn],
                                                            ["KT0:%d" % tg, "KTaug0"]), w, wn, 1)
                            proj_V(w, wn, contig, True)
                            ufs = []
                            for s in range(2):
                                hd = 2 * j + s

                                def units_D(c, s=s, hd=hd, e_=e_, en=en):
                                    if cur_o.get("c%d" % s) != (s, c, "D", j):
                                        cur_o["b%d" % s] = bank(OB)
                                        cur_o["c%d" % s] = (s, c, "D", j)
                                    ob, obn = cur_o["b%d" % s]
                                    us = []
                                    rb = s * 64
                                    for kt in range(4 * c + 4):
                                        jj = kt - 4 * c
                                        qlo = max(jj, 0) * 128
                                        masks = [(jj, e_[:, 0, :], en)] if jj >= 0 else []
                                        us.append(dict(
                                            qlo=qlo, diag=(jj >= 0), first=(kt == 0),
                                            kT=KT[rb:rb + 64, 0, kt * 128:(kt + 1) * 128],
                                            qT=QT[rb:rb + 64, 0, c * 512 + qlo:(c + 1) * 512],
                                            sreads=["KT0:%d" % (kt // 4), "QT0:%d" % c],
                                            bias=bfox[:, c, kt, hd:hd + 1], masks=masks,
                                            pv=[(Vaug[:, kt, s, :], ["Vaug"], ob, obn)]))
                                    return us

                                ufs.append(units_D)
                            dense_attn(merge_units(ufs), merged_finish(4 + j), LOOK=2, SBK=(0, 1, 2, 7), PAIR=True, MENG="dve")
                    P.barrier()
                if stage < 99 and stage == 2 * (2 * b + l):
                    P.dma("sp", dbg["mixT"], mixT[:], reads=["mixT:%d" % i for i in range(8)])
                with ExitStack() as ws:
                    wos = sb(ws, "wos", [128, 8, D], BF16)
                    LNG = sb(ws, "LNG", [128, D], F32)
                    LNB = sb(ws, "LNB", [128, D], F32)
                    GB = sb(ws, "GB", [128, D], F32)
                    zts = [sb(ws, "zt%d" % i, [128, D], F32) for i in range(2)]
                    load_ln(l, 0, b, LNG, LNB, GB)
                    for k in range(8):
                        P.dma("sp", wos[:, k, :], wo16[l, :, k, :], reads=["wo16:%d" % l], writes=["wos:%d" % k])
                        P.op("dve" if k % 2 == 0 else "pool", lambda e, k=k: e.tensor_tensor(out=wos[:, k, :], in0=wos[:, k, :], in1=GB[:], op=ALU.mult),
                             reads=["wos:%d" % k, "GB"], writes=["wos:%d" % k])
                    pend = None
                    for ti in range(NT):
                        psrc = []
                        for hf in range(2):
                            pb, pbn = bank()
                            for k in range(8):
                                mm(pb[:], mixT[:, k, ti * 128:(ti + 1) * 128], wos[:, k, hf * 512:(hf + 1) * 512], k == 0, k == 7,
                                   ["mixT:%d" % k, "wos:%d" % k], [pbn])
                            psrc.append((pb, pbn))
                        fin = ln_residual(ti, zts, LNG, LNB, psrc)
                        if pend is not None:
                            pend()
                        pend = fin
                    pend()
                    P.barrier()

        def ffn(l, b):
            with ExitStack() as fs:
                aT = sb(fs, "aT", [128, NF, 512], BF16)
                wout = sb(fs, "wout", [128, NF, D], BF16)
                wgu = [sb(fs, "wgu%d" % i, [128, 8, 256], BF16) for i in range(3)]
                sg = [sb(fs, "sg%d" % i, [128, 512], F32) for i in range(2)]
                LNG = sb(fs, "LNG", [128, D], F32)
                LNB = sb(fs, "LNB", [128, D], F32)
                GB = sb(fs, "GB", [128, D], F32)
                zts = [sb(fs, "zt%d" % i, [128, D], F32) for i in range(2)]
                load_ln(l, 1, b, LNG, LNB, GB)
                for f in range(NF):
                    P.dma("sp", wout[:, f, :], wfo16[l, f], reads=["wfo16:%d" % l], writes=["wout:%d" % f])
                    P.op("pool", lambda e, f=f: e.tensor_tensor(out=wout[:, f, :], in0=wout[:, f, :], in1=GB[:], op=ALU.mult),
                         reads=["wout:%d" % f, "GB"], writes=["wout:%d" % f])
                it = 0
                for tc in range(4):
                    for f in range(NF):
                        wi = it % 3
                        it += 1
                        P.dma("sp", wgu[wi][:], wfi16[l, f], reads=["wfi16:%d" % l], writes=["wgu%d" % wi])
                        pg, pgn = bank()
                        pu, pun = bank()
                        for k in range(8):
                            mm(pg[:], wgu[wi][:, k, 0:128], hT[:, k, tc * 512:(tc + 1) * 512], k == 0, k == 7, hT_reads([tc]) + ["wgu%d" % wi], [pgn])
                        for k in range(8):
                            mm(pu[:], wgu[wi][:, k, 128:256], hT[:, k, tc * 512:(tc + 1) * 512], k == 0, k == 7, hT_reads([tc]) + ["wgu%d" % wi], [pun])
                        s_ = sg[f % 2]
                        sn = "sg%d" % (f % 2)
                        P.op("act", lambda e, s_=s_, pg=pg: e.activation(out=s_[:], in_=pg[:], func=AF.Silu), reads=[pgn], writes=[sn])
                        P.op("dve", lambda e, s_=s_, pu=pu, f=f: e.tensor_tensor(out=aT[:, f, :], in0=pu[:], in1=s_[:], op=ALU.mult),
                             reads=[pun, sn], writes=["aT:%d" % f])
                    pend = None
                    for jt in range(4):
                        ti = tc * 4 + jt
                        psrc = []
                        for hf in range(2):
                            pb, pbn = bank()
                            for f in range(NF):
                                mm(pb[:], aT[:, f, jt * 128:(jt + 1) * 128], wout[:, f, hf * 512:(hf + 1) * 512], f == 0, f == NF - 1,
                                   ["aT:%d" % f, "wout:%d" % f], [pbn])
                            psrc.append((pb, pbn))
                        fin = ln_residual(ti, zts, LNG, LNB, psrc)
                        if pend is not None:
                            pend()
                        pend = fin
                    pend()
                    pend = None
                P.barrier()

        done = False
        for b in range(NB):
            for tg in range(4):
                P.dma("sp", X[:, tg * 4:(tg + 1) * 4, :], x_d[b, tg * 512:(tg + 1) * 512, :].rearrange("(t p) d -> p t d", p=128),
                      writes=["X:%d" % (tg * 4 + i) for i in range(4)])
            if stage == -1:
                done = True
            for l in range(2):
                if done:
                    break
                load_mod(l, b)
                transposes(0)
                mixer(l, b)
                if stage == 2 * (2 * b + l):
                    done = True
                    break
                transposes(1)
                ffn(l, b)
                if stage == 2 * (2 * b + l) + 1:
                    done = True
                    break
            if done:
                P.dma("sp", dbg["X"].rearrange("(t p) d -> p t d", p=128), X[:], reads=["X:%d" % i for i in range(NT)])
                break
            for tg in range(4):
                P.dma("sp", out_d[b, tg * 512:(tg + 1) * 512, :].rearrange("(t p) d -> p t d", p=128), X[:, tg * 4:(tg + 1) * 4, :],
                      reads=["X:%d" % (tg * 4 + i) for i in range(4)])
        P.barrier()
        build_program.stats = dict(n_ins=dict(P.n_ins), count=dict(P.count), dma=list(P.dma_cnt))
    return nc


def host_inputs(inputs):
    global ECOLS
    rel_bias = np.asarray(inputs["rel_bias"], np.float32)
    tiles, cols = make_bias_tiles(rel_bias)
    ECOLS = cols
    k = np.arange(128)
    tri = (k[:, None] <= k[None, :]).astype(np.float32)
    kaug = np.zeros((8, T), np.float32)
    for n in range(8):
        kaug[n, n * 256:(n + 1) * 256] = 1.0
    shared = {
        "rel_bias": rel_bias,
        "w_ada": np.ascontiguousarray(inputs["w_ada"], np.float32),
        "b_ada": np.ascontiguousarray(inputs["b_ada"], np.float32),
        "ln_g": np.ascontiguousarray(inputs["ln_g"], np.float32),
        "ln_b": np.ascontiguousarray(inputs["ln_b"], np.float32),
        "w_in_ab": np.ascontiguousarray(np.asarray(inputs["w_in_ab"], np.float32)[0]),
        "diff_lambda": np.ascontiguousarray(np.asarray(inputs["diff_lambda"], np.float32).reshape(256)),
        "diff_subln_g": np.ascontiguousarray(np.asarray(inputs["diff_subln_g"], np.float32).reshape(128)),
        "w_in_cd": np.ascontiguousarray(np.asarray(inputs["w_in_cd"], np.float32)[0]),
        "forget_b": np.ascontiguousarray(np.asarray(inputs["forget_b"], np.float32).reshape(8)),
        "w_o": np.ascontiguousarray(inputs["w_o"], np.float32),
        "w_ffn_in": np.ascontiguousarray(inputs["w_ffn_in"], np.float32),
        "w_ffn_out": np.ascontiguousarray(inputs["w_ffn_out"], np.float32),
        "btiles": tiles,
        "ident": np.eye(128, dtype=np.float32),
        "tri": tri,
        "kaug": kaug,
    }
    return shared


def kernel(**inputs):
    shared = host_inputs(inputs)
    x = np.asarray(inputs["x"], np.float32)
    c = np.asarray(inputs["c"], np.float32)
    n = 8
    nc = build_program()
    in_maps = []
    for i in range(n):
        m = dict(shared)
        m["x"] = np.ascontiguousarray(x[NB * i:NB * (i + 1)])
        m["c"] = np.ascontiguousarray(c[NB * i:NB * (i + 1)])
        in_maps.append(m)
    res = run_bass_kernel_spmd(nc, in_maps, core_ids=list(range(n)))
    return np.concatenate([np.asarray(r["out"], np.float32) for r in res.results], axis=0)
```

```python
import math
import numpy as np
from contextlib import ExitStack
import concourse.bass as bass
import concourse.mybir as mybir
from concourse.bass_utils import run_bass_kernel_spmd

F32 = mybir.dt.float32
BF16 = mybir.dt.bfloat16
AF = mybir.ActivationFunctionType
ALU = mybir.AluOpType
AX = mybir.AxisListType

T = 2048
D = 1024
NT = 16
DFF = 2816
NF = 22
NB = 2
ALPHA = 4 ** 0.25
EPS = 1e-5
NEG = -30000.0
N_ETILES = 53

ENGS = ("pe", "act", "dve", "pool", "sp")
N_DMA_SEMS = 24


def run_interleaved(gens):
    live = list(gens)
    while live:
        nxt = []
        for g in live:
            try:
                next(g)
                nxt.append(g)
            except StopIteration:
                pass
        live = nxt


class Prog:
    def __init__(self, nc, es):
        self.nc = nc
        self.eng_obj = {"pe": nc.tensor, "act": nc.scalar, "dve": nc.vector,
                        "pool": nc.gpsimd, "sp": nc.sync}
        self.count = {e: 0 for e in ENGS}
        self.known = {e: {} for e in ENGS}
        self.last_w = {}
        self.readers = {}
        self.dma_rr = 0
        self.dma_cnt = [0] * N_DMA_SEMS
        self.sems = {}
        self.n_ins = {e: 0 for e in ENGS}
        for e in ENGS:
            self.sems[e] = es.enter_context(nc.semaphore("s_" + e))
        for j in range(N_DMA_SEMS):
            self.sems[("d", j)] = es.enter_context(nc.semaphore("s_d%d" % j))

    def _deps(self, reads, writes):
        toks = []
        for r in reads:
            t = self.last_w.get(r)
            if t is not None:
                toks.append(t)
        for w in writes:
            t = self.last_w.get(w)
            if t is not None:
                toks.append(t)
            toks.extend(self.readers.get(w, ()))
        return toks

    def _commit(self, tok, reads, writes):
        for r in reads:
            self.readers.setdefault(r, []).append(tok)
        for w in writes:
            self.last_w[w] = tok
            self.readers[w] = []

    def _waits(self, eng, toks):
        need = {}
        kn = self.known[eng]
        for (k, v) in toks:
            if eng == "pe" and k == "pe":
                continue
            if kn.get(k, 0) >= v:
                continue
            if need.get(k, 0) < v:
                need[k] = v
        for k, v in need.items():
            kn[k] = v
        return list(need.items())

    def _emit(self, eng, waits, fn, inc):
        e = self.eng_obj[eng]
        for k, v in waits:
            e.wait_ge(self.sems[k], v)
        if fn is not None:
            ins = fn(e)
            self.n_ins[eng] += 1
            if inc is not None:
                ins.then_inc(self.sems[inc[0]], inc[1])

    def op(self, eng, fn, reads=(), writes=(), inc=True):
        toks = self._deps(reads, writes)
        waits = self._waits(eng, toks)
        if inc:
            self.count[eng] += 1
            tok = (eng, self.count[eng])
            self._emit(eng, waits, fn, (eng, 1))
        else:
            tok = (eng, self.count[eng] + 1)
            self._emit(eng, waits, fn, None)
        self._commit(tok, reads, writes)
        return tok

    def dma(self, q, out, in_, reads=(), writes=(), slow=False):
        toks = self._deps(reads, writes)
        j = self.dma_rr
        self.dma_rr = (self.dma_rr + 1) % N_DMA_SEMS
        if self.dma_cnt[j] > 0:
            toks.append((("d", j), 16 * self.dma_cnt[j]))
        waits = self._waits(q, toks)
        self.dma_cnt[j] += 1
        tok = (("d", j), 16 * self.dma_cnt[j])
        if slow:
            fn = lambda e: e.dma_start(out=out, in_=in_, allow_slow_non_contiguous=True)
        else:
            fn = lambda e: e.dma_start(out=out, in_=in_)
        self._emit(q, waits, fn, (("d", j), 16))
        self._commit(tok, reads, writes)
        return tok

    def all_tokens(self):
        toks = [(e, self.count[e]) for e in ENGS if self.count[e] > 0]
        toks += [(("d", j), 16 * c) for j, c in enumerate(self.dma_cnt) if c > 0]
        return toks

    def barrier(self, engs=ENGS):
        toks = self.all_tokens()
        for e in engs:
            self._emit(e, self._waits(e, list(toks)), None, None)
        self.last_w = {}
        self.readers = {}


def t5_bucket_np(n):
    n = np.maximum(n, 0)
    nf = np.maximum(n, 1).astype(np.float32)
    large = 16 + (np.log(nf / np.float32(16)) / np.float32(math.log(128 / 16)) * np.float32(16)).astype(np.int32)
    large = np.minimum(large, 31)
    return np.where(n < 16, n, large)


def make_bias_tiles(rel_bias):
    k = np.arange(128)[:, None]
    q = np.arange(128)[None, :]
    tiles = np.full((N_ETILES, 128, 128), NEG, np.float32)
    cols = np.zeros((N_ETILES,), np.int64)

    def fill(idx, dist, valid, col):
        b = t5_bucket_np(np.where(valid, dist, 0))
        tiles[idx] = np.where(valid, rel_bias[b, col], np.float32(NEG))
        cols[idx] = col

    for col in range(12):
        fill(col, q - k, q >= k, col)
    for h in range(8):
        fill(12 + h, q - k + 128, np.ones((128, 128), bool), h)
    for hb in range(8):
        col = 4 + hb
        fill(20 + hb, q - k + 128, q <= k, col)
        fill(28 + hb, 4 * (q - k), q >= k, col)
        fill(36 + hb, 4 * (q - k + 128), q <= k, col)
        fill(44 + hb, 16 * (q - k), q >= k, col)
    tiles[52] = np.where(q >= k, np.float32(0.0), np.float32(NEG))
    cols[52] = 12
    return tiles, cols


ECOLS = None


def build_program(stage=99):
    nc = bass.Bass("TRN2", target_bir_lowering=False)
    dram = lambda name, shape, dt, kind: nc.dram_tensor(name, shape, dt, kind=kind).ap()
    x_d = dram("x", [NB, T, D], F32, "ExternalInput")
    c_d = dram("c", [NB, D], F32, "ExternalInput")
    relb_d = dram("rel_bias", [32, 12], F32, "ExternalInput")
    wada_d = dram("w_ada", [2, D, 6 * D], F32, "ExternalInput")
    bada_d = dram("b_ada", [2, 6 * D], F32, "ExternalInput")
    lng_d = dram("ln_g", [2, 2, D], F32, "ExternalInput")
    lnb_d = dram("ln_b", [2, 2, D], F32, "ExternalInput")
    wab_d = dram("w_in_ab", [D, 3072], F32, "ExternalInput")
    lam_d = dram("diff_lambda", [256], F32, "ExternalInput")
    subg_d = dram("diff_subln_g", [128], F32, "ExternalInput")
    wcd_d = dram("w_in_cd", [D, 3080], F32, "ExternalInput")
    fb_d = dram("forget_b", [8], F32, "ExternalInput")
    wo_d = dram("w_o", [2, D, D], F32, "ExternalInput")
    wfi_d = dram("w_ffn_in", [2, D, 2 * DFF], F32, "ExternalInput")
    wfo_d = dram("w_ffn_out", [2, DFF, D], F32, "ExternalInput")
    bt_d = dram("btiles", [N_ETILES, 128, 128], F32, "ExternalInput")
    ident_d = dram("ident", [128, 128], F32, "ExternalInput")
    tri_d = dram("tri", [128, 128], F32, "ExternalInput")
    kaug_d = dram("kaug", [8, T], F32, "ExternalInput")
    out_d = dram("out", [NB, T, D], F32, "ExternalOutput")
    wg16 = dram("wg16", [2, 24, 128, 8, 128], BF16, "Internal")
    kaug16 = dram("kaug16", [8, T], BF16, "Internal")
    wo16 = dram("wo16", [2, 128, 8, D], BF16, "Internal")
    wfi16 = dram("wfi16", [2, NF, 128, 8, 256], BF16, "Internal")
    wfo16 = dram("wfo16", [2, NF, 128, D], BF16, "Internal")
    ada_s = dram("ada_s", [2, NB, 6 * D], F32, "Internal")
    E_d = dram("E_d", [N_ETILES, 128, 128], BF16, "Internal")
    dbg = {}
    if stage < 99:
        dbg["mixT"] = dram("dbg_mixT", [128, 8, T], BF16, "ExternalOutput")
        dbg["X"] = dram("dbg_X", [T, D], F32, "ExternalOutput")

    with ExitStack() as es:
        P = Prog(nc, es)
        ctr = {"ev": 0, "bank": 0, "pt": 0}

        def sb(st, name, shape, dt):
            ctr["uid"] = ctr.get("uid", 0) + 1
            return st.enter_context(nc.sbuf_tensor("%s_u%d" % (name, ctr["uid"]), shape, dt))

        X = sb(es, "X", [128, NT, D], F32)
        hT = sb(es, "hT", [128, 8, T], BF16)
        ident = sb(es, "ident", [128, 128], F32)
        ident16 = sb(es, "ident16", [128, 128], BF16)
        tri = sb(es, "tri", [128, 128], F32)
        ones32 = sb(es, "ones32", [128, 128], F32)
        ones16 = sb(es, "ones16", [128, 128], BF16)
        gmask = sb(es, "gmask", [128, NT, 8], F32)
        modT = sb(es, "modT", [128, 4, 8], F32)
        neglam = sb(es, "neglam", [128, 1], F32)
        subg = sb(es, "subg", [128, 1], F32)
        fbb = sb(es, "fbb", [128, 8], F32)
        epsc = sb(es, "epsc", [128, 1], F32)
        stt = sb(es, "stt", [128, 2, 2, 6], F32)
        mv = sb(es, "mv", [128, 2, 8], F32)
        banks = [es.enter_context(nc.psum_tensor("ps%d" % i, [128, 512], F32)) for i in range(8)]

        def bank(group=None):
            lst = group if group is not None else list(range(8))
            key = "bank" + str(lst)
            i = ctr.get(key, 0)
            ctr[key] = i + 1
            b = lst[i % len(lst)]
            return banks[b], "ps%d" % b

        def evac(out, in_, reads, writes, eng=None):
            if eng is None:
                eng = "act" if ctr["ev"] % 2 == 0 else "dve"
                ctr["ev"] += 1
            if eng == "act":
                P.op("act", lambda e: e.activation(out=out, in_=in_, func=AF.Copy), reads, writes)
            else:
                P.op(eng, lambda e: e.tensor_copy(out=out, in_=in_), reads, writes)

        def mm(out, lhsT, rhs, start, stop, reads, writes, inc=None):
            if inc is None:
                inc = stop
            P.op("pe", lambda e: e.matmul(out, lhsT=lhsT, rhs=rhs, start=start, stop=stop), reads, writes, inc=inc)

        P.dma("sp", ident[:], ident_d, writes=["ident"])
        P.dma("sp", tri[:], tri_d, writes=["tri"])
        P.op("dve", lambda e: e.memset(ones32[:], 1.0), writes=["ones32"])
        P.op("dve", lambda e: e.memset(ones16[:], 1.0), writes=["ones16"])
        P.op("dve", lambda e: e.memset(epsc[:], EPS), writes=["epsc"])
        P.op("dve", lambda e: e.memset(gmask[:], -1e30), writes=["gmask"])
        for ti in range(NT):
            own = ti // 2
            if own > 0:
                P.op("dve", lambda e, ti=ti, own=own: e.memset(gmask[:, ti, 0:own], 0.0), reads=["gmask"], writes=["gmask"])
            P.op("dve", lambda e, ti=ti, own=own: e.memset(gmask[:, ti, own:own + 1], 1e30), reads=["gmask"], writes=["gmask"])
        P.dma("sp", subg[:], subg_d.rearrange("(p o) -> p o", o=1), writes=["subg"])
        P.op("dve", lambda e: e.tensor_scalar(out=subg[:], in0=subg[:], scalar1=0.8, scalar2=None, op0=ALU.mult),
             reads=["subg"], writes=["subg"])
        P.dma("sp", fbb[:], fb_d.partition_broadcast(128), writes=["fbb"])

        modall = sb(es, "modall", [128, 2, 48, NB], F32)
        wfs = sb(es, "wfs", [128, 8, 8], BF16)
        with ExitStack() as ps_:
            ps1 = ExitStack()
            negfar = sb(ps1, "negfar", [128, 16], F32)
            lam = sb(ps1, "lam", [128, 256], F32)
            lamp = sb(ps1, "lamp", [128, 128], F32)
            ls = sb(ps1, "ls", [128, 4], F32)
            btl = [sb(ps1, "btl%d" % i, [128, 4, 128], F32) for i in range(2)]
            etl = [sb(ps1, "etl%d" % i, [128, 4, 128], BF16) for i in range(2)]
            kg32 = sb(ps1, "kg32", [8, T], F32)
            kg16 = sb(ps1, "kg16", [8, T], BF16)

            P.op("dve", lambda e: e.tensor_copy(out=ident16[:], in_=ident[:]), reads=["ident"], writes=["ident16"])
            P.dma("sp", kg32[:], kaug_d, writes=["kg32"])
            P.op("dve", lambda e: e.tensor_copy(out=kg16[:], in_=kg32[:]), reads=["kg32"], writes=["kg16"])
            P.dma("sp", kaug16, kg16[:], reads=["kg16"], writes=["kaug16"])

            P.dma("sp", lam[:], lam_d.partition_broadcast(128), writes=["lam"])
            P.op("dve", lambda e: e.tensor_tensor(out=lamp[:, 0:64], in0=lam[:, 0:64], in1=lam[:, 64:128], op=ALU.mult),
                 reads=["lam"], writes=["lamp"])
            P.op("dve", lambda e: e.tensor_tensor(out=lamp[:, 64:128], in0=lam[:, 128:192], in1=lam[:, 192:256], op=ALU.mult),
                 reads=["lam", "lamp"], writes=["lamp"])
            P.op("dve", lambda e: e.reduce_sum(out=ls[:, 0:1], in_=lamp[:, 0:64], axis=AX.X), reads=["lamp"], writes=["ls"])
            P.op("dve", lambda e: e.reduce_sum(out=ls[:, 1:2], in_=lamp[:, 64:128], axis=AX.X), reads=["lamp", "ls"], writes=["ls"])
            P.op("act", lambda e: e.activation(out=ls[:, 2:4], in_=ls[:, 0:2], func=AF.Exp), reads=["ls"], writes=["ls"])
            P.op("dve", lambda e: e.tensor_tensor(out=neglam[:], in0=ls[:, 3:4], in1=ls[:, 2:3], op=ALU.subtract),
                 reads=["ls"], writes=["neglam"])
            P.op("dve", lambda e: e.tensor_scalar(out=neglam[:], in0=neglam[:], scalar1=-0.2, scalar2=None, op0=ALU.add),
                 reads=["neglam"], writes=["neglam"])

            P.op("dve", lambda e: e.memset(negfar[:], 0.0), writes=["negfar"])
            P.dma("sp", negfar[:, 0:12], relb_d[31, :].partition_broadcast(128), reads=["negfar"], writes=["negfar"])
            P.op("dve", lambda e: e.tensor_scalar(out=negfar[:], in0=negfar[:], scalar1=-1.0, scalar2=None, op0=ALU.mult),
                 reads=["negfar"], writes=["negfar"])
            for gi, t0 in enumerate(range(0, N_ETILES, 4)):
                n = min(4, N_ETILES - t0)
                bb, ee = btl[gi % 2], etl[gi % 2]
                bn, en = "btl%d" % (gi % 2), "etl%d" % (gi % 2)
                P.dma("sp", bb[:, 0:n, :], bt_d[t0:t0 + n].rearrange("t k q -> k t q"), writes=[bn])
                for i in range(n):
                    col = int(ECOLS[t0 + i])
                    P.op("act", lambda e, ee=ee, bb=bb, i=i, col=col: e.activation(
                        out=ee[:, i, :], in_=bb[:, i, :], func=AF.Exp, bias=negfar[:, col:col + 1], scale=1.0),
                        reads=[bn, "negfar"], writes=[en])
                P.dma("sp", E_d[t0:t0 + n].rearrange("t k q -> k t q"), ee[:, 0:n, :], reads=[en], writes=["E_d"])

            P.barrier()
            ps1.close()
            csb = sb(ps_, "csb", [NB, D], F32)
            cT32 = sb(ps_, "cT32", [128, 8, NB], F32)
            badas = [sb(ps_, "bada%d" % i, [NB, 512], F32) for i in range(2)]
            adasb = sb(ps_, "adasb", [NB, 6 * D], F32)
            NSTG = 3
            stg32 = [sb(ps_, "stg32_%d" % i, [128, 8 * 520], F32) for i in range(NSTG)]
            stg16 = [sb(ps_, "stg16_%d" % i, [128, 8 * 512], BF16) for i in range(NSTG)]
            P.dma("sp", csb[:], c_d, writes=["csb"])
            P.op("act", lambda e: e.activation(out=csb[:], in_=csb[:], func=AF.Silu), reads=["csb"], writes=["csb"])
            pb, pbn = bank()
            for k in range(8):
                P.op("pe", lambda e, k=k: e.transpose(pb[:, k * NB:(k + 1) * NB], csb[0:NB, k * 128:(k + 1) * 128], ident[0:NB, 0:NB]),
                     reads=["csb", "ident"], writes=[pbn], inc=(k == 7))
            P.op("dve", lambda e: e.tensor_copy(out=cT32[:].rearrange("p k b -> p (k b)"), in_=pb[:, 0:8 * NB]), reads=[pbn], writes=["cT32"])
            it = 0
            for l in range(2):
                for n in range(12):
                    bada, bdn = badas[it % 2], "bada%d" % (it % 2)
                    P.dma("sp", bada[:], bada_d[l, n * 512:(n + 1) * 512].partition_broadcast(NB), writes=[bdn])
                    si = it % NSTG
                    it += 1
                    wv = stg32[si][:, 0:8 * 512].rearrange("p (c n) -> p c n", c=8)
                    P.dma("sp", wv, wada_d[l, :, n * 512:(n + 1) * 512].rearrange("(c p) n -> p c n", p=128), writes=["stg32_%d" % si])
                    pb, pbn = bank()
                    for k in range(8):
                        mm(pb[0:NB, :], cT32[:, k, :], wv[:, k, :], k == 0, k == 7, ["cT32", "stg32_%d" % si], [pbn])
                    P.op("dve", lambda e, pb=pb, n=n: e.tensor_tensor(
                        out=adasb[:, n * 512:(n + 1) * 512], in0=pb[0:NB, :], in1=bada[:], op=ALU.add),
                        reads=[pbn, bdn], writes=["adasb"])
                P.dma("sp", ada_s[l], adasb[:], reads=["adasb"], writes=["ada_s"])
                pb, pbn = bank()
                for ch in range(48):
                    P.op("pe", lambda e, ch=ch: e.transpose(pb[:, ch * NB:(ch + 1) * NB], adasb[0:NB, ch * 128:(ch + 1) * 128], ident[0:NB, 0:NB]),
                         reads=["adasb", "ident"], writes=[pbn], inc=(ch == 47))
                P.op("dve", lambda e, l=l, pb=pb: e.tensor_copy(out=modall[:, l].rearrange("p c b -> p (c b)"), in_=pb[:, 0:48 * NB]),
                     reads=[pbn], writes=["modall"])
            for c0 in (8, 32):
                P.op("dve", lambda e, c0=c0: e.tensor_scalar(out=modall[:, :, c0:c0 + 8, :], in0=modall[:, :, c0:c0 + 8, :], scalar1=1.0,
                                                               scalar2=None, op0=ALU.add), reads=["modall"], writes=["modall"])

            cast_engs = ["dve", "act", "pool"]
            pieces = []

            def do_cast(eng, ov, iv, n32, n16):
                if eng == "act":
                    P.op("act", lambda e: e.activation(out=ov, in_=iv, func=AF.Copy), reads=[n32], writes=[n16])
                else:
                    P.op(eng, lambda e: e.tensor_copy(out=ov, in_=iv), reads=[n32], writes=[n16])

            def add_piece(loads, casts, stores):
                k = len(pieces)
                eng = cast_engs[k % 3]

                def ld(si):
                    for vf, src in loads:
                        P.dma("sp", vf(stg32[si]), src, reads=["stg32_%d" % si], writes=["stg32_%d" % si])

                def cs(si):
                    for of, inf in casts:
                        do_cast(eng, of(stg16[si]), inf(stg32[si]), "stg32_%d" % si, "stg16_%d" % si)

                def st(si):
                    for dst, vf, names in stores:
                        P.dma("sp", dst, vf(stg16[si]), reads=["stg16_%d" % si], writes=names)
                pieces.append((ld, cs, st))

            for l, src in ((0, wab_d), (1, wcd_d)):
                for g in range(6):
                    if l == 1 and g < 5:
                        continue
                    ncol = 520 if (l == 1 and g == 5) else 512
                    casts = [(lambda t: t[:, 0:4096].rearrange("p (s c n) -> p c s n", s=4, c=8),
                              lambda t, ncol=ncol: t[:, 0:8 * ncol].rearrange("p (c n) -> p c n", c=8)[:, :, 0:512].rearrange(
                                  "p c (s n) -> p c s n", s=4))]
                    if ncol == 520:
                        casts.append((lambda t: wfs[:], lambda t: t[:, 0:8 * 520].rearrange("p (c n) -> p c n", c=8)[:, :, 512:520]))
                    add_piece([(lambda t, ncol=ncol: t[:, 0:8 * ncol].rearrange("p (c n) -> p c n", c=8),
                                src[:, g * 512:g * 512 + ncol].rearrange("(c p) n -> p c n", p=128))],
                              casts,
                              [(wg16[l, g * 4:(g + 1) * 4].rearrange("s p c n -> p s (c n)"),
                                lambda t: t[:, 0:4096].rearrange("p (s x) -> p s x", s=4),
                                ["wg16:%d:%d" % (l, g * 4 + i) for i in range(4)])])
            npc = len(pieces)
            for i in range(npc + 1):
                if i < npc:
                    pieces[i][0](i % NSTG)
                if i >= 1:
                    pieces[i - 1][1]((i - 1) % NSTG)
                    pieces[i - 1][2]((i - 1) % NSTG)
            P.barrier()

        class BgCast:
            def __init__(self, mixT, pieces, every):
                base = mixT[:, 4:8, :].rearrange("p a n -> p (a n)")
                self.s32 = [base[:, i * 2048:(i + 1) * 2048].bitcast(F32) for i in range(3)]
                self.s16 = [base[:, 6144 + i * 1024:6144 + (i + 1) * 1024] for i in range(2)]
                self.pieces = pieces
                self.t = 0
                self.n = 0
                self.every = every

            def view(self, ap, like):
                if len(like.shape) == 3:
                    return ap.rearrange("p (c n) -> p c n", c=like.shape[1])
                return ap

            def step(self):
                t, pcs = self.t, self.pieces
                if t < len(pcs):
                    src, dst, names = pcs[t]
                    P.dma("sp", self.view(self.s32[t % 3], src), src, reads=["bg32_%d" % (t % 3)], writes=["bg32_%d" % (t % 3)])
                u = t - 2
                if 0 <= u < len(pcs):
                    src, dst, names = pcs[u]
                    i32, i16 = u % 3, u % 2
                    eng = "dve"
                    P.op(eng, lambda e: e.tensor_copy(out=self.s16[i16], in_=self.s32[i32]), reads=["bg32_%d" % i32], writes=["bg16_%d" % i16])
                    P.dma("sp", dst, self.view(self.s16[i16], dst), reads=["bg16_%d" % i16], writes=names)
                self.t += 1

            def tick(self):
                self.n += 1
                if self.n % self.every == 0 and self.t < len(self.pieces) + 2:
                    self.step()

            def flush(self):
                while self.t < len(self.pieces) + 2:
                    self.step()

        def col_piece(src2d, dst3d, names):
            return (src2d.rearrange("(c p) n -> p c n", p=128), dst3d, names)

        bg_l0 = []
        for n0 in range(8):
            bg_l0.append(col_piece(wo_d[0, :, n0 * 128:(n0 + 1) * 128], wo16[0, :, :, n0 * 128:(n0 + 1) * 128], ["wo16:0"]))
        for f in range(NF):
            bg_l0.append(col_piece(wfi_d[0, :, f * 128:(f + 1) * 128], wfi16[0, f, :, :, 0:128], ["wfi16:0"]))
            bg_l0.append(col_piece(wfi_d[0, :, DFF + f * 128:DFF + (f + 1) * 128], wfi16[0, f, :, :, 128:256], ["wfi16:0"]))
            bg_l0.append((wfo_d[0, f * 128:(f + 1) * 128, :], wfo16[0, f], ["wfo16:0"]))
        for sl in range(20):
            bg_l0.append(col_piece(wcd_d[:, sl * 128:(sl + 1) * 128], wg16[1, sl], ["wg16:1:%d" % sl]))
        for n0 in range(8):
            bg_l0.append(col_piece(wo_d[1, :, n0 * 128:(n0 + 1) * 128], wo16[1, :, :, n0 * 128:(n0 + 1) * 128], ["wo16:1"]))
        bg_l1 = []
        for f in range(NF):
            bg_l1.append(col_piece(wfi_d[1, :, f * 128:(f + 1) * 128], wfi16[1, f, :, :, 0:128], ["wfi16:1"]))
            bg_l1.append(col_piece(wfi_d[1, :, DFF + f * 128:DFF + (f + 1) * 128], wfi16[1, f, :, :, 128:256], ["wfi16:1"]))
            bg_l1.append((wfo_d[1, f * 128:(f + 1) * 128, :], wfo16[1, f], ["wfo16:1"]))
        bgs = {"st": None}

        def load_mod(l, b):
            for j, c0 in enumerate((0, 8, 24, 32)):
                P.op("dve", lambda e, j=j, c0=c0: e.tensor_copy(out=modT[:, j, :], in_=modall[:, l, c0:c0 + 8, b]),
                     reads=["modall"], writes=["modT%d" % j])

        def transposes(sub):
            jsh, jsc = (0, 1) if sub == 0 else (2, 3)
            for tg in range(4):
                for c in range(8):
                    pb, pbn = bank()
                    for j in range(4):
                        ti = tg * 4 + j
                        P.op("pe", lambda e, pb=pb, j=j, ti=ti, c=c: e.transpose(
                            pb[:, j * 128:(j + 1) * 128], X[:, ti, c * 128:(c + 1) * 128], ident[:]),
                            reads=["X:%d" % ti, "ident"], writes=[pbn], inc=(j == 3))
                    rd = [pbn, "modT%d" % jsh, "modT%d" % jsc]
                    wr = ["hT:%d:%d" % (c, tg)]
                    if ctr["ev"] % 2 == 0:
                        P.op("act", lambda e, pb=pb, c=c, tg=tg: e.activation(
                            out=hT[:, c, tg * 512:(tg + 1) * 512], in_=pb[:], func=AF.Identity,
                            bias=modT[:, jsh, c:c + 1], scale=modT[:, jsc, c:c + 1]), rd, wr)
                    else:
                        P.op("dve", lambda e, pb=pb, c=c, tg=tg: e.tensor_scalar(
                            out=hT[:, c, tg * 512:(tg + 1) * 512], in0=pb[:], scalar1=modT[:, jsc, c:c + 1],
                            scalar2=modT[:, jsh, c:c + 1], op0=ALU.mult, op1=ALU.add), rd, wr)
                    ctr["ev"] += 1

        def hT_reads(tgs):
            return ["hT:%d:%d" % (c, tg) for c in range(8) for tg in tgs]

        def ln_residual(ti, zts, LNG, LNB, psrc):
            p = ti % 2
            zt = zts[p]
            zn = ["zt%d_%d" % (p, hf) for hf in range(2)]
            for hf in range(2):
                pb, pbn = psrc[hf]
                sl = slice(hf * 512, (hf + 1) * 512)
                P.op("dve", lambda e, pb=pb, sl=sl: e.scalar_tensor_tensor(out=zt[:, sl], in0=X[:, ti, sl], scalar=ALPHA, in1=pb[:],
                                                                            op0=ALU.mult, op1=ALU.add),
                     reads=["X:%d" % ti, pbn], writes=[zn[hf]])
                P.op("dve", lambda e, sl=sl, hf=hf: e.bn_stats(out=stt[:, p, hf, :], in_=zt[:, sl]), reads=[zn[hf]], writes=["stt%d_%d" % (p, hf)])
            P.op("dve", lambda e: e.bn_aggr(out=mv[:, p, 0:2], in_=stt[:, p].rearrange("p a b -> p (a b)")),
                 reads=["stt%d_0" % p, "stt%d_1" % p], writes=["mv%d" % p])
            P.op("act", lambda e: e.activation(out=mv[:, p, 2:3], in_=mv[:, p, 1:2], func=AF.Ln, bias=epsc[:], scale=1.0),
                 reads=["mv%d" % p, "epsc"], writes=["mvb%d" % p])
            P.op("act", lambda e: e.activation(out=mv[:, p, 3:4], in_=mv[:, p, 2:3], func=AF.Exp, scale=-0.5),
                 reads=["mvb%d" % p], writes=["mvc%d" % p])
            P.op("dve", lambda e: e.tensor_scalar(out=mv[:, p, 4:5], in0=mv[:, p, 0:1], scalar1=-1.0, scalar2=mv[:, p, 3:4],
                                                  op0=ALU.mult, op1=ALU.mult), reads=["mv%d" % p, "mvc%d" % p], writes=["mvd%d" % p])
            P.op("act", lambda e: e.activation(out=zt[:], in_=zt[:], func=AF.Identity, bias=mv[:, p, 4:5], scale=mv[:, p, 3:4]),
                 reads=zn + ["mvc%d" % p, "mvd%d" % p], writes=zn)
            P.op("pool", lambda e: e.tensor_tensor(out=zt[:], in0=zt[:], in1=LNG[:], op=ALU.mult),
                 reads=zn + ["LNG"], writes=zn)

            def stage_b():
                P.op("dve", lambda e: e.tensor_tensor(out=X[:, ti, :], in0=zt[:], in1=LNB[:], op=ALU.add),
                     reads=zn + ["LNB"], writes=["X:%d" % ti])
            return stage_b

        def load_ln(l, sub, b, LNG, LNB, GB):
            off = 2048 if sub == 0 else 5120
            P.dma("sp", GB[:], ada_s[l, b, off:off + 1024].partition_broadcast(128), writes=["GB"])
            P.op("dve", lambda e: e.tensor_scalar(out=GB[:], in0=GB[:], scalar1=1.0, scalar2=None, op0=ALU.add),
                 reads=["GB"], writes=["GB"])
            P.dma("sp", LNG[:], lng_d[l, sub].partition_broadcast(128), writes=["LNG"])
            P.dma("sp", LNB[:], lnb_d[l, sub].partition_broadcast(128), writes=["LNB"])

        def mixer(l, b):
            with ExitStack() as ms:
                mixT = sb(ms, "mixT", [128, 8, T], BF16)
                with ExitStack() as gs:
                    QT = sb(gs, "QT", [128, 2, T], BF16)
                    KT = sb(gs, "KT", [128, 2, T], BF16)
                    Vaug = sb(gs, "Vaug", [128, NT, 2, 128], BF16)
                    wgr = [sb(gs, "wgr%d" % i, [128, 3, 8, 128], BF16) for i in range(2)]
                    PTs = [sb(gs, "PT%d" % i, [128, 512], BF16) for i in range(4)]
                    Eg = [sb(gs, "Eg%d" % i, [128, 10, 128], BF16) for i in range(2)]
                    rec = sb(gs, "rec", [128, 512], F32)
                    rec2 = sb(gs, "rec2", [128, 512], F32)
                    gate = sb(gs, "gate", [128, NT, 8], F32)
                    top8 = sb(gs, "top8", [128, NT, 8], F32)
                    sel = sb(gs, "sel", [128, NT, 8], F32)
                    negpad = sb(gs, "negpad", [128, NT, 72], BF16)
                    gts = [(gate, top8, sel, negpad)]
                    if l == 1:
                        gts.append((sb(gs, "gate2", [128, NT, 8], F32), sb(gs, "top82", [128, NT, 8], F32),
                                    sb(gs, "sel2", [128, NT, 8], F32), sb(gs, "negpad2", [128, NT, 72], BF16)))
                        P.op("pool", lambda e: e.memset(gts[1][3][:], 0.0), writes=["negpad1"])
                    ksum = sb(gs, "ksum", [128, 2, 8], F32)
                    kmb = sb(gs, "kmb", [128, 2, 8], BF16)
                    zf = sb(gs, "zf", [128, NT, 8], F32)
                    cwt = sb(gs, "cwt", [128, 2, NT, 8], F32)
                    offn = sb(gs, "offn", [128, NT + 1, 8], F32)
                    ncum = sb(gs, "ncum", [128, NT, 8], F32)
                    bfox = sb(gs, "bfox", [128, 4, NT, 8], F32)
                    P.op("pool", lambda e: e.memset(Vaug[:, :, :, 64:128], 1.0), writes=["Vaug"])
                    P.op("pool", lambda e: e.memset(negpad[:], 0.0), writes=["negpad0"])

                    def pt_next():
                        i = ctr["pt"] % 4
                        ctr["pt"] += 1
                        return PTs[i], "PT%d" % i

                    SB = [0, 1, 2]
                    OB = [3, 4, 5, 6]

                    def load_group_w(gi, slices):
                        w = wgr[gi % 2]
                        wn = "wgr%d" % (gi % 2)
                        for j, s in enumerate(slices):
                            P.dma("sp", w[:, j], wg16[l, s], reads=["wg16:%d:%d" % (l, s)], writes=[wn + ":%d" % j])
                        return w, wn

                    def proj_T(dst_fn, w, wn, j):
                        for tg in range(4):
                            pb, pbn = bank()
                            for k in range(8):
                                mm(pb[:], w[:, j, k, :], hT[:, k, tg * 512:(tg + 1) * 512], k == 0, k == 7,
                                   hT_reads([tg]) + [wn + ":%d" % j], [pbn])
                            dst_fn(tg, pb, pbn)

                    def proj_V(w, wn, tok_ap_fn, pair):
                        for tg in range(4):
                            pb, pbn = bank()
                            for j in range(4):
                                slot = tg * 4 + j
                                sl, tgs = tok_ap_fn(slot)
                                for k in range(8):
                                    mm(pb[:, j * 128:(j + 1) * 128], hT[:, k, sl], w[:, 2, k, :], k == 0, k == 7,
                                       hT_reads(tgs) + [wn + ":2"], [pbn], inc=(k == 7 and j == 3))
                            if pair:
                                evac(Vaug[:, tg * 4:(tg + 1) * 4, :, 0:64],
                                     pb[:].rearrange("p (t h d) -> p t h d", t=4, h=2), [pbn], ["Vaug"])
                            else:
                                evac(Vaug[:, tg * 4:(tg + 1) * 4, 0, :],
                                     pb[:].rearrange("p (t d) -> p t d", t=4), [pbn], ["Vaug"])

                    contig = lambda slot: (slice(slot * 128, (slot + 1) * 128), [slot // 4])

                    def load_E(gi, idxs):
                        e_ = Eg[gi % 2]
                        en = "Eg%d" % (gi % 2)
                        for j, ix in enumerate(idxs):
                            P.dma("sp", e_[:, j, :], E_d[ix], reads=["E_d"], writes=[en])
                        return e_, en

                    def dense_attn(units_for_chunk, finish_chunk, LOOK=2, SBK=(0, 1, 2), PAIR=False, MENG="pool"):
                        for c in range(4):
                            units = units_for_chunk(c)
                            SBK = list(SBK)

                            def issue_S(u):
                                pb, pbn = bank(SBK)
                                qlo = u["qlo"]
                                mm(pb[:, qlo:512], u["kT"], u["qT"], True, True, u["sreads"], [pbn])
                                u["sb"], u["sbn"] = pb, pbn

                            last_idx = {}
                            for i, u in enumerate(units):
                                for (_l, _r, _ob, obn) in u["pv"]:
                                    last_idx[obn] = i
                            for i in range(min(LOOK, len(units))):
                                issue_S(units[i])
                            for i, u in enumerate(units):
                                if PAIR:
                                    if i % 2 == 0:
                                        for k2 in (i + LOOK, i + LOOK + 1):
                                            if k2 < len(units):
                                                issue_S(units[k2])
                                elif i + LOOK < len(units):
                                    issue_S(units[i + LOOK])
                                if bgs["st"] is not None:
                                    bgs["st"].tick()
                                qlo = u["qlo"]
                                pt, ptn = pt_next()
                                bias = u["bias"]
                                if bias is None:
                                    P.op("act", lambda e, pt=pt, u=u, qlo=qlo: e.activation(
                                        out=pt[:, qlo:512], in_=u["sb"][:, qlo:512], func=AF.Exp, scale=0.125),
                                        reads=[u["sbn"]], writes=[ptn])
                                else:
                                    P.op("act", lambda e, pt=pt, u=u, qlo=qlo, bias=bias: e.activation(
                                        out=pt[:, qlo:512], in_=u["sb"][:, qlo:512], func=AF.Exp, scale=0.125, bias=bias),
                                        reads=[u["sbn"], "bfox"], writes=[ptn])
                                for (ii, et, en) in u["masks"]:
                                    P.op(MENG, lambda e, pt=pt, ii=ii, et=et: e.tensor_tensor(
                                        out=pt[:, ii * 128:(ii + 1) * 128], in0=pt[:, ii * 128:(ii + 1) * 128], in1=et, op=ALU.mult),
                                        reads=[ptn, en], writes=[ptn])
                                for (lhsT, lreads, ob, obn) in u["pv"]:
                                    lastu = (last_idx[obn] == i)
                                    if u["diag"] and qlo > 0:
                                        jj = qlo // 128
                                        for ii in range(jj, 4):
                                            mm(ob[:, ii * 128:(ii + 1) * 128], lhsT, pt[:, ii * 128:(ii + 1) * 128],
                                               u["first"], lastu and ii == 3, [ptn] + lreads, [obn], inc=(ii == 3))
                                    else:
                                        mm(ob[:], lhsT, pt[:], u["first"], lastu, [ptn] + lreads, [obn], inc=True)
                            finish_chunk(c)

                    gi = 0
                    if l == 0:
                        with ExitStack() as at:
                            r0 = sb(at, "r0", [128, 512], F32)
                            r1 = sb(at, "r1", [128, 512], F32)
                            oo = sb(at, "oo", [128, 512], F32)
                            t1 = sb(at, "t1", [128, 512], F32)
                            sq = sb(at, "sq", [128, 512], F32)
                            if b == 0:
                                bgs["st"] = BgCast(mixT, bg_l0, 3)
                            for h in range(4):
                                w, wn = load_group_w(gi, [h, 4 + h, 8 + h])
                                e_, en = load_E(gi, [h, 12 + h])
                                gi += 1
                                proj_T(lambda tg, pb, pbn: evac(QT[:, 0, tg * 512:(tg + 1) * 512], pb[:], [pbn], ["QT:%d" % tg]), w, wn, 0)
                                proj_T(lambda tg, pb, pbn: evac(KT[:, 0, tg * 512:(tg + 1) * 512], pb[:], [pbn], ["KT:%d" % tg]), w, wn, 1)
                                proj_V(w, wn, contig, False)
                                obs = [(banks[3], "ps3"), (banks[4], "ps4"), (banks[5], "ps5"), (banks[6], "ps6")]

                                def units_A(c, e_=e_, en=en):
                                    us = []
                                    for j in range(4 * c + 4):
                                        for m in range(2):
                                            jj = j - 4 * c
                                            qlo = max(jj, 0) * 128
                                            masks = []
                                            for ii in range(4):
                                                i = 4 * c + ii
                                                if j == i:
                                                    masks.append((ii, e_[:, 0, :], en))
                                                elif j == i - 1:
                                                    masks.append((ii, e_[:, 1, :], en))
                                            rb = m * 64
                                            us.append(dict(
                                                qlo=qlo, diag=(jj >= 0), first=(j == 0),
                                                kT=KT[rb:rb + 64, 0, j * 128:(j + 1) * 128],
                                                qT=QT[rb:rb + 64, 0, c * 512 + qlo:(c + 1) * 512],
                                                sreads=["KT:%d" % (j // 4), "QT:%d" % c], bias=None, masks=masks,
                                                pv=[(Vaug[:, j, 0, :], ["Vaug"], obs[2 * m][0], obs[2 * m][1]),
                                                    (ones16[:], ["ones16"], obs[2 * m + 1][0], obs[2 * m + 1][1])]))
                                    return us

                                def finish_A(c, h=h):
                                    cs = slice(c * 512, (c + 1) * 512)
                                    P.op("act", lambda e: e.activation(out=r0[:], in_=banks[4][:], func=AF.Ln), reads=["ps4"], writes=["r0"])
                                    P.op("dve", lambda e: e.tensor_copy(out=oo[:], in_=banks[3][:]), reads=["ps3"], writes=["oo"])
                                    P.op("act", lambda e: e.activation(out=r1[:], in_=banks[6][:], func=AF.Ln), reads=["ps6"], writes=["r1"])
                                    P.op("dve", lambda e: e.tensor_copy(out=t1[:], in_=banks[5][:]), reads=["ps5"], writes=["t1"])
                                    P.op("act", lambda e: e.activation(out=r0[:], in_=r0[:], func=AF.Exp, scale=-1.0), reads=["r0"], writes=["r0"])
                                    P.op("act", lambda e: e.activation(out=r1[:], in_=r1[:], func=AF.Exp, scale=-1.0), reads=["r1"], writes=["r1"])
                                    P.op("dve", lambda e: e.tensor_tensor(out=oo[:], in0=oo[:], in1=r0[:], op=ALU.mult),
                                         reads=["oo", "r0"], writes=["oo"])
                                    P.op("dve", lambda e: e.tensor_tensor(out=t1[:], in0=t1[:], in1=r1[:], op=ALU.mult),
                                         reads=["t1", "r1"], writes=["t1"])
                                    P.op("dve", lambda e: e.scalar_tensor_tensor(out=oo[:], in0=t1[:], scalar=neglam[:, 0:1], in1=oo[:],
                                                                                  op0=ALU.mult, op1=ALU.add),
                                         reads=["t1", "oo", "neglam"], writes=["oo"])
                                    P.op("act", lambda e: e.activation(out=sq[:], in_=oo[:], func=AF.Square), reads=["oo"], writes=["sq"])
                                    pm, pmn = bank([0, 1, 2, 7])
                                    mm(pm[:], ones32[:], sq[:], True, True, ["sq", "ones32"], [pmn])
                                    P.op("act", lambda e: e.activation(out=sq[:], in_=pm[:], func=AF.Ln, bias=epsc[:], scale=1.0 / 128.0),
                                         reads=[pmn, "epsc"], writes=["sq"])
                                    P.op("act", lambda e: e.activation(out=sq[:], in_=sq[:], func=AF.Exp, scale=-0.5), reads=["sq"], writes=["sq"])
                                    P.op("dve", lambda e: e.scalar_tensor_tensor(out=mixT[:, h, cs], in0=oo[:], scalar=subg[:, 0:1], in1=sq[:],
                                                                                  op0=ALU.mult, op1=ALU.mult),
                                         reads=["oo", "sq", "subg"], writes=["mixT:%d" % h])

                                dense_attn(units_A, finish_A, LOOK=2, SBK=(0, 1, 2, 7), PAIR=True, MENG="dve")
                            if bgs["st"] is not None:
                                bgs["st"].flush()
                                bgs["st"] = None
                            P.barrier()
                        P.op("pool", lambda e: e.memset(Vaug[:, :, :, 64:128], 1.0), writes=["Vaug"])
                        bt_ = ExitStack()
                        accs = [sb(bt_, "acc%d" % i, [128, T], F32) for i in range(2)]

                        for j in range(4):
                            w, wn = load_group_w(gi, [12 + j, 16 + j, 20 + j])
                            eidx = []
                            for s in range(2):
                                hb = 2 * j + s
                                eidx += [4 + hb, 20 + hb, 28 + hb, 36 + hb, 44 + hb]
                            e_, en = load_E(gi, eidx)
                            gi += 1
                            proj_T(lambda tg, pb, pbn: evac(QT[:, 0, tg * 512:(tg + 1) * 512], pb[:], [pbn], ["QT:%d" % tg]), w, wn, 0)
                            proj_T(lambda tg, pb, pbn: evac(KT[:, 0, tg * 512:(tg + 1) * 512], pb[:], [pbn], ["KT:%d" % tg]), w, wn, 1)
                            QA = ["QT:%d" % i for i in range(4)]
                            KA = ["KT:%d" % i for i in range(4)]

                            def pth_next():
                                i = ctr.get("pth", 0) % 8
                                ctr["pth"] = ctr.get("pth", 0) + 1
                                return PTs[i // 2][:, (i % 2) * 256:(i % 2) * 256 + 256], "PTh%d" % i

                            def window_pattern(s, nset, kset_ap, qset_ap, e2, vslot, obank_of, flush):
                                rb = s * 64
                                pts = {}

                                def issue(i):
                                    ncol = 256 if i + 1 < nset else 128
                                    pb, pbn = bank(SB)
                                    mm(pb[:, 0:ncol], kset_ap(rb, i), qset_ap(rb, i, ncol), True, True, QA + KA, [pbn])
                                    pt, ptn = pth_next()
                                    P.op("act", lambda e: e.activation(out=pt[:, 0:ncol], in_=pb[:, 0:ncol], func=AF.Exp, scale=0.125),
                                         reads=[pbn], writes=[ptn])
                                    P.op("dve", lambda e: e.tensor_tensor(out=pt[:, 0:ncol], in0=pt[:, 0:ncol], in1=e2[:, 0:ncol], op=ALU.mult),
                                         reads=[ptn, en], writes=[ptn])
                                    pts[i] = (pt, ptn)

                                issue(0)
                                if nset > 1:
                                    issue(1)
                                yield
                                for i in range(nset):
                                    if i + 2 < nset:
                                        issue(i + 2)
                                    ob, obn, col = obank_of(i)
                                    if i > 0:
                                        pt, ptn = pts[i - 1]
                                        mm(ob[:, col:col + 128], Vaug[:, vslot(i - 1), s, :], pt[:, 128:256], True, False,
                                           [ptn, "Vaug"], [obn], inc=False)
                                    pt, ptn = pts[i]
                                    mm(ob[:, col:col + 128], Vaug[:, vslot(i), s, :], pt[:, 0:128], i == 0, True,
                                       [ptn, "Vaug"], [obn], inc=True)
                                    flush(i, ob, obn)
                                    yield

                            SB = [0, 1, 2, 7]
                            proj_V(w, wn, contig, True)

                            def pat1(s):
                                acc, an = accs[s], "acc%d" % s
                                e2 = e_[:, s * 5:s * 5 + 2, :].rearrange("p a q -> p (a q)")
                                cur = {}

                                def ob1(i):
                                    if i % 4 == 0:
                                        cur["b"] = bank(OB)
                                    return cur["b"][0], cur["b"][1], (i % 4) * 128

                                def fl1(i, ob, obn):
                                    if i % 4 == 3:
                                        n = i // 4
                                        evac(acc[:, n * 512:(n + 1) * 512], ob[:], [obn], [an])

                                return window_pattern(s, 16,
                                                      lambda rb, i: KT[rb:rb + 64, 0, i * 128:(i + 1) * 128],
                                                      lambda rb, i, ncol: QT[rb:rb + 64, 0, i * 128:i * 128 + ncol],
                                                      e2, lambda i: i, ob1, fl1)

                            run_interleaved([pat1(0), pat1(1)])
                            proj_V(w, wn, lambda slot: (slice(512 * (slot % 4) + slot // 4, 512 * (slot % 4) + 512, 4), [slot % 4]), True)

                            def pat2(s, r):
                                acc, an = accs[s], "acc%d" % s
                                e2 = e_[:, s * 5 + 2:s * 5 + 4, :].rearrange("p a q -> p (a q)")
                                cur = {"b": bank(OB)}

                                def fl2(i, ob, obn):
                                    if i == 3:
                                        av = acc[:, :].rearrange("p (n u f) -> p n u f", n=4, u=128, f=4)[:, :, :, r]
                                        P.op("dve", lambda e: e.tensor_tensor(out=av, in0=av, in1=ob[:].rearrange("p (n u) -> p n u", n=4),
                                                                              op=ALU.add), reads=[obn, an], writes=[an])

                                return window_pattern(s, 4,
                                                      lambda rb, n: KT[rb:rb + 64, 0, 512 * n + r:512 * n + 512:4],
                                                      lambda rb, n, ncol: QT[rb:rb + 64, 0, 512 * n + r:512 * n + 4 * ncol:4],
                                                      e2, lambda n: r * 4 + n,
                                                      lambda n: (cur["b"][0], cur["b"][1], n * 128), fl2)

                            for r in range(4):
                                run_interleaved([pat2(0, r), pat2(1, r)])
                            proj_V(w, wn, lambda slot: (slice(slot, T, 16), [0, 1, 2, 3]), True)
                            p3u = [(r16, s) for r16 in range(16) for s in range(2)]
                            ob3 = {}
                            pts3 = {}
                            L3 = 4

                            def issue3(idx):
                                r16, s = p3u[idx]
                                rb = s * 64
                                e3 = e_[:, s * 5 + 4, :]
                                pb, pbn = bank(SB)
                                mm(pb[:, 0:128], KT[rb:rb + 64, 0, r16:T:16], QT[rb:rb + 64, 0, r16:T:16], True, True, QA + KA, [pbn])
                                pt, ptn = pth_next()
                                P.op("act", lambda e: e.activation(out=pt[:, 0:128], in_=pb[:, 0:128], func=AF.Exp, scale=0.125),
                                     reads=[pbn], writes=[ptn])
                                P.op("dve", lambda e: e.tensor_tensor(out=pt[:, 0:128], in0=pt[:, 0:128], in1=e3, op=ALU.mult),
                                     reads=[ptn, en], writes=[ptn])
                                pts3[idx] = (pt, ptn)

                            for idx in range(min(L3, len(p3u))):
                                issue3(idx)
                            for idx, (r16, s) in enumerate(p3u):
                                if idx + L3 < len(p3u):
                                    issue3(idx + L3)
                                rr = r16 % 4
                                if rr == 0:
                                    ob3[s] = bank(OB)
                                ob, obn = ob3[s]
                                pt, ptn = pts3.pop(idx)
                                mm(ob[:, rr * 128:(rr + 1) * 128], Vaug[:, r16, s, :], pt[:, 0:128], True, True, [ptn, "Vaug"], [obn],
                                   inc=True)
                                if rr == 3:
                                    r0_ = r16 - 3
                                    acc, an = accs[s], "acc%d" % s
                                    av = acc[:, :].rearrange("p (u f) -> p f u", f=16)[:, r0_:r0_ + 4, :]
                                    P.op("dve", lambda e: e.tensor_tensor(out=av, in0=av, in1=ob[:].rearrange("p (f u) -> p f u", f=4),
                                                                          op=ALU.add), reads=[obn, an], writes=[an])
                            for s in range(2):
                                acc, an = accs[s], "acc%d" % s
                                for c in range(4):
                                    cs = slice(c * 512, (c + 1) * 512)
                                    rc, rcn = (rec, "rec") if c % 2 == 0 else (rec2, "rec2")
                                    P.op("act", lambda e, acc=acc, cs=cs, rc=rc: e.activation(out=rc[0:64, :], in_=acc[64:128, cs], func=AF.Ln),
                                         reads=[an, rcn], writes=[rcn])
                                    P.op("act", lambda e, rc=rc: e.activation(out=rc[0:64, :], in_=rc[0:64, :], func=AF.Exp, scale=-1.0),
                                         reads=[rcn], writes=[rcn])
                                    P.op("dve", lambda e, acc=acc, cs=cs, s=s, j=j, rc=rc: e.tensor_tensor(
                                        out=mixT[s * 64:(s + 1) * 64, 4 + j, cs], in0=acc[0:64, cs], in1=rc[0:64, :], op=ALU.mult),
                                        reads=[an, rcn], writes=["mixT:%d" % (4 + j)])
                        P.barrier()
                        bt_.close()
                    else:
                        def finish_pair(s, mc):
                            def fin(c, s=s, mc=mc):
                                ob, obn = cur_o["b%d" % s]
                                cs = slice(c * 512, (c + 1) * 512)
                                rc, rcn = (rec, "rec") if s == 0 else (rec2, "rec2")
                                P.op("act", lambda e: e.activation(out=rc[0:64, :], in_=ob[64:128, :], func=AF.Ln), reads=[obn, rcn], writes=[rcn])
                                P.op("act", lambda e: e.activation(out=rc[0:64, :], in_=rc[0:64, :], func=AF.Exp, scale=-1.0), reads=[rcn], writes=[rcn])
                                P.op("dve", lambda e: e.tensor_tensor(out=mixT[s * 64:(s + 1) * 64, mc, cs], in0=ob[0:64, :], in1=rc[0:64, :],
                                                                      op=ALU.mult), reads=[obn, rcn], writes=["mixT:%d" % mc])
                            return fin

                        cur_o = {}

                        def merge_units(ufs):
                            def mu(c):
                                lists = [uf(c) for uf in ufs]
                                out = []
                                for i in range(max(len(x) for x in lists)):
                                    for x in lists:
                                        if i < len(x):
                                            out.append(x[i])
                                return out
                            return mu

                        def merged_finish(mc):
                            def mf(c):
                                for s in range(2):
                                    finish_pair(s, mc)(c)
                            return mf

                        if b == 0:
                            bgs["st"] = BgCast(mixT, bg_l1, 4)
                        for j in range(4):
                            w, wn = load_group_w(gi, [j, 4 + j, 8 + j])
                            e_, en = load_E(gi, [2 * j, 12 + 2 * j, 2 * j + 1, 12 + 2 * j + 1])
                            gi += 1
                            for s in range(2):
                                P.dma("sp", KT[64:72, s, :], kaug16, reads=["KTaug%d" % s], writes=["KTaug%d" % s])

                            def dq(tg, pb, pbn):
                                for s in range(2):
                                    evac(QT[0:64, s, tg * 512:(tg + 1) * 512], pb[s * 64:(s + 1) * 64, :], [pbn], ["QT%d:%d" % (s, tg)])

                            def dk(tg, pb, pbn):
                                for s in range(2):
                                    evac(KT[0:64, s, tg * 512:(tg + 1) * 512], pb[s * 64:(s + 1) * 64, :], [pbn], ["KT%d:%d" % (s, tg)])

                            proj_T(dq, w, wn, 0)
                            proj_T(dk, w, wn, 1)
                            proj_V(w, wn, contig, True)
                            ufs = []
                            ggens = []
                            for s in range(2):
                                QAs = ["QT%d:%d" % (s, i) for i in range(4)]
                                KAs = ["KT%d:%d" % (s, i) for i in range(4)]
                                def gate_gen(s=s, QAs=QAs, KAs=KAs):
                                    gate_, top8_, sel_, negpad_ = gts[s]
                                    gn, tn, sn_, nn = "gate%d" % s, "top8%d" % s, "sel%d" % s, "negpad%d" % s
                                    P.op("dve", lambda e: e.tensor_reduce(out=ksum[0:64, s, :],
                                                                          in_=KT[0:64, s, :].rearrange("p (n t) -> p n t", t=256),
                                                                          axis=AX.X, op=ALU.add), reads=KAs, writes=["ksum%d" % s])
                                    P.op("dve", lambda e: e.tensor_copy(out=kmb[0:64, s, :], in_=ksum[0:64, s, :]),
                                         reads=["ksum%d" % s], writes=["kmb%d" % s])
                                    yield
                                    pg, pgn = bank()
                                    for ti in range(NT):
                                        mm(pg[:, ti * 8:(ti + 1) * 8], QT[0:64, s, ti * 128:(ti + 1) * 128], kmb[0:64, s, :], True, True,
                                           QAs + ["kmb%d" % s], [pgn], inc=(ti == NT - 1))
                                    yield
                                    P.op("dve", lambda e: e.tensor_tensor(out=gate_[:], in0=pg[:, 0:128].rearrange("p (t n) -> p t n", n=8),
                                                                          in1=gmask[:], op=ALU.add), reads=[pgn, "gmask"], writes=[gn])
                                    for ti in range(NT):
                                        P.op("dve", lambda e, ti=ti: e.max(out=top8_[:, ti, :], in_=gate_[:, ti, :]), reads=[gn], writes=[tn])
                                    P.op("dve", lambda e: e.tensor_tensor(out=sel_[:], in0=gate_[:], in1=top8_[:, :, 3:4].to_broadcast([128, NT, 8]),
                                                                          op=ALU.is_ge), reads=[gn, tn], writes=[sn_])
                                    P.op("dve", lambda e: e.tensor_scalar(out=negpad_[:, :, 64:72], in0=sel_[:], scalar1=1.0, scalar2=-NEG,
                                                                          op0=ALU.subtract, op1=ALU.mult), reads=[sn_], writes=[nn])
                                    yield
                                    for tg in range(4):
                                        pa, pan = bank()
                                        for jj in range(4):
                                            ti = tg * 4 + jj
                                            mm(pa[0:72, jj * 128:(jj + 1) * 128], negpad_[:, ti, :], ident16[:], True, True,
                                               [nn, "ident16"], [pan], inc=(jj == 3))
                                        evac(QT[64:72, s, tg * 512:(tg + 1) * 512], pa[64:72, :], [pan], ["QTaug%d:%d" % (s, tg)])

                                ggens.append(gate_gen())

                                def units_C(c, s=s, e_=e_, en=en, QAs=QAs, KAs=KAs):
                                    if cur_o.get("c%d" % s) != (s, c, "C", j):
                                        cur_o["b%d" % s] = bank(OB)
                                        cur_o["c%d" % s] = (s, c, "C", j)
                                    ob, obn = cur_o["b%d" % s]
                                    us = []
                                    for kt in range(4 * c + 4):
                                        jj = kt - 4 * c
                                        qlo = max(jj, 0) * 128
                                        masks = []
                                        for ii in range(4):
                                            i = 4 * c + ii
                                            if kt == i:
                                                masks.append((ii, e_[:, 2 * s, :], en))
                                            elif kt == i - 1:
                                                masks.append((ii, e_[:, 2 * s + 1, :], en))
                                        us.append(dict(
                                            qlo=qlo, diag=(jj >= 0), first=(kt == 0),
                                            kT=KT[0:72, s, kt * 128:(kt + 1) * 128],
                                            qT=QT[0:72, s, c * 512 + qlo:(c + 1) * 512],
                                            sreads=["KT%d:%d" % (s, kt // 4), "KTaug%d" % s, "QT%d:%d" % (s, c), "QTaug%d:%d" % (s, c)],
                                            bias=None, masks=masks,
                                            pv=[(Vaug[:, kt, s, :], ["Vaug"], ob, obn)]))
                                    return us

                                ufs.append(units_C)
                            run_interleaved(ggens)
                            dense_attn(merge_units(ufs), merged_finish(j), LOOK=3, SBK=(0, 1, 2, 7), MENG="dve")

                        if bgs["st"] is not None:
                            bgs["st"].flush()
                            bgs["st"] = None
                            P.barrier()
                        pf, pfn = bank()
                        for ti in range(NT):
                            for k in range(8):
                                mm(pf[:, ti * 8:(ti + 1) * 8], hT[:, k, ti * 128:(ti + 1) * 128], wfs[:, k, :], k == 0, k == 7,
                                   hT_reads([ti // 4]) + ["wfs"], [pfn], inc=(k == 7 and ti == NT - 1))
                        P.op("dve", lambda e: e.tensor_tensor(out=zf[:], in0=pf[:, 0:128].rearrange("p (t n) -> p t n", n=8),
                                                              in1=fbb[:, :].unsqueeze(1).to_broadcast([128, NT, 8]), op=ALU.add),
                             reads=[pfn, "fbb"], writes=["zf"])
                        P.op("act", lambda e: e.activation(out=zf[:], in_=zf[:], func=AF.Exp, scale=-1.0), reads=["zf"], writes=["zf"])
                        P.op("act", lambda e: e.activation(out=zf[:], in_=zf[:], func=AF.Ln, bias=1.0, scale=1.0), reads=["zf"], writes=["zf"])
                        pc, pcn = bank()
                        zf2 = zf[:].rearrange("p t n -> p (t n)")
                        mm(pc[:, 0:128], tri[:], zf2, True, True, ["tri", "zf"], [pcn], inc=False)
                        mm(pc[:, 128:256], ones32[:], zf2, True, True, ["ones32", "zf"], [pcn], inc=True)
                        P.op("dve", lambda e: e.tensor_copy(out=cwt[:].rearrange("p a t n -> p (a t n)"), in_=pc[:, 0:256]), reads=[pcn], writes=["cwt"])
                        P.op("dve", lambda e: e.memset(offn[:, 0, :], 0.0), writes=["offn"])
                        for ti in range(NT):
                            P.op("dve", lambda e, ti=ti: e.tensor_tensor(out=offn[:, ti + 1, :], in0=offn[:, ti, :], in1=cwt[:, 1, ti, :], op=ALU.add),
                                 reads=["offn", "cwt"], writes=["offn"])
                        P.op("dve", lambda e: e.tensor_tensor(out=ncum[:], in0=offn[:, 0:NT, :], in1=cwt[:, 0, :, :], op=ALU.add),
                             reads=["offn", "cwt"], writes=["ncum"])
                        for c in range(4):
                            P.op("dve", lambda e, c=c: e.tensor_tensor(out=bfox[:, c, :, :], in0=ncum[:],
                                                                       in1=offn[:, 4 * c + 2:4 * c + 3, :].to_broadcast([128, NT, 8]),
                                                                       op=ALU.subtract), reads=["ncum", "offn"], writes=["bfox"])

                        for j in range(4):
                            w, wn = load_group_w(gi, [12 + j, 16 + j, 20 + j])
                            e_, en = load_E(gi, [52])
                            gi += 1
                            proj_T(lambda tg, pb, pbn: evac(QT[:, 0, tg * 512:(tg + 1) * 512], pb[:], [pbn],
                                                            ["QT0:%d" % tg, "QTaug0:%d" % tg]), w, wn, 0)
                            proj_T(lambda tg, pb, pbn: evac(KT[:, 0, tg * 512:(tg + 1) * 512], pb[:], [pbn],
                                                            ["KT0:%d" % tg, "KTaug0"]), w, wn, 1)
                            proj_V(w, wn, contig, True)
                            ufs = []
                            for s in range(2):
                                hd = 2 * j + s

                                def units_D(c, s=s, hd=hd, e_=e_, en=en):
                                    if cur_o.get("c%d" % s) != (s, c, "D", j):
                                        cur_o["b%d" % s] = bank(OB)
                                        cur_o["c%d" % s] = (s, c, "D", j)
                                    ob, obn = cur_o["b%d" % s]
                                    us = []
                                    rb = s * 64
                                    for kt in range(4 * c + 4):
                                        jj = kt - 4 * c
                                        qlo = max(jj, 0) * 128
                                        masks = [(jj, e_[:, 0, :], en)] if jj >= 0 else []
                                        us.append(dict(
                                            qlo=qlo, diag=(jj >= 0), first=(kt == 0),
                                            kT=KT[rb:rb + 64, 0, kt * 128:(kt + 1) * 128],
                                            qT=QT[rb:rb + 64, 0, c * 512 + qlo:(c + 1) * 512],
                                            sreads=["KT0:%d" % (kt // 4), "QT0:%d" % c],
                                            bias=bfox[:, c, kt, hd:hd + 1], masks=masks,
                                            pv=[(Vaug[:, kt, s, :], ["Vaug"], ob, obn)]))
                                    return us

                                ufs.append(units_D)
                            dense_attn(merge_units(ufs), merged_finish(4 + j), LOOK=2, SBK=(0, 1, 2, 7), PAIR=True, MENG="dve")
                    P.barrier()
                if stage < 99 and stage == 2 * (2 * b + l):
                    P.dma("sp", dbg["mixT"], mixT[:], reads=["mixT:%d" % i for i in range(8)])
                with ExitStack() as ws:
                    wos = sb(ws, "wos", [128, 8, D], BF16)
                    LNG = sb(ws, "LNG", [128, D], F32)
                    LNB = sb(ws, "LNB", [128, D], F32)
                    GB = sb(ws, "GB", [128, D], F32)
                    zts = [sb(ws, "zt%d" % i, [128, D], F32) for i in range(2)]
                    load_ln(l, 0, b, LNG, LNB, GB)
                    for k in range(8):
                        P.dma("sp", wos[:, k, :], wo16[l, :, k, :], reads=["wo16:%d" % l], writes=["wos:%d" % k])
                        P.op("dve" if k % 2 == 0 else "pool", lambda e, k=k: e.tensor_tensor(out=wos[:, k, :], in0=wos[:, k, :], in1=GB[:], op=ALU.mult),
                             reads=["wos:%d" % k, "GB"], writes=["wos:%d" % k])
                    pend = None
                    for ti in range(NT):
                        psrc = []
                        for hf in range(2):
                            pb, pbn = bank()
                            for k in range(8):
                                mm(pb[:], mixT[:, k, ti * 128:(ti + 1) * 128], wos[:, k, hf * 512:(hf + 1) * 512], k == 0, k == 7,
                                   ["mixT:%d" % k, "wos:%d" % k], [pbn])
                            psrc.append((pb, pbn))
                        fin = ln_residual(ti, zts, LNG, LNB, psrc)
                        if pend is not None:
                            pend()
                        pend = fin
                    pend()
                    P.barrier()

        def ffn(l, b):
            with ExitStack() as fs:
                aT = sb(fs, "aT", [128, NF, 512], BF16)
                wout = sb(fs, "wout", [128, NF, D], BF16)
                wgu = [sb(fs, "wgu%d" % i, [128, 8, 256], BF16) for i in range(3)]
                sg = [sb(fs, "sg%d" % i, [128, 512], F32) for i in range(2)]
                LNG = sb(fs, "LNG", [128, D], F32)
                LNB = sb(fs, "LNB", [128, D], F32)
                GB = sb(fs, "GB", [128, D], F32)
                zts = [sb(fs, "zt%d" % i, [128, D], F32) for i in range(2)]
                load_ln(l, 1, b, LNG, LNB, GB)
                for f in range(NF):
                    P.dma("sp", wout[:, f, :], wfo16[l, f], reads=["wfo16:%d" % l], writes=["wout:%d" % f])
                    P.op("pool", lambda e, f=f: e.tensor_tensor(out=wout[:, f, :], in0=wout[:, f, :], in1=GB[:], op=ALU.mult),
                         reads=["wout:%d" % f, "GB"], writes=["wout:%d" % f])
                it = 0
                for tc in range(4):
                    for f in range(NF):
                        wi = it % 3
                        it += 1
                        P.dma("sp", wgu[wi][:], wfi16[l, f], reads=["wfi16:%d" % l], writes=["wgu%d" % wi])
                        pg, pgn = bank()
                        pu, pun = bank()
                        for k in range(8):
                            mm(pg[:], wgu[wi][:, k, 0:128], hT[:, k, tc * 512:(tc + 1) * 512], k == 0, k == 7, hT_reads([tc]) + ["wgu%d" % wi], [pgn])
                        for k in range(8):
                            mm(pu[:], wgu[wi][:, k, 128:256], hT[:, k, tc * 512:(tc + 1) * 512], k == 0, k == 7, hT_reads([tc]) + ["wgu%d" % wi], [pun])
                        s_ = sg[f % 2]
                        sn = "sg%d" % (f % 2)
                        P.op("act", lambda e, s_=s_, pg=pg: e.activation(out=s_[:], in_=pg[:], func=AF.Silu), reads=[pgn], writes=[sn])
                        P.op("dve", lambda e, s_=s_, pu=pu, f=f: e.tensor_tensor(out=aT[:, f, :], in0=pu[:], in1=s_[:], op=ALU.mult),
                             reads=[pun, sn], writes=["aT:%d" % f])
                    pend = None
                    for jt in range(4):
                        ti = tc * 4 + jt
                        psrc = []
                        for hf in range(2):
                            pb, pbn = bank()
                            for f in range(NF):
                                mm(pb[:], aT[:, f, jt * 128:(jt + 1) * 128], wout[:, f, hf * 512:(hf + 1) * 512], f == 0, f == NF - 1,
                                   ["aT:%d" % f, "wout:%d" % f], [pbn])
                            psrc.append((pb, pbn))
                        fin = ln_residual(ti, zts, LNG, LNB, psrc)
                        if pend is not None:
                            pend()
                        pend = fin
                    pend()
                    pend = None
                P.barrier()

        done = False
        for b in range(NB):
            for tg in range(4):
                P.dma("sp", X[:, tg * 4:(tg + 1) * 4, :], x_d[b, tg * 512:(tg + 1) * 512, :].rearrange("(t p) d -> p t d", p=128),
                      writes=["X:%d" % (tg * 4 + i) for i in range(4)])
            if stage == -1:
                done = True
            for l in range(2):
                if done:
                    break
                load_mod(l, b)
                transposes(0)
                mixer(l, b)
                if stage == 2 * (2 * b + l):
                    done = True
                    break
                transposes(1)
                ffn(l, b)
                if stage == 2 * (2 * b + l) + 1:
                    done = True
                    break
            if done:
                P.dma("sp", dbg["X"].rearrange("(t p) d -> p t d", p=128), X[:], reads=["X:%d" % i for i in range(NT)])
                break
            for tg in range(4):
                P.dma("sp", out_d[b, tg * 512:(tg + 1) * 512, :].rearrange("(t p) d -> p t d", p=128), X[:, tg * 4:(tg + 1) * 4, :],
                      reads=["X:%d" % (tg * 4 + i) for i in range(4)])
        P.barrier()
        build_program.stats = dict(n_ins=dict(P.n_ins), count=dict(P.count), dma=list(P.dma_cnt))
    return nc


def host_inputs(inputs):
    global ECOLS
    rel_bias = np.asarray(inputs["rel_bias"], np.float32)
    tiles, cols = make_bias_tiles(rel_bias)
    ECOLS = cols
    k = np.arange(128)
    tri = (k[:, None] <= k[None, :]).astype(np.float32)
    kaug = np.zeros((8, T), np.float32)
    for n in range(8):
        kaug[n, n * 256:(n + 1) * 256] = 1.0
    shared = {
        "rel_bias": rel_bias,
        "w_ada": np.ascontiguousarray(inputs["w_ada"], np.float32),
        "b_ada": np.ascontiguousarray(inputs["b_ada"], np.float32),
        "ln_g": np.ascontiguousarray(inputs["ln_g"], np.float32),
        "ln_b": np.ascontiguousarray(inputs["ln_b"], np.float32),
        "w_in_ab": np.ascontiguousarray(np.asarray(inputs["w_in_ab"], np.float32)[0]),
        "diff_lambda": np.ascontiguousarray(np.asarray(inputs["diff_lambda"], np.float32).reshape(256)),
        "diff_subln_g": np.ascontiguousarray(np.asarray(inputs["diff_subln_g"], np.float32).reshape(128)),
        "w_in_cd": np.ascontiguousarray(np.asarray(inputs["w_in_cd"], np.float32)[0]),
        "forget_b": np.ascontiguousarray(np.asarray(inputs["forget_b"], np.float32).reshape(8)),
        "w_o": np.ascontiguousarray(inputs["w_o"], np.float32),
        "w_ffn_in": np.ascontiguousarray(inputs["w_ffn_in"], np.float32),
        "w_ffn_out": np.ascontiguousarray(inputs["w_ffn_out"], np.float32),
        "btiles": tiles,
        "ident": np.eye(128, dtype=np.float32),
        "tri": tri,
        "kaug": kaug,
    }
    return shared


def kernel(**inputs):
    shared = host_inputs(inputs)
    x = np.asarray(inputs["x"], np.float32)
    c = np.asarray(inputs["c"], np.float32)
    n = 8
    nc = build_program()
    in_maps = []
    for i in range(n):
        m = dict(shared)
        m["x"] = np.ascontiguousarray(x[NB * i:NB * (i + 1)])
        m["c"] = np.ascontiguousarray(c[NB * i:NB * (i + 1)])
        in_maps.append(m)
    res = run_bass_kernel_spmd(nc, in_maps, core_ids=list(range(n)))
    return np.concatenate([np.asarray(r["out"], np.float32) for r in res.results], axis=0)
```
